# Optimizing a Trainium2 kernel written in Bass

```python
import jax, jax.numpy as jnp
from jax import lax
import numpy as np

D_MODEL = 4096
BATCH = 1
SEQ = 8192
DEPTH = 2

SSM_GROUP = 16
SSM_WIDTH = D_MODEL // 2
SSM_GROUPS = SSM_WIDTH // SSM_GROUP
SSM_STATE = 64
GDN_HEADS = 16
GDN_HEAD_K = 128
GDN_HEAD_V = 128
GDN_KEY = GDN_HEADS * GDN_HEAD_K
GDN_VAL = GDN_HEADS * GDN_HEAD_V
CONV_WIDTH = 4
CHUNK = 64
D_FF = 4 * D_MODEL
EPS = 1e-6
IN_SPLITS = (SSM_WIDTH, GDN_KEY, GDN_KEY, GDN_VAL, GDN_HEADS, GDN_HEADS, GDN_VAL, D_MODEL, D_MODEL)
D_IN_PROJ = SSM_WIDTH + 2 * GDN_KEY + 2 * GDN_VAL + 2 * GDN_HEADS + 2 * D_MODEL
CONV_CH = 2 * GDN_KEY + GDN_VAL

kernel_name = "hybrid_s5_gated_deltanet_sandwich_block"


def rmsnorm(x, w):
    xf = x.astype(jnp.float32)
    y = xf * lax.rsqrt(jnp.mean(xf * xf, axis=-1, keepdims=True) + EPS) * w.astype(jnp.float32)
    return y.astype(x.dtype)


def l2norm(t):
    return t * lax.rsqrt(jnp.sum(t * t, axis=-1, keepdims=True) + EPS)


def s5_ssm(u, a_re, a_im, log_dt, b_re, b_im, c_re, c_im, d_skip):
    bsz, L, _ = u.shape
    uf = u.astype(jnp.float32).reshape(bsz, L, SSM_GROUPS, SSM_GROUP)
    dt = jnp.exp(log_dt.astype(jnp.float32))[:, None]
    ar = a_re.astype(jnp.float32)
    ai = a_im.astype(jnp.float32)
    mag = jnp.exp(ar * dt)
    abar_r = mag * jnp.cos(ai * dt)
    abar_i = mag * jnp.sin(ai * dt)
    den = ar * ar + ai * ai
    fr = ((abar_r - 1.0) * ar + abar_i * ai) / den
    fi = (abar_i * ar - (abar_r - 1.0) * ai) / den
    br = b_re.astype(jnp.float32)
    bi = b_im.astype(jnp.float32)
    bbar_r = fr[..., None] * br - fi[..., None] * bi
    bbar_i = fr[..., None] * bi + fi[..., None] * br
    bu_r = jnp.einsum('blgh,gnh->blgn', uf, bbar_r)
    bu_i = jnp.einsum('blgh,gnh->blgn', uf, bbar_i)

    def combine(e1, e2):
        a1r, a1i, b1r, b1i = e1
        a2r, a2i, b2r, b2i = e2
        return (a2r * a1r - a2i * a1i,
                a2r * a1i + a2i * a1r,
                a2r * b1r - a2i * b1i + b2r,
                a2r * b1i + a2i * b1r + b2i)

    ar_b = jnp.broadcast_to(abar_r, bu_r.shape)
    ai_b = jnp.broadcast_to(abar_i, bu_r.shape)
    _, _, xr, xi = lax.associative_scan(combine, (ar_b, ai_b, bu_r, bu_i), axis=1)
    y = (jnp.einsum('blgn,ghn->blgh', xr, c_re.astype(jnp.float32))
         - jnp.einsum('blgn,ghn->blgh', xi, c_im.astype(jnp.float32)))
    y = y + d_skip.astype(jnp.float32).reshape(SSM_GROUPS, SSM_GROUP) * uf
    return y.reshape(bsz, L, SSM_WIDTH).astype(u.dtype)


def causal_conv_silu(x, w):
    K = w.shape[0]
    L = x.shape[1]
    xp = jnp.pad(x, ((0, 0), (K - 1, 0), (0, 0)))
    y = xp[:, 0:L] * w[0]
    for j in range(1, K):
        y = y + xp[:, j:j + L] * w[j]
    return jax.nn.silu(y)


def chunk_gated_delta_rule(q, k, v, g, beta):
    bsz, L, H, Dk = q.shape
    Dv = v.shape[-1]
    n = L // CHUNK
    q = l2norm(q) * (Dk ** -0.5)
    k = l2norm(k)

    def chunked(t):
        return t.reshape(bsz, n, CHUNK, H, -1).transpose(0, 3, 1, 2, 4)

    q, k, v = chunked(q), chunked(k), chunked(v)
    g = g.reshape(bsz, n, CHUNK, H).transpose(0, 3, 1, 2)
    beta = beta.reshape(bsz, n, CHUNK, H).transpose(0, 3, 1, 2)
    g = jnp.cumsum(g, axis=-1)
    causal = jnp.tril(jnp.ones((CHUNK, CHUNK), dtype=bool))
    strict = jnp.tril(jnp.ones((CHUNK, CHUNK), dtype=bool), -1)
    decay = jnp.exp(jnp.where(causal, g[..., :, None] - g[..., None, :], -jnp.inf))
    kk = jnp.einsum('bhncd,bhnsd->bhncs', k * beta[..., None], k)
    a_mat = jnp.where(strict, kk * decay, 0.0) + jnp.eye(CHUNK, dtype=kk.dtype)
    u = lax.linalg.triangular_solve(a_mat, v * beta[..., None], left_side=True, lower=True, unit_diagonal=True)
    w = lax.linalg.triangular_solve(a_mat, k * (beta * jnp.exp(g))[..., None], left_side=True, lower=True, unit_diagonal=True)
    qk = jnp.where(causal, jnp.einsum('bhncd,bhnsd->bhncs', q, k) * decay, 0.0)

    def to_chunk_major(t):
        return jnp.moveaxis(t, 2, 0)

    xs = (to_chunk_major(q), to_chunk_major(k), to_chunk_major(u), to_chunk_major(w),
          to_chunk_major(qk), to_chunk_major(g))

    def step(S, inp):
        qc, kc, uc, wc, qkc, gc = inp
        v_new = uc - jnp.einsum('bhcd,bhde->bhce', wc, S)
        o = (jnp.einsum('bhcd,bhde->bhce', qc * jnp.exp(gc)[..., None], S)
             + jnp.einsum('bhcs,bhse->bhce', qkc, v_new))
        g_last = gc[..., -1]
        S = (S * jnp.exp(g_last)[..., None, None]
             + jnp.einsum('bhcd,bhce->bhde', kc * jnp.exp(g_last[..., None] - gc)[..., None], v_new))
        return S, o

    S0 = jnp.zeros((bsz, H, Dk, Dv), dtype=q.dtype)
    _, o = lax.scan(step, S0, xs)
    return o.transpose(1, 0, 3, 2, 4).reshape(bsz, L, H, Dv)


def hybrid_mixer(h, w_in, a_re, a_im, log_dt, b_re, b_im, c_re, c_im, d_skip, w_glu,
                 conv_w, a_log, dt_bias, gdn_norm_w, w_gdn_out, w_out):
    bsz, L, _ = h.shape
    proj = h @ w_in
    offsets = np.cumsum(np.array(IN_SPLITS))[:-1].tolist()
    u, q, k, v, a, b, z, gate_s, gate_d = jnp.split(proj, offsets, axis=-1)

    y = jax.nn.gelu(s5_ssm(u, a_re, a_im, log_dt, b_re, b_im, c_re, c_im, d_skip))
    glu_a, glu_b = jnp.split(y @ w_glu, 2, axis=-1)
    y_ssm = glu_a * jax.nn.sigmoid(glu_b)

    qkv = causal_conv_silu(jnp.concatenate([q, k, v], axis=-1), conv_w).astype(jnp.float32)
    qc, kc, vc = jnp.split(qkv, [GDN_KEY, 2 * GDN_KEY], axis=-1)
    qc = qc.reshape(bsz, L, GDN_HEADS, GDN_HEAD_K)
    kc = kc.reshape(bsz, L, GDN_HEADS, GDN_HEAD_K)
    vc = vc.reshape(bsz, L, GDN_HEADS, GDN_HEAD_V)
    g = -jnp.exp(a_log.astype(jnp.float32)) * jax.nn.softplus(a.astype(jnp.float32) + dt_bias.astype(jnp.float32))
    beta = jax.nn.sigmoid(b.astype(jnp.float32))
    o = chunk_gated_delta_rule(qc, kc, vc, g, beta).astype(h.dtype)
    o = rmsnorm(o, gdn_norm_w) * jax.nn.silu(z.reshape(bsz, L, GDN_HEADS, GDN_HEAD_V))
    y_gdn = o.reshape(bsz, L, GDN_VAL) @ w_gdn_out

    merged = jax.nn.sigmoid(gate_s) * y_ssm + jax.nn.sigmoid(gate_d) * y_gdn
    return merged @ w_out


def squared_relu_mlp(h, w1, w2):
    return jnp.square(jax.nn.relu(h @ w1)) @ w2


def setup_inputs(seed: int = 0) -> dict:
    key = jax.random.key(seed)
    ks = jax.random.split(key, 24)
    f32 = jnp.float32

    def nrm(k, shape, scale):
        return jax.random.normal(k, shape, f32) * scale

    def gain(k, shape):
        return 1.0 + 0.02 * jax.random.normal(k, shape, f32)

    x = jax.random.normal(ks[0], (BATCH, SEQ, D_MODEL), f32)
    w_in = nrm(ks[1], (DEPTH, D_MODEL, D_IN_PROJ), D_MODEL ** -0.5)
    ssm_a_re = -0.5 * jnp.exp(0.05 * jax.random.normal(ks[2], (DEPTH, SSM_GROUPS, SSM_STATE), f32))
    n_idx = jnp.arange(SSM_STATE, dtype=f32)
    ssm_a_im = jnp.pi * n_idx + 0.01 * jax.random.normal(ks[3], (DEPTH, SSM_GROUPS, SSM_STATE), f32)
    ssm_log_dt = jax.random.uniform(ks[4], (DEPTH, SSM_GROUPS), f32, np.log(1e-3), np.log(1e-1))
    ssm_b_re = nrm(ks[5], (DEPTH, SSM_GROUPS, SSM_STATE, SSM_GROUP), (2 * SSM_GROUP) ** -0.5)
    ssm_b_im = nrm(ks[6], (DEPTH, SSM_GROUPS, SSM_STATE, SSM_GROUP), (2 * SSM_GROUP) ** -0.5)
    ssm_c_re = nrm(ks[7], (DEPTH, SSM_GROUPS, SSM_GROUP, SSM_STATE), (2 * SSM_STATE) ** -0.5)
    ssm_c_im = nrm(ks[8], (DEPTH, SSM_GROUPS, SSM_GROUP, SSM_STATE), (2 * SSM_STATE) ** -0.5)
    ssm_d = 1.0 + 0.1 * jax.random.normal(ks[9], (DEPTH, SSM_WIDTH), f32)
    w_glu = nrm(ks[10], (DEPTH, SSM_WIDTH, 2 * D_MODEL), SSM_WIDTH ** -0.5)
    conv_w = nrm(ks[11], (DEPTH, CONV_WIDTH, CONV_CH), CONV_WIDTH ** -0.5)
    gdn_a_log = jnp.log(jax.random.uniform(ks[12], (DEPTH, GDN_HEADS), f32, 1.0, 16.0))
    dt0 = jnp.exp(jax.random.uniform(ks[13], (DEPTH, GDN_HEADS), f32, np.log(1e-3), np.log(1e-1)))
    gdn_dt_bias = dt0 + jnp.log(-jnp.expm1(-dt0))
    gdn_norm_w = gain(ks[14], (DEPTH, GDN_HEAD_V))
    w_gdn_out = nrm(ks[15], (DEPTH, GDN_VAL, D_MODEL), GDN_VAL ** -0.5)
    w_out = nrm(ks[16], (DEPTH, D_MODEL, D_MODEL), D_MODEL ** -0.5)
    mix_pre_w = gain(ks[17], (DEPTH, D_MODEL))
    mix_post_w = gain(ks[18], (DEPTH, D_MODEL))
    ffn_pre_w = gain(ks[19], (DEPTH, D_MODEL))
    ffn_post_w = gain(ks[20], (DEPTH, D_MODEL))
    w_ff1 = nrm(ks[21], (DEPTH, D_MODEL, D_FF), D_MODEL ** -0.5)
    w_ff2 = nrm(ks[22], (DEPTH, D_FF, D_MODEL), D_FF ** -0.5)
    return {"x": x, "w_in": w_in, "ssm_a_re": ssm_a_re, "ssm_a_im": ssm_a_im,
            "ssm_log_dt": ssm_log_dt, "ssm_b_re": ssm_b_re, "ssm_b_im": ssm_b_im,
            "ssm_c_re": ssm_c_re, "ssm_c_im": ssm_c_im, "ssm_d": ssm_d, "w_glu": w_glu,
            "conv_w": conv_w, "gdn_a_log": gdn_a_log, "gdn_dt_bias": gdn_dt_bias,
            "gdn_norm_w": gdn_norm_w, "w_gdn_out": w_gdn_out, "w_out": w_out,
            "mix_pre_w": mix_pre_w, "mix_post_w": mix_post_w, "ffn_pre_w": ffn_pre_w,
            "ffn_post_w": ffn_post_w, "w_ff1": w_ff1, "w_ff2": w_ff2}


def reference(x, w_in, ssm_a_re, ssm_a_im, ssm_log_dt, ssm_b_re, ssm_b_im, ssm_c_re, ssm_c_im,
              ssm_d, w_glu, conv_w, gdn_a_log, gdn_dt_bias, gdn_norm_w, w_gdn_out, w_out,
              mix_pre_w, mix_post_w, ffn_pre_w, ffn_post_w, w_ff1, w_ff2):
    for i in range(DEPTH):
        h = rmsnorm(x, mix_pre_w[i])
        h = hybrid_mixer(h, w_in[i], ssm_a_re[i], ssm_a_im[i], ssm_log_dt[i], ssm_b_re[i], ssm_b_im[i],
                         ssm_c_re[i], ssm_c_im[i], ssm_d[i], w_glu[i], conv_w[i], gdn_a_log[i],
                         gdn_dt_bias[i], gdn_norm_w[i], w_gdn_out[i], w_out[i])
        x = x + rmsnorm(h, mix_post_w[i])
        h = squared_relu_mlp(rmsnorm(x, ffn_pre_w[i]), w_ff1[i], w_ff2[i])
        x = x + rmsnorm(h, ffn_post_w[i])
    return x
```

```python
import numpy as np
from contextlib import ExitStack
V, S, G, T_ = "vector", "scalar", "gpsimd", "tensor"
import concourse.bass as bass
import concourse.mybir as mybir
from concourse.bass_utils import run_bass_kernel_spmd

F32 = mybir.dt.float32
BF16 = mybir.dt.bfloat16
ALU = mybir.AluOpType
AF = mybir.ActivationFunctionType
AX = mybir.AxisListType

SAME_ENGINE_SYNC = True
NDMASEM = 48
COMPUTE = ("tensor", "vector", "scalar", "gpsimd")
ENGS = ("tensor", "vector", "scalar", "gpsimd", "sync")


class Prog:
    def __init__(self, name="k"):
        self.nc = bass.Bass("TRN2", target_bir_lowering=False)
        self.es = ExitStack()
        self.ops = {e: [] for e in ENGS}
        self.last_w = {}
        self.readers = {}
        self.ndma = 0
        self.dma_ops = []
        self.waited = {e: {x: -1 for x in COMPUTE} for e in ENGS}
        self.dma_waited = {e: set() for e in ENGS}
        self.nbuf = 0

    def dram(self, name, shape, dt, kind):
        return self.nc.dram_tensor(name, list(shape), dt, kind=kind).ap()

    def sbuf(self, name, shape, dt):
        return self.es.enter_context(self.nc.sbuf_tensor(name, list(shape), dt))

    def psum(self, name, shape, dt=F32):
        return self.es.enter_context(self.nc.psum_tensor(name, list(shape), dt))

    def _dep(self, eng, rec, dep):
        kind, deng, didx = dep
        if kind == "c":
            if deng == eng and (eng == "tensor" or not SAME_ENGINE_SYNC):
                return
            if self.waited[eng][deng] >= didx:
                return
            self.waited[eng][deng] = didx
            rec["waits"].append(dep)
            self.ops[deng][didx]["inc"] = True
        else:
            if didx in self.dma_waited[eng]:
                return
            self.dma_waited[eng].add(didx)
            rec["waits"].append(dep)

    def op(self, eng, fn, reads=(), writes=(), dma=False):
        rec = {"fn": fn, "waits": [], "inc": False, "dma": None}
        idx = len(self.ops[eng])
        for b in reads:
            lw = self.last_w.get(b)
            if lw is not None:
                self._dep(eng, rec, lw)
        for b in writes:
            lw = self.last_w.get(b)
            if lw is not None:
                self._dep(eng, rec, lw)
            for r in self.readers.get(b, ()):
                self._dep(eng, rec, r)
        if dma:
            d = self.ndma
            self.ndma += 1
            rec["dma"] = d
            if d >= NDMASEM:
                self._dep(eng, rec, ("d", None, d - NDMASEM))
            me = ("d", eng, d)
            self.dma_ops.append((eng, idx))
        else:
            me = ("c", eng, idx)
        for b in reads:
            self.readers.setdefault(b, []).append(me)
        for b in writes:
            self.last_w[b] = me
            self.readers[b] = []
        self.ops[eng].append(rec)
        return me

    def dma(self, eng, out, in_, reads=(), writes=(), **kw):
        return self.op(eng, lambda e: e.dma_start(out=out, in_=in_, **kw), reads, writes, dma=True)

    def build(self):
        nc = self.nc
        for eng in ("sync", "gpsimd", "scalar"):
            mine = [d for (e, i) in self.dma_ops for d in [self.ops[e][i]["dma"]] if e == eng]
            if not mine:
                continue
            rec = {"fn": None, "waits": [], "inc": False, "dma": None}
            for d in mine:
                if d not in self.dma_waited[eng]:
                    rec["waits"].append(("d", None, d))
            self.ops[eng].append(rec)
        dsems = [self.es.enter_context(nc.semaphore("d%d" % i)) for i in range(NDMASEM)]
        EPOCH = 8192
        sems = {}
        for e in COMPUTE:
            c = 0
            for rec in self.ops[e]:
                if rec["inc"]:
                    rec["ticket"] = (c // EPOCH, c % EPOCH + 1)
                    c += 1
            sems[e] = [self.es.enter_context(nc.semaphore("s_%s%d" % (e, k))) for k in range(max(1, (c + EPOCH - 1) // EPOCH))]
        ops = self.ops

        def replay(ename, eh):
            for rec in ops[ename]:
                for (kind, deng, didx) in rec["waits"]:
                    if kind == "c":
                        ep, val = ops[deng][didx]["ticket"]
                        eh.wait_ge(sems[deng][ep], val)
                    else:
                        eh.wait_ge(dsems[didx % NDMASEM], 16 * (didx // NDMASEM + 1))
                if rec["fn"] is None:
                    continue
                ins = rec["fn"](eh)
                if rec["dma"] is not None:
                    ins.then_inc(dsems[rec["dma"] % NDMASEM], 16)
                elif rec["inc"]:
                    ins.then_inc(sems[ename][rec["ticket"][0]], 1)

        with nc.Block() as block:
            @block.tensor
            def _(t):
                replay("tensor", t)

            @block.vector
            def _(v):
                replay("vector", v)

            @block.scalar
            def _(s):
                replay("scalar", s)

            @block.gpsimd
            def _(g):
                replay("gpsimd", g)

            @block.sync
            def _(s):
                replay("sync", s)
        self.es.close()
        return nc


class Pool:
    def __init__(self, P, name, n, shape, dt, psum=False):
        self.bufs = [(P.psum if psum else P.sbuf)("%s%d" % (name, i), shape, dt) for i in range(n)]
        self.name = name
        self.i = 0

    def next(self):
        k = self.i % len(self.bufs)
        self.i += 1
        return self.bufs[k], (self.name, k)


class Gemm:
    def __init__(self, P, wbytes=16384, npsum=6, nstage=4):
        self.P = P
        self.wel = wbytes // 2
        self.wb = Pool(P, "wb", 2, [128, self.wel], BF16)
        self.ps = Pool(P, "ps", npsum, [128, 512], F32, psum=True)
        self.stage = Pool(P, "stg", nstage, [128, 512], F32)
        self.evq = 0

    def run(self, actT, act_tok, KT, W, ncols, evac, ntb=2, col_groups=None):
        P = self.P
        gc = max(128, (self.wel // KT) // 128 * 128)
        gc = min(gc, 512)
        if col_groups is None:
            col_groups = []
            c = 0
            while c < ncols:
                w = min(gc, ncols - c)
                col_groups.append([(c, w)])
                c += w
        Wv = W.rearrange("(kt p) n -> p kt n", p=128)
        nt_idx = 0
        for grp in col_groups:
            wbuf, wtok = self.wb.next()
            tot = sum(w for _, w in grp)
            wv = wbuf[:, 0:KT * tot].rearrange("p (kt n) -> p kt n", kt=KT)
            off = 0
            for (c0, w) in grp:
                P.dma("gpsimd", wv[:, :, off:off + w], Wv[:, :, c0:c0 + w], writes=[wtok])
                off += w
            for j in range(tot // 128):
                for tb in range(ntb):
                    ps, ptok = self.ps.next()
                    for kt in range(KT):
                        P.op("tensor", (lambda e, ps=ps, kt=kt, j=j, tb=tb, wv=wv: e.matmul(
                            ps[:], wv[:, kt, j * 128:(j + 1) * 128], actT[:, kt, tb * 512:(tb + 1) * 512],
                            start=(kt == 0), stop=(kt == KT - 1))),
                            reads=[wtok, act_tok], writes=[ptok])
                    evac(nt_idx, tb, ps, ptok)
                nt_idx += 1

    def evac_to_dram(self, outT, tok=None):
        P = self.P
        def ev(nt, tb, ps, ptok):
            st, stok = self.stage.next()
            eng = "vector" if (self.evq % 2 == 0) else "scalar"
            self.evq += 1
            if eng == "vector":
                P.op("vector", lambda e: e.tensor_copy(st[:], ps[:]), reads=[ptok], writes=[stok])
            else:
                P.op("scalar", lambda e: e.copy(st[:], ps[:]), reads=[ptok], writes=[stok])
            P.dma("sync", outT[nt * 128:(nt + 1) * 128, tb * 512:(tb + 1) * 512], st[:], reads=[stok], writes=([(tok, nt, tb)] if tok else []))
        return ev

import math
PI = math.pi

def ssm_host_inputs(j, a_re, a_im, log_dt, b_re, b_im, c_re, c_im, d_skip):
    G0 = 16 * j
    gs = np.arange(G0, G0 + 16)
    ssmA = np.zeros((128, 5, 2, 64), np.float32)
    for ct in range(2):
        for gp in range(8):
            g = G0 + ct * 8 + gp
            sl = slice(gp * 16, gp * 16 + 16)
            ssmA[sl, 0, ct, :] = a_re[g][None, :]
            ssmA[sl, 1, ct, :] = a_im[g][None, :]
            ssmA[sl, 2, ct, :] = log_dt[g]
            ssmA[sl, 3, ct, :] = b_re[g].T
            ssmA[sl, 4, ct, :] = b_im[g].T
    ssmB = np.zeros((128, 3, 16), np.float32)
    ssmB[:, 0, :] = np.concatenate([a_re[gs].T, a_re[gs].T], 0)
    ssmB[:, 1, :] = np.concatenate([a_im[gs].T, a_im[gs].T], 0)
    ssmB[:, 2, :] = log_dt[gs][None, :]
    cc = np.zeros((128, 2, 16, 16), np.float32)
    cre = np.transpose(c_re[gs], (2, 0, 1))
    cim = np.transpose(c_im[gs], (2, 0, 1))
    cc[0:64, 0] = cre; cc[64:128, 0] = cim
    cc[0:64, 1] = cim; cc[64:128, 1] = cre
    dsk = np.ascontiguousarray(d_skip[G0 * 16:G0 * 16 + 256].reshape(2, 128).T)
    return {"ssmA": ssmA, "ssmB": ssmB, "ssmcc": cc, "ssmd": dsk}

def ssm_consts():
    iota = np.tile(np.arange(512, dtype=np.float32)[None, :], (128, 1))
    ident = np.eye(128, dtype=np.float32)
    Jm = np.zeros((128, 128), np.float32)
    for n in range(64):
        Jm[64 + n, n] = -1.0
        Jm[n, 64 + n] = 1.0
    sgn = np.ones((128, 2), np.float32); sgn[64:, 0] = -1.0; sgn[:, 1] = -1.0
    mask = np.zeros((128, 16), np.float32)
    for p in range(128):
        mask[p, p // 16] = 1.0
        mask[p, 8 + p // 16] = -1.0
    return {"c_iota": iota, "c_ident": ident, "c_J": Jm, "c_sgn": sgn, "c_mask": mask}

def emit_ssm(P, uT, yT, D, NCH=16):
    V, S, G, T_ = "vector", "scalar", "gpsimd", "tensor"
    def ld(name, shape, src, dt=F32):
        t = P.sbuf("sb_" + name, shape, dt)
        P.dma("sync", t[:], src, writes=[name])
        return t
    sA = ld("ssmA", [128, 5, 2, 64], D["ssmA"])
    sB = ld("ssmB", [128, 3, 16], D["ssmB"])
    cc = ld("ssmcc", [128, 2, 16, 16], D["ssmcc"])
    dsk = ld("ssmd", [128, 2], D["ssmd"])
    iota = ld("c_iota", [128, 512], D["c_iota"])
    ident = ld("c_ident", [128, 128], D["c_ident"])
    Jm = ld("c_J", [128, 128], D["c_J"])
    sgn = ld("c_sgn", [128, 2], D["c_sgn"])
    mask = ld("c_mask", [128, 16], D["c_mask"])
    negpi = P.sbuf("negpi", [128, 1], F32)
    P.op(V, lambda e: e.memset(negpi[:], -PI), writes=["negpi"])

    cnt = [0]
    def tmp(shape, dt=F32):
        cnt[0] += 1
        n = "tmp%d" % cnt[0]
        return P.sbuf(n, shape, dt), n
    def vop(fn, reads, writes, eng=V):
        P.op(eng, fn, reads=reads, writes=writes)
    I32 = mybir.dt.int32
    C1 = 6.28125
    C2 = 2 * PI - C1
    sc_tmp = {}
    def sincos_into(x_ap, xn, s_ap, sn, c_ap, cn, shape):
        key = tuple(shape)
        if key not in sc_tmp:
            sc_tmp[key] = (tmp(shape), tmp(shape), tmp(shape), tmp(shape, I32), tmp(shape))
        (y, yn), (kf, kfn), (r, rn), (ki, kin), (m, mn) = sc_tmp[key]
        vop(lambda e: e.tensor_scalar(y[:], x_ap, PI, None, ALU.add), [xn], [yn])
        vop(lambda e: e.tensor_scalar(kf[:], y[:], 1.0 / (2 * PI), None, ALU.mult), [yn], [kfn])
        vop(lambda e: e.tensor_copy(ki[:], kf[:]), [kfn], [kin])
        vop(lambda e: e.tensor_copy(kf[:], ki[:]), [kin], [kfn])
        vop(lambda e: e.scalar_tensor_tensor(r[:], kf[:], -C1, y[:], ALU.mult, ALU.add), [kfn, yn], [rn])
        vop(lambda e: e.scalar_tensor_tensor(r[:], kf[:], -C2, r[:], ALU.mult, ALU.add), [kfn, rn], [rn])
        vop(lambda e: e.tensor_scalar(m[:], r[:], 0.0, 2 * PI, ALU.is_lt, ALU.mult), [rn], [mn])
        vop(lambda e: e.tensor_tensor(r[:], r[:], m[:], ALU.add), [rn, mn], [rn])
        vop(lambda e: e.tensor_scalar(m[:], r[:], 2 * PI, -2 * PI, ALU.is_ge, ALU.mult), [rn], [mn])
        vop(lambda e: e.tensor_tensor(r[:], r[:], m[:], ALU.add), [rn, mn], [rn])
        vop(lambda e: e.activation(s_ap, r[:], AF.Sin, bias=negpi[:, 0:1], scale=1.0), [rn, "negpi"], [sn], eng=S)
        vop(lambda e: e.tensor_scalar(y[:], r[:], 0.5 * PI, None, ALU.add), [rn], [yn])
        vop(lambda e: e.tensor_scalar(m[:], y[:], 2 * PI, -2 * PI, ALU.is_ge, ALU.mult), [yn], [mn])
        vop(lambda e: e.tensor_tensor(y[:], y[:], m[:], ALU.add), [yn, mn], [yn])
        vop(lambda e: e.activation(c_ap, y[:], AF.Sin, bias=negpi[:, 0:1], scale=1.0), [yn, "negpi"], [cn], eng=S)
    def sincos(x, xn, shape):
        s_, sn = tmp(shape); c_, cn = tmp(shape)
        sincos_into(x[:], xn, s_[:], sn, c_[:], cn, shape)
        return (s_, sn), (c_, cn)

    shA = [128, 128]
    def A(i):
        return sA[:, i].rearrange("p a n -> p (a n)")
    dtA, dtAn = tmp(shA); ardt, ardtn = tmp(shA); aidt, aidtn = tmp(shA); mag, magn = tmp(shA)
    vop(lambda e: e.activation(dtA[:], A(2), AF.Exp), ["ssmA"], [dtAn], eng=S)
    vop(lambda e: e.tensor_tensor(ardt[:], A(0), dtA[:], ALU.mult), ["ssmA", dtAn], [ardtn])
    vop(lambda e: e.tensor_tensor(aidt[:], A(1), dtA[:], ALU.mult), ["ssmA", dtAn], [aidtn])
    vop(lambda e: e.activation(mag[:], ardt[:], AF.Exp), [ardtn], [magn], eng=S)
    (sn_, snn), (cs_, csn) = sincos(aidt, aidtn, shA)
    abr, abrn = tmp(shA); abi, abin = tmp(shA); den, denn = tmp(shA); t0, t0n = tmp(shA); t1, t1n = tmp(shA)
    fr, frn = tmp(shA); fi, fin = tmp(shA); bbr, bbrn = tmp(shA); bbi, bbin = tmp(shA)
    vop(lambda e: e.tensor_tensor(abr[:], mag[:], cs_[:], ALU.mult), [magn, csn], [abrn])
    vop(lambda e: e.tensor_tensor(abi[:], mag[:], sn_[:], ALU.mult), [magn, snn], [abin])
    vop(lambda e: e.tensor_tensor(t0[:], A(0), A(0), ALU.mult), ["ssmA"], [t0n])
    vop(lambda e: e.tensor_tensor(t1[:], A(1), A(1), ALU.mult), ["ssmA"], [t1n])
    vop(lambda e: e.tensor_tensor(den[:], t0[:], t1[:], ALU.add), [t0n, t1n], [denn])
    vop(lambda e: e.reciprocal(den[:], den[:]), [denn], [denn])
    vop(lambda e: e.tensor_scalar(abr[:], abr[:], -1.0, None, ALU.add), [abrn], [abrn])
    vop(lambda e: e.tensor_tensor(t0[:], abr[:], A(0), ALU.mult), [abrn, "ssmA"], [t0n])
    vop(lambda e: e.tensor_tensor(t1[:], abi[:], A(1), ALU.mult), [abin, "ssmA"], [t1n])
    vop(lambda e: e.tensor_tensor(fr[:], t0[:], t1[:], ALU.add), [t0n, t1n], [frn])
    vop(lambda e: e.tensor_tensor(fr[:], fr[:], den[:], ALU.mult), [frn, denn], [frn])
    vop(lambda e: e.tensor_tensor(t0[:], abi[:], A(0), ALU.mult), [abin, "ssmA"], [t0n])
    vop(lambda e: e.tensor_tensor(t1[:], abr[:], A(1), ALU.mult), [abrn, "ssmA"], [t1n])
    vop(lambda e: e.tensor_tensor(fi[:], t0[:], t1[:], ALU.subtract), [t0n, t1n], [fin])
    vop(lambda e: e.tensor_tensor(fi[:], fi[:], den[:], ALU.mult), [fin, denn], [fin])
    vop(lambda e: e.tensor_tensor(t0[:], fr[:], A(3), ALU.mult), [frn, "ssmA"], [t0n])
    vop(lambda e: e.tensor_tensor(t1[:], fi[:], A(4), ALU.mult), [fin, "ssmA"], [t1n])
    vop(lambda e: e.tensor_tensor(bbr[:], t0[:], t1[:], ALU.subtract), [t0n, t1n], [bbrn])
    vop(lambda e: e.tensor_tensor(t0[:], fr[:], A(4), ALU.mult), [frn, "ssmA"], [t0n])
    vop(lambda e: e.tensor_tensor(t1[:], fi[:], A(3), ALU.mult), [fin, "ssmA"], [t1n])
    vop(lambda e: e.tensor_tensor(bbi[:], t0[:], t1[:], ALU.add), [t0n, t1n], [bbin])
    LB1 = P.sbuf("LB1", [128, 16, 128], BF16); LB2 = P.sbuf("LB2", [128, 16, 128], BF16)
    for g in range(16):
        ct, gp = g // 8, g % 8
        cs = slice(ct * 64, ct * 64 + 64)
        vop(lambda e, g=g, gp=gp, cs=cs: e.tensor_scalar(LB1[:, g, 0:64], bbr[:, cs], mask[:, gp:gp + 1], None, ALU.mult), [bbrn, "c_mask"], ["LB"])
        vop(lambda e, g=g, gp=gp, cs=cs: e.tensor_scalar(LB1[:, g, 64:128], bbi[:, cs], mask[:, gp:gp + 1], None, ALU.mult), [bbin, "c_mask"], ["LB"])
        vop(lambda e, g=g, gp=gp, cs=cs: e.tensor_scalar(LB2[:, g, 0:64], bbi[:, cs], mask[:, gp:gp + 1], None, ALU.mult), [bbin, "c_mask"], ["LB"])
        vop(lambda e, g=g, gp=gp, cs=cs: e.tensor_scalar(LB2[:, g, 64:128], bbr[:, cs], mask[:, 8 + gp:9 + gp], None, ALU.mult), [bbrn, "c_mask"], ["LB"])
    shB = [128, 16]
    dtB, dtBn = tmp(shB); th, thn = tmp(shB); rho, rhon = tmp(shB); thT, thTn = tmp(shB)
    vop(lambda e: e.activation(dtB[:], sB[:, 2, :], AF.Exp), ["ssmB"], [dtBn], eng=S)
    vop(lambda e: e.tensor_tensor(th[:], sB[:, 1, :], dtB[:], ALU.mult), ["ssmB", dtBn], [thn])
    vop(lambda e: e.tensor_tensor(rho[:], sB[:, 0, :], dtB[:], ALU.mult), ["ssmB", dtBn], [rhon])
    vop(lambda e: e.activation(rho[:], rho[:], AF.Exp), [rhon], [rhon], eng=S)
    vop(lambda e: e.tensor_scalar(thT[:], th[:], 512.0, None, ALU.mult), [thn], [thTn])
    (sT, sTn), (cT, cTn) = sincos(thT, thTn, shB)
    St = P.sbuf("St", [128, 16, 512], F32); Ct = P.sbuf("Ct", [128, 16, 512], F32)
    ang = P.sbuf("ang", [128, 512], F32)
    for g in range(16):
        vop(lambda e, g=g: e.tensor_scalar(ang[:], iota[:], th[:, g:g + 1], None, ALU.mult), ["c_iota", thn], ["ang"])
        sincos_into(ang[:], "ang", St[:, g, :], ("St", g), Ct[:, g, :], ("Ct", g), [128, 512])
    Rm = P.sbuf("Rm", [128, 16, 128], F32)
    rt, rtn = tmp([128, 128])
    for g in range(16):
        vop(lambda e, g=g: e.tensor_scalar(rt[:], Jm[:], sT[:, g:g + 1], None, ALU.mult), ["c_J", sTn], [rtn])
        vop(lambda e, g=g: e.scalar_tensor_tensor(Rm[:, g, :], ident[:], cT[:, g:g + 1], rt[:], ALU.mult, ALU.add), ["c_ident", cTn, rtn], [("Rm", g)])
    LC1 = P.sbuf("LC1", [128, 16, 128], BF16); LC2 = P.sbuf("LC2", [128, 16, 128], BF16)
    vop(lambda e: e.memset(LC1[:], 0.0), [], ["LC"])
    vop(lambda e: e.memset(LC2[:], 0.0), [], ["LC"])
    for g in range(16):
        gp = g % 8
        vop(lambda e, g=g, gp=gp: e.tensor_scalar(LC1[:, g, gp * 16:gp * 16 + 16], cc[:, 0, g, :], sgn[:, 0:1], None, ALU.mult), ["ssmcc", "c_sgn"], ["LC"])
        vop(lambda e, g=g, gp=gp: e.tensor_scalar(LC2[:, g, gp * 16:gp * 16 + 16], cc[:, 1, g, :], sgn[:, 1:2], None, ALU.mult), ["ssmcc", "c_sgn"], ["LC"])

    u32p = Pool(P, "u32", 3, [128, 512], F32)
    ubfp = Pool(P, "ubf", 3, [128, 512], BF16)
    abp = Pool(P, "abps", 4, [128, 512], F32, psum=True)
    yps = Pool(P, "yps", 2, [128, 512], F32, psum=True)
    ips = Pool(P, "ips", 1, [128, 16], F32, psum=True)
    t1p = Pool(P, "st1", 2, [128, 512], F32)
    t2p = Pool(P, "st2", 2, [128, 512], F32)
    winp = Pool(P, "win", 2, [128, 512], F32)
    wstp = Pool(P, "wst", 3, [128, 512], F32)
    p1p = Pool(P, "p1", 2, [128, 512], BF16)
    p2p = Pool(P, "p2", 2, [128, 512], BF16)
    ytp = Pool(P, "yt", 2, [128, 512], F32)
    ybp = Pool(P, "yb", 2, [128, 512], BF16)
    init = P.sbuf("init", [128, 16], F32)
    wlast = P.sbuf("wlast", [128, 16], F32)
    for c in range(NCH):
        for ct in range(2):
            u32, u32n = u32p.next(); ubf, ubfn = ubfp.next()
            P.dma("sync", u32[:], uT[ct * 128:(ct + 1) * 128, c * 512:(c + 1) * 512], writes=[u32n])
            vop(lambda e, ubf=ubf, u32=u32: e.copy(ubf[:], u32[:]), [u32n], [ubfn], eng=S)
            Y, Yn = yps.next()
            for gp in range(8):
                g = ct * 8 + gp
                Aps, An = abp.next(); Bps, Bn = abp.next()
                P.op(T_, lambda e, Aps=Aps, g=g, ubf=ubf: e.matmul(Aps[:], LB1[:, g, :], ubf[:], start=True, stop=True), reads=["LB", ubfn], writes=[An])
                P.op(T_, lambda e, Bps=Bps, g=g, ubf=ubf: e.matmul(Bps[:], LB2[:, g, :], ubf[:], start=True, stop=True), reads=["LB", ubfn], writes=[Bn])
                a1, a1n = t1p.next(); a2, a2n = t2p.next(); win, winn = winp.next(); wst, wstn = wstp.next()
                vop(lambda e, a1=a1, Aps=Aps, g=g: e.tensor_tensor(a1[:], Ct[:, g, :], Aps[:], ALU.mult), [("Ct", g), An], [a1n])
                vop(lambda e, a2=a2, Bps=Bps, g=g: e.tensor_tensor(a2[:], St[:, g, :], Bps[:], ALU.mult), [("St", g), Bn], [a2n])
                vop(lambda e, a1=a1, a2=a2, win=win: e.tensor_tensor(win[:], a1[:], a2[:], ALU.add), [a1n, a2n], [winn], eng=G)
                if c > 0:
                    ip, ipn = ips.next()
                    P.op(T_, lambda e, ip=ip, g=g: e.matmul(ip[:, g:g + 1], Rm[:, g, :], wlast[:, g:g + 1], start=True, stop=True), reads=[("Rm", g), ("wlast", g)], writes=[ipn])
                    vop(lambda e, ip=ip, g=g: e.copy(init[:, g:g + 1], ip[:, g:g + 1]), [ipn], [("init", g)], eng=S)
                    vop(lambda e, wst=wst, win=win, g=g: e.tensor_tensor_scan(wst[:], rho[:, g:g + 1].to_broadcast([128, 512]), win[:], init[:, g:g + 1], ALU.mult, ALU.add),
                        [rhon, winn, ("init", g)], [wstn])
                else:
                    vop(lambda e, wst=wst, win=win, g=g: e.tensor_tensor_scan(wst[:], rho[:, g:g + 1].to_broadcast([128, 512]), win[:], 0.0, ALU.mult, ALU.add),
                        [rhon, winn], [wstn])
                vop(lambda e, wst=wst, g=g: e.copy(wlast[:, g:g + 1], wst[:, 511:512]), [wstn], [("wlast", g)], eng=S)
                p1, p1n = p1p.next(); p2, p2n = p2p.next()
                vop(lambda e, p1=p1, wst=wst, g=g: e.tensor_tensor(p1[:], Ct[:, g, :], wst[:], ALU.mult), [("Ct", g), wstn], [p1n])
                vop(lambda e, p2=p2, wst=wst, g=g: e.tensor_tensor(p2[:], St[:, g, :], wst[:], ALU.mult), [("St", g), wstn], [p2n], eng=G)
                P.op(T_, lambda e, Y=Y, g=g, p1=p1, gp=gp: e.matmul(Y[:], LC1[:, g, :], p1[:], start=(gp == 0), stop=False), reads=["LC", p1n], writes=[Yn])
                P.op(T_, lambda e, Y=Y, g=g, p2=p2, gp=gp: e.matmul(Y[:], LC2[:, g, :], p2[:], start=False, stop=(gp == 7)), reads=["LC", p2n], writes=[Yn])
            yt, ytn = ytp.next(); yb, ybn = ybp.next()
            vop(lambda e, yt=yt, u32=u32, Y=Y, ct=ct: e.scalar_tensor_tensor(yt[:], u32[:], dsk[:, ct:ct + 1], Y[:], ALU.mult, ALU.add), [u32n, "ssmd", Yn], [ytn])
            vop(lambda e, yt=yt, yb=yb: e.activation(yb[:], yt[:], AF.Gelu), [ytn], [ybn], eng=S)
            P.dma("sync", yT[ct * 128:(ct + 1) * 128, c * 512:(c + 1) * 512], yb[:], reads=[ybn])


def gdn_host_params(j, conv_w, a_log, dt_bias, norm_w):
    cw = np.zeros((128, 3, 2, 4), np.float32)
    for s in range(3):
        for h in range(2):
            c0 = s * 2048 + (2 * j + h) * 128
            cw[:, s, h, :] = conv_w[:, c0:c0 + 128].T
    return {"g_cw": cw,
            "g_alog": np.ascontiguousarray(a_log[2 * j:2 * j + 2].reshape(2, 1)),
            "g_dtb": np.ascontiguousarray(dt_bias[2 * j:2 * j + 2].reshape(2, 1)),
            "g_nw": np.ascontiguousarray(norm_w.reshape(128, 1)),
            "g_ident": np.eye(128, dtype=np.float32)}

def emit_gdn(P, gin, oT, D, T=8192, dbg=None):
    nc = P.nc
    NB = T // 512
    def vop(fn, reads, writes, eng=V):
        P.op(eng, fn, reads=reads, writes=writes)
    def ld(name, shape, src):
        t = P.sbuf("sb_" + name, shape, F32)
        P.dma("sync", t[:], src, writes=[name])
        return t
    cw = ld("g_cw", [128, 3, 2, 4], D["g_cw"])
    alog = ld("g_alog", [2, 1], D["g_alog"])
    dtb = ld("g_dtb", [2, 1], D["g_dtb"])
    nw = ld("g_nw", [128, 1], D["g_nw"])
    ident = ld("g_ident", [128, 128], D["g_ident"])
    ones_bf = P.sbuf("g_ones", [128, 128], BF16)
    vop(lambda e: e.memset(ones_bf[:], 1.0), [], ["g_ones"])
    epsc = P.sbuf("g_eps", [128, 1], F32)
    vop(lambda e: e.memset(epsc[:], 1e-6), [], ["g_eps"])
    onec = P.sbuf("g_one", [128, 1], F32)
    vop(lambda e: e.memset(onec[:], 1.0), [], ["g_one"])
    RC = min(T, 2048)
    arow = P.sbuf("g_arow", [2, RC], F32); brow = P.sbuf("g_brow", [2, RC], F32); tr = P.sbuf("g_tr", [2, RC], F32)
    nega = P.sbuf("g_nega", [2, 1], F32)
    vop(lambda e: e.activation(nega[:], alog[:], AF.Exp), ["g_alog"], ["g_nega"], eng=S)
    vop(lambda e: e.tensor_scalar(nega[:], nega[:], -1.0, None, ALU.mult), ["g_nega"], ["g_nega"])
    scr = nc.dram_tensor("g_scr", [3, 2, T], F32).ap()
    for rc in range(T // RC):
        cs = slice(rc * RC, (rc + 1) * RC)
        P.dma("sync", arow[:], gin[1024:1026, cs], writes=["g_arow"])
        P.dma("sync", brow[:], gin[1026:1028, cs], writes=["g_brow"])
        vop(lambda e: e.activation(tr[:], arow[:], AF.Exp, bias=dtb[:, 0:1], scale=1.0), ["g_arow", "g_dtb"], ["g_tr"], eng=S)
        vop(lambda e: e.activation(tr[:], tr[:], AF.Ln, bias=onec[0:2, 0:1], scale=1.0), ["g_tr", "g_one"], ["g_tr"], eng=S)
        vop(lambda e: e.tensor_scalar(tr[:], tr[:], nega[:, 0:1], None, ALU.mult), ["g_tr", "g_nega"], ["g_tr"])
        vop(lambda e: e.activation(arow[:], tr[:], AF.Exp), ["g_tr"], ["g_arow"], eng=S)
        vop(lambda e: e.activation(brow[:], brow[:], AF.Sigmoid), ["g_brow"], ["g_brow"], eng=S)
        vop(lambda e: e.scalar_tensor_tensor(tr[:], arow[:], -1.0, brow[:], ALU.mult, ALU.mult), ["g_arow", "g_brow"], ["g_tr"])
        P.dma("sync", scr[0, :, cs], arow[:], reads=["g_arow"], writes=[("scr", 0, rc)])
        P.dma("sync", scr[1, :, cs], tr[:], reads=["g_tr"], writes=[("scr", 1, rc)])
        P.dma("sync", scr[2, :, cs], brow[:], reads=["g_brow"], writes=[("scr", 2, rc)])
    NRC = T // RC
    natok = P.sbuf("g_natok", [128, 2, T // 128], F32)
    for h in range(2):
        P.dma("sync", natok[:, h, :], scr[1, h].rearrange("(b p) -> p b", p=128), reads=[("scr", 1, r_) for r_ in range(NRC)], writes=[("natok", h)],
              allow_slow_non_contiguous=True)
    kcol = [P.sbuf("g_kcol%d" % h, [128, T], BF16) for h in range(2)]
    qcol = [P.sbuf("g_qcol%d" % h, [128, T], BF16) for h in range(2)]
    S32 = [[P.sbuf("g_S32_%d_%d" % (h, i), [128, 128], F32) for i in range(2)] for h in range(2)]
    Sbf = [[P.sbuf("g_Sbf_%d_%d" % (h, i), [128, 128], BF16) for i in range(2)] for h in range(2)]
    for h in range(2):
        vop(lambda e, h=h: e.memset(S32[h][0][:], 0.0), [], [("S32", h, 0)])
        vop(lambda e, h=h: e.memset(Sbf[h][0][:], 0.0), [], [("Sbf", h, 0)])
    rawp = Pool(P, "g_raw", 3, [128, 515], F32)
    accp = Pool(P, "g_acc", 3, [128, 512], F32)
    sqp = Pool(P, "g_sq", 2, [128, 512], BF16)
    rnp = Pool(P, "g_rn", 2, [128, 512], F32)
    bcp = Pool(P, "g_bc", 2, [128, 512], F32)
    A128p = [Pool(P, "g_A%d" % h, 2, [128, 512], F32) for h in range(2)]
    ktokp = [Pool(P, "g_kt%d" % h, 2, [128, 4, 128], BF16) for h in range(2)]
    bvtokp = [Pool(P, "g_bv%d" % h, 2, [128, 4, 128], F32) for h in range(2)]
    knp = Pool(P, "g_kn", 2, [128, 512], F32)
    bvp = Pool(P, "g_bvf", 2, [128, 512], F32)
    zp = [Pool(P, "g_z%d" % h, 2, [128, 512], F32) for h in range(2)]
    kmp = [Pool(P, "g_km%d" % h, 3, [128, 128], BF16) for h in range(2)]
    tmpp = [Pool(P, "g_tmp%d" % h, 2, [128, 128], BF16) for h in range(2)]
    pb = [P.psum("g_pb%d" % i, [128, 512], F32) for i in range(8)]
    ksps = [[(pb[0 + h][:, i * 128:(i + 1) * 128], ("ksps", h, i)) for i in range(2)] for h in range(2)]
    dsps = [[(pb[4 + h][:, i * 128:(i + 1) * 128], ("dsps", h, i)) for i in range(2)] for h in range(2)]
    ops_ = [[(pb[2 + h], ("ops", h, 0)) for i in range(2)] for h in range(2)]
    ssps = (pb[6], "ssps")
    trps = (pb[7], "trps")

    def conv_silu(h, s, blk):
        raw, rn_ = rawp.next(); acc, an = accp.next()
        r0 = s * 256 + h * 128
        t0 = blk * 512
        if blk == 0:
            vop(lambda e: e.memset(raw[:, 0:3], 0.0), [], [rn_])
            P.dma("sync", raw[:, 3:515], gin[r0:r0 + 128, 0:512], writes=[rn_])
        else:
            P.dma("sync", raw[:], gin[r0:r0 + 128, t0 - 3:t0 + 512], writes=[rn_])
        vop(lambda e: e.tensor_scalar(acc[:], raw[:, 0:512], cw[:, s, h, 0:1], None, ALU.mult), [rn_, "g_cw"], [an])
        for j in range(1, 4):
            vop(lambda e, j=j: e.scalar_tensor_tensor(acc[:], raw[:, j:j + 512], cw[:, s, h, j:j + 1], acc[:], ALU.mult, ALU.add), [rn_, "g_cw", an], [an])
        vop(lambda e: e.activation(acc[:], acc[:], AF.Silu), [an], [an], eng=S)
        return acc, an

    def rnorm(acc, an, scale):
        sq, sn = sqp.next(); r, rn_ = rnp.next()
        vop(lambda e: e.activation(sq[:], acc[:], AF.Square), [an], [sn], eng=S)
        P.op(T_, lambda e: e.matmul(ssps[0][:], ones_bf[:], sq[:], start=True, stop=True), reads=["g_ones", sn], writes=[ssps[1]])
        vop(lambda e: e.activation(r[:], ssps[0][:], AF.Sqrt, bias=epsc[:, 0:1], scale=1.0), [ssps[1], "g_eps"], [rn_], eng=S)
        vop(lambda e: e.reciprocal(r[:], r[:]), [rn_], [rn_])
        if scale != 1.0:
            vop(lambda e: e.tensor_scalar(r[:], r[:], scale, None, ALU.mult), [rn_], [rn_])
        return r, rn_

    blkstate = {}
    def prep(h, blk):
        t0 = blk * 512
        acc, an = conv_silu(h, 0, blk)
        r, rn_ = rnorm(acc, an, 128.0 ** -0.5)
        vop(lambda e, acc=acc, r=r: e.tensor_tensor(qcol[h][:, t0:t0 + 512], acc[:], r[:], ALU.mult), [an, rn_], [("qcol", h, blk)])
        acc, an = conv_silu(h, 1, blk)
        r, rn_ = rnorm(acc, an, 1.0)
        kn, knn = knp.next()
        vop(lambda e, acc=acc, r=r, kn=kn: e.tensor_tensor(kn[:], acc[:], r[:], ALU.mult), [an, rn_], [knn])
        vop(lambda e, kn=kn: e.copy(kcol[h][:, t0:t0 + 512], kn[:]), [knn], [("kcol", h, blk)], eng=S)
        acc, an = conv_silu(h, 2, blk)
        bc, bcn = bcp.next()
        P.dma("sync", bc[:], scr[2, h:h + 1, t0:t0 + 512].partition_broadcast(128), reads=[("scr", 2, t0 // RC)], writes=[bcn])
        bv, bvn = bvp.next()
        vop(lambda e, acc=acc, bv=bv, bc=bc: e.tensor_tensor(bv[:], acc[:], bc[:], ALU.mult), [an, bcn], [bvn], eng=G)
        A1, A1n = A128p[h].next()
        P.dma("sync", A1[:], scr[0, h:h + 1, t0:t0 + 512].partition_broadcast(128), reads=[("scr", 0, t0 // RC)], writes=[A1n])
        z, zn = zp[h].next()
        P.dma("sync", z[:], gin[768 + h * 128:768 + (h + 1) * 128, t0:t0 + 512], writes=[zn])
        vop(lambda e, z=z: e.activation(z[:], z[:], AF.Silu), [zn], [zn], eng=S)
        kt, ktn = ktokp[h].next(); bvt, bvtn = bvtokp[h].next()
        for src, srcn, dst, dstn in ((kn, knn, kt, ktn), (bv, bvn, bvt, bvtn)):
            for i in range(4):
                P.op(T_, lambda e, src=src, i=i: e.transpose(trps[0][:, i * 128:(i + 1) * 128], src[:, i * 128:(i + 1) * 128], ident[:]),
                     reads=[srcn, "g_ident"], writes=[trps[1]])
            vop(lambda e, dst=dst: e.tensor_copy(dst[:].rearrange("p a b -> p (a b)"), trps[0][:]), [trps[1]], [dstn])
        blkstate[(h, blk)] = dict(A=A1, An=A1n, kt=kt, ktn=ktn, bvt=bvt, bvtn=bvtn, z=z, zn=zn)

    cur = [0, 0]
    def token_step(h, t):
        blk, b4, p = t // 512, (t % 512) // 128, t % 128
        st = blkstate[(h, blk)]
        c = cur[h]; n = 1 - c
        km, kmn = kmp[h].next(); tm, tmn = tmpp[h].next()
        ks, ksn = ksps[h][t % 2]; ds, dsn = dsps[h][t % 2]
        ob, obn = ops_[h][blk % 2]
        b128 = t // 128
        vop(lambda e: e.tensor_scalar(km[:], st["kt"][:, b4, :], ident[:, p:p + 1], None, ALU.mult), [st["ktn"], "g_ident"], [kmn], eng=G)
        P.op(T_, lambda e: e.matmul(ks, kcol[h][:, b128 * 128:(b128 + 1) * 128], Sbf[h][c][:], start=True, stop=True),
             reads=[("kcol", h, blk), ("Sbf", h, c)], writes=[ksn])
        vop(lambda e: e.scalar_tensor_tensor(tm[:], ks, natok[:, h, b128:b128 + 1], st["bvt"][:, b4, :], ALU.mult, ALU.add),
            [ksn, ("natok", h), st["bvtn"]], [tmn])
        P.op(T_, lambda e: e.matmul(ds, km[:], tm[:], start=True, stop=True), reads=[kmn, tmn], writes=[dsn])
        tl = t % 512
        vop(lambda e: e.scalar_tensor_tensor(Sbf[h][n][:], S32[h][c][:], st["A"][:, tl:tl + 1], ds, ALU.mult, ALU.add),
            [("S32", h, c), st["An"], dsn], [("Sbf", h, n)])
        vop(lambda e: e.scalar_tensor_tensor(S32[h][n][:], S32[h][c][:], st["A"][:, tl:tl + 1], ds, ALU.mult, ALU.add),
            [("S32", h, c), st["An"], dsn], [("S32", h, n)])
        P.op(T_, lambda e: e.matmul(ob[:, tl:tl + 1], Sbf[h][n][:], qcol[h][:, t:t + 1], start=True, stop=True),
             reads=[("Sbf", h, n), ("qcol", h, blk)], writes=[obn])
        cur[h] = n

    onp = Pool(P, "g_on", 2, [128, 512], F32)
    obp = Pool(P, "g_ob", 2, [128, 512], BF16)
    def finish(h, blk):
        st = blkstate[(h, blk)]
        ob, obn = ops_[h][blk % 2]
        sq, sn = sqp.next(); r, rn_ = rnp.next(); on, onn = onp.next(); o16, o16n = obp.next()
        vop(lambda e: e.activation(sq[:], ob[:], AF.Square), [obn], [sn], eng=S)
        P.op(T_, lambda e: e.matmul(ssps[0][:], ones_bf[:], sq[:], start=True, stop=True), reads=["g_ones", sn], writes=[ssps[1]])
        vop(lambda e: e.activation(r[:], ssps[0][:], AF.Sqrt, bias=epsc[:, 0:1], scale=1.0 / 128), [ssps[1], "g_eps"], [rn_], eng=S)
        vop(lambda e: e.reciprocal(r[:], r[:]), [rn_], [rn_])
        vop(lambda e: e.scalar_tensor_tensor(on[:], ob[:], nw[:, 0:1], r[:], ALU.mult, ALU.mult), [obn, "g_nw", rn_], [onn])
        vop(lambda e: e.tensor_tensor(o16[:], on[:], st["z"][:], ALU.mult), [onn, st["zn"]], [o16n], eng=G)
        P.dma("sync", oT[h * 128:(h + 1) * 128, blk * 512:(blk + 1) * 512], o16[:], reads=[o16n])

    for h in range(2):
        prep(h, 0)
    if dbg is not None:
        st = blkstate[(0, 0)]
        P.dma("sync", dbg["d_q"], qcol[0][:, 0:512], reads=[("qcol", 0, 0)])
        P.dma("sync", dbg["d_k"], kcol[0][:, 0:512], reads=[("kcol", 0, 0)])
        P.dma("sync", dbg["d_kt"], st["kt"][:].rearrange("p a b -> p (a b)"), reads=[st["ktn"]])
        P.dma("sync", dbg["d_bvt"], st["bvt"][:].rearrange("p a b -> p (a b)"), reads=[st["bvtn"]])
        P.dma("sync", dbg["d_A"], st["A"][:], reads=[st["An"]])
        P.dma("sync", dbg["d_na"], natok[:, 0, :], reads=[("natok", 0)])
    for blk in range(NB):
        if blk + 1 < NB:
            for h in range(2):
                prep(h, blk + 1)
        for t in range(blk * 512, (blk + 1) * 512):
            for h in range(2):
                token_step(h, t)
        for h in range(2):
            finish(h, blk)

NEG = -30000.0

def gdn2_consts():
    i = np.arange(64)
    negs = np.where(i[None, :] > i[:, None], 0.0, NEG).astype(np.float32)
    nega = np.where(i[:, None] > i[None, :], 0.0, NEG).astype(np.float32)
    return {"g_negs": np.ascontiguousarray(np.tile(negs[:, None, :], (1, 8, 1))),
            "g_nega": np.ascontiguousarray(np.tile(nega[:, None, :], (1, 8, 1))),
            "g_id8": np.ascontiguousarray(np.tile(np.eye(64, dtype=np.float32)[:, None, :], (1, 8, 1)))}

def emit_gdn2(P, gin, oT, D, T=8192, dbg=None):
    nc = P.nc
    NB = T // 512
    NCH = T // 64
    def vop(fn, reads, writes, eng=V):
        P.op(eng, fn, reads=reads, writes=writes)
    def ld(name, shape, src):
        t = P.sbuf("sb_" + name, shape, F32)
        P.dma("sync", t[:], src, writes=[name])
        return t
    cw = ld("g_cw", [128, 3, 2, 4], D["g_cw"])
    alog = ld("g_alog", [2, 1], D["g_alog"])
    dtb = ld("g_dtb", [2, 1], D["g_dtb"])
    nw = ld("g_nw", [128, 1], D["g_nw"])
    ident = ld("g_ident", [128, 128], D["g_ident"])
    negs = ld("g_negs", [64, 8, 64], D["g_negs"])
    negA = ld("g_nega", [64, 8, 64], D["g_nega"])
    id8 = ld("g_id8", [64, 8, 64], D["g_id8"])
    identb = P.sbuf("g_identb", [128, 128], BF16)
    vop(lambda e: e.tensor_copy(identb[:], ident[:]), ["g_ident"], ["g_identb"])
    ones_bf = P.sbuf("g_ones", [128, 128], BF16)
    vop(lambda e: e.memset(ones_bf[:], 1.0), [], ["g_ones"])
    epsc = P.sbuf("g_eps", [128, 1], F32)
    vop(lambda e: e.memset(epsc[:], 1e-6), [], ["g_eps"])
    onec = P.sbuf("g_one", [128, 1], F32)
    vop(lambda e: e.memset(onec[:], 1.0), [], ["g_one"])
    RC = min(T, 2048)
    NRC = T // RC
    arow = P.sbuf("g_arow", [2, RC], F32); brow = P.sbuf("g_brow", [2, RC], F32); tr = P.sbuf("g_tr", [2, RC], F32)
    cmask = P.sbuf("g_cmask", [2, RC], F32)
    vop(lambda e: e.memset(cmask[:], 1.0), [], ["g_cmask"])
    vop(lambda e: e.memset(cmask[:].rearrange("p (c i) -> p c i", i=64)[:, :, 0:1], 0.0), ["g_cmask"], ["g_cmask"])
    nega_ = P.sbuf("g_negal", [2, 1], F32)
    vop(lambda e: e.activation(nega_[:], alog[:], AF.Exp), ["g_alog"], ["g_negal"], eng=S)
    vop(lambda e: e.tensor_scalar(nega_[:], nega_[:], -1.0, None, ALU.mult), ["g_negal"], ["g_negal"])
    scr = nc.dram_tensor("g_scr", [2, 2, T], F32).ap()
    for rc in range(NRC):
        cs = slice(rc * RC, (rc + 1) * RC)
        P.dma("sync", arow[:], gin[1024:1026, cs], writes=["g_arow"])
        P.dma("sync", brow[:], gin[1026:1028, cs], writes=["g_brow"])
        vop(lambda e: e.activation(tr[:], arow[:], AF.Exp, bias=dtb[:, 0:1], scale=1.0), ["g_arow", "g_dtb"], ["g_tr"], eng=S)
        vop(lambda e: e.activation(tr[:], tr[:], AF.Ln, bias=onec[0:2, 0:1], scale=1.0), ["g_tr", "g_one"], ["g_tr"], eng=S)
        vop(lambda e: e.tensor_scalar(tr[:], tr[:], nega_[:, 0:1], None, ALU.mult), ["g_tr", "g_negal"], ["g_tr"])
        vop(lambda e: e.tensor_tensor_scan(arow[:], cmask[:], tr[:], 0.0, ALU.mult, ALU.add), ["g_cmask", "g_tr", "g_arow"], ["g_arow"])
        vop(lambda e: e.activation(brow[:], brow[:], AF.Sigmoid), ["g_brow"], ["g_brow"], eng=S)
        P.dma("sync", scr[0, :, cs], arow[:], reads=["g_arow"], writes=[("scr", 0, rc)])
        P.dma("sync", scr[1, :, cs], brow[:], reads=["g_brow"], writes=[("scr", 1, rc)])
    gcol = P.sbuf("g_gcol", [64, 2, NCH], F32); bcol = P.sbuf("g_bcol", [64, 2, NCH], F32)
    for h in range(2):
        P.dma("sync", gcol[:, h, :], scr[0, h].rearrange("(c i) -> i c", i=64), reads=[("scr", 0, r_) for r_ in range(NRC)], writes=[("gcol", h)], allow_slow_non_contiguous=True)
        P.dma("sync", bcol[:, h, :], scr[1, h].rearrange("(c i) -> i c", i=64), reads=[("scr", 1, r_) for r_ in range(NRC)], writes=[("bcol", h)], allow_slow_non_contiguous=True)
    S32 = [[P.sbuf("g_S32_%d_%d" % (h, i), [128, 128], F32) for i in range(2)] for h in range(2)]
    Sbf = [[P.sbuf("g_Sbf_%d_%d" % (h, i), [128, 128], BF16) for i in range(2)] for h in range(2)]
    for h in range(2):
        vop(lambda e, h=h: e.memset(S32[h][0][:], 0.0), [], [("S32", h, 0)])
        vop(lambda e, h=h: e.memset(Sbf[h][0][:], 0.0), [], [("Sbf", h, 0)])
    def mk(name, n, shape, dt=F32):
        return Pool(P, name, n, shape, dt)
    rawp = mk("g_raw", 3, [128, 515]); accp = mk("g_acc", 4, [128, 512]); sqp = mk("g_sq", 2, [128, 512], BF16); rnp = mk("g_rn", 2, [128, 512])
    f64p = mk("g_f64", 14, [64, 8, 64])
    b64p = mk("g_b64", 4, [64, 8, 64], BF16)
    bigp = mk("g_big", 5, [128, 512])
    hbfp = mk("g_hbf", 4, [128, 512], BF16)
    tokp = mk("g_tok", 3, [64, 8, 128], BF16)
    smallp = mk("g_small", 8, [64, 8])
    vnp = [mk("g_vn%d" % h, 2, [64, 128], BF16) for h in range(2)]
    p_qg = [mk("g_pqg%d" % h, 2, [128, 512], BF16) for h in range(2)]
    p_WT = [mk("g_pWT%d" % h, 2, [128, 512], BF16) for h in range(2)]
    p_z = [mk("g_pz%d" % h, 2, [128, 512]) for h in range(2)]
    p_QKm = [mk("g_pQK%d" % h, 2, [64, 8, 64], BF16) for h in range(2)]
    p_U = [mk("g_pU%d" % h, 2, [64, 8, 128]) for h in range(2)]
    p_Kd = [mk("g_pKd%d" % h, 2, [64, 8, 128], BF16) for h in range(2)]
    p_egl = [mk("g_pegl%d" % h, 2, [128, 8]) for h in range(2)]
    onp = mk("g_on", 2, [128, 512]); obp = mk("g_ob", 2, [128, 512], BF16)
    gpp = Pool(P, "g_gp", 3, [128, 512], F32, psum=True)
    trps = P.psum("g_trps", [128, 1024], BF16)
    chps = [P.psum("g_chps%d" % h, [128, 512], F32) for h in range(2)]
    ops_ = [P.psum("g_ops%d" % h, [128, 512], F32) for h in range(2)]

    def conv_silu(h, s, blk):
        raw, rn_ = rawp.next(); acc, an = accp.next()
        r0 = s * 256 + h * 128
        t0 = blk * 512
        if blk == 0:
            vop(lambda e: e.memset(raw[:, 0:3], 0.0), [], [rn_])
            P.dma("sync", raw[:, 3:515], gin[r0:r0 + 128, 0:512], writes=[rn_])
        else:
            P.dma("sync", raw[:], gin[r0:r0 + 128, t0 - 3:t0 + 512], writes=[rn_])
        vop(lambda e: e.tensor_scalar(acc[:], raw[:, 0:512], cw[:, s, h, 0:1], None, ALU.mult), [rn_, "g_cw"], [an])
        for j in range(1, 4):
            vop(lambda e, j=j: e.scalar_tensor_tensor(acc[:], raw[:, j:j + 512], cw[:, s, h, j:j + 1], acc[:], ALU.mult, ALU.add), [rn_, "g_cw", an], [an])
        vop(lambda e: e.activation(acc[:], acc[:], AF.Silu), [an], [an], eng=S)
        return acc, an

    def rnorm(acc, an, scale):
        sq, sn = sqp.next(); r, rn_ = rnp.next(); ps, pn = gpp.next()
        vop(lambda e: e.activation(sq[:], acc[:], AF.Square), [an], [sn], eng=S)
        P.op(T_, lambda e: e.matmul(ps[:], ones_bf[:], sq[:], start=True, stop=True), reads=["g_ones", sn], writes=[pn])
        vop(lambda e: e.activation(r[:], ps[:], AF.Sqrt, bias=epsc[:, 0:1], scale=1.0), [pn, "g_eps"], [rn_], eng=S)
        vop(lambda e: e.reciprocal(r[:], r[:]), [rn_], [rn_])
        if scale != 1.0:
            vop(lambda e: e.tensor_scalar(r[:], r[:], scale, None, ALU.mult), [rn_], [rn_])
        return r, rn_

    blkstate = {}
    def prep(h, blk):
        t0 = blk * 512
        c0 = blk * 8
        st = {}
        Gr, Grn = bigp.next()
        P.dma("sync", Gr[:], scr[0, h:h + 1, t0:t0 + 512].partition_broadcast(128), reads=[("scr", 0, t0 // RC)], writes=[Grn])
        Br, Brn = bigp.next()
        P.dma("sync", Br[0:64, :], scr[1, h:h + 1, t0:t0 + 512].partition_broadcast(64), reads=[("scr", 1, t0 // RC)], writes=[Brn])
        Gr3 = Gr[0:64, :].rearrange("p (c i) -> p c i", i=64)
        Br3 = Br[0:64, :].rearrange("p (c i) -> p c i", i=64)
        gcb = gcol[:, h, c0:c0 + 8]; bcb = bcol[:, h, c0:c0 + 8]
        gct, bct = ("gcol", h), ("bcol", h)
        eG, eGn = bigp.next()
        vop(lambda e: e.activation(eG[:], Gr[:], AF.Exp), [Grn], [eGn], eng=S)
        egl, egln = p_egl[h].next()
        vop(lambda e: e.tensor_copy(egl[:], eG[:].rearrange("p (c i) -> p c i", i=64)[:, :, 63]), [eGn], [egln])
        bg, bgn = smallp.next(); ed, edn = smallp.next()
        vop(lambda e: e.activation(bg[:], gcb, AF.Exp), [gct], [bgn], eng=S)
        vop(lambda e: e.tensor_tensor(bg[:], bg[:], bcb, ALU.mult), [bgn, bct], [bgn])
        vop(lambda e: e.tensor_tensor(ed[:], Gr3[:, :, 63], gcb, ALU.subtract), [Grn, gct], [edn])
        vop(lambda e: e.activation(ed[:], ed[:], AF.Exp), [edn], [edn], eng=S)
        acc, an = conv_silu(h, 0, blk)
        r, rn_ = rnorm(acc, an, 128.0 ** -0.5)
        qn, qnn = bigp.next()
        vop(lambda e, acc=acc, r=r: e.tensor_tensor(qn[:], acc[:], r[:], ALU.mult), [an, rn_], [qnn])
        qb, qbn = hbfp.next(); qg, qgn = p_qg[h].next()
        vop(lambda e: e.copy(qb[:], qn[:]), [qnn], [qbn], eng=S)
        vop(lambda e: e.tensor_tensor(qg[:], qn[:], eG[:], ALU.mult), [qnn, eGn], [qgn], eng=G)
        acc, an = conv_silu(h, 1, blk)
        r, rn_ = rnorm(acc, an, 1.0)
        kb, kbn = hbfp.next()
        vop(lambda e, acc=acc, r=r: e.tensor_tensor(kb[:], acc[:], r[:], ALU.mult), [an, rn_], [kbn])
        acc, an = conv_silu(h, 2, blk)
        vb, vbn = hbfp.next()
        vop(lambda e, acc=acc: e.copy(vb[:], acc[:]), [an], [vbn], eng=S)
        z, zn = p_z[h].next()
        P.dma("sync", z[:], gin[768 + h * 128:768 + (h + 1) * 128, t0:t0 + 512], writes=[zn])
        vop(lambda e: e.activation(z[:], z[:], AF.Silu), [zn], [zn], eng=S)
        KK, KKn = gpp.next(); QK, QKn = gpp.next()
        for c in range(8):
            cs = slice(c * 64, (c + 1) * 64)
            P.op(T_, lambda e, cs=cs: e.matmul(KK[0:64, cs], kb[:, cs], kb[:, cs], start=True, stop=True), reads=[kbn], writes=[KKn])
            P.op(T_, lambda e, cs=cs: e.matmul(QK[0:64, cs], kb[:, cs], qb[:, cs], start=True, stop=True), reads=[kbn, qbn], writes=[QKn])
        KK3 = KK[0:64, :].rearrange("p (c i) -> p c i", i=64)
        QK3 = QK[0:64, :].rearrange("p (c i) -> p c i", i=64)
        gcb_b = gcb.unsqueeze(2).to_broadcast([64, 8, 64])
        bcb_b = bcb.unsqueeze(2).to_broadcast([64, 8, 64])
        ES, ESn = f64p.next(); EA, EAn = f64p.next()
        vop(lambda e: e.tensor_tensor(ES[:], Gr3, negs[:], ALU.add), [Grn, "g_negs"], [ESn], eng=G)
        vop(lambda e: e.tensor_tensor(ES[:], ES[:], gcb_b, ALU.subtract), [ESn, gct], [ESn])
        vop(lambda e: e.activation(ES[:], ES[:], AF.Exp), [ESn], [ESn], eng=S)
        vop(lambda e: e.scalar_tensor_tensor(EA[:], Gr3, -1.0, negA[:], ALU.mult, ALU.add), [Grn, "g_nega"], [EAn])
        vop(lambda e: e.tensor_tensor(EA[:], EA[:], gcb_b, ALU.add), [EAn, gct], [EAn])
        vop(lambda e: e.activation(EA[:], EA[:], AF.Exp), [EAn], [EAn], eng=S)
        Pk, Pkn = f64p.next(); PTk, PTkn = f64p.next()
        vop(lambda e: e.scalar_tensor_tensor(Pk[:], KK3, -1.0, EA[:], ALU.mult, ALU.mult), [KKn, EAn], [Pkn])
        vop(lambda e: e.tensor_tensor(Pk[:], Pk[:], bcb_b, ALU.mult), [Pkn, bct], [Pkn], eng=G)
        vop(lambda e: e.scalar_tensor_tensor(PTk[:], KK3, -1.0, ES[:], ALU.mult, ALU.mult), [KKn, ESn], [PTkn])
        vop(lambda e: e.tensor_tensor(PTk[:], PTk[:], Br3, ALU.mult), [PTkn, Brn], [PTkn], eng=G)
        if dbg is not None and h == 0 and blk == 0:
            P.dma("sync", dbg["d_M"], Pk[:].rearrange("p c i -> p (c i)"), reads=[Pkn])
            P.dma("sync", dbg["d_ES"], ES[:].rearrange("p c i -> p (c i)"), reads=[ESn])
            P.dma("sync", dbg["d_EA"], EA[:].rearrange("p c i -> p (c i)"), reads=[EAn])
            P.dma("sync", dbg["d_MT"], PTk[:].rearrange("p c i -> p (c i)"), reads=[PTkn])
        QKm, QKmn = p_QKm[h].next()
        vop(lambda e: e.tensor_tensor(ES[:], ES[:], id8[:], ALU.add), [ESn, "g_id8"], [ESn], eng=G)
        vop(lambda e: e.tensor_tensor(QKm[:], QK3, ES[:], ALU.mult), [QKn, ESn], [QKmn])
        TT, TTn = f64p.next()
        vop(lambda e: e.tensor_tensor(TT[:], PTk[:], id8[:], ALU.add), [PTkn, "g_id8"], [TTn], eng=G)
        Pc, Pcn, PTc, PTcn, TTc, TTcn = Pk, Pkn, PTk, PTkn, TT, TTn
        for k in range(1, 6):
            Pp, Ppn = gpp.next()
            for c in range(8):
                P.op(T_, lambda e, c=c, Pp=Pp, PTc=PTc, Pc=Pc: e.matmul(Pp[0:64, c * 64:(c + 1) * 64], PTc[:, c, :], Pc[:, c, :], start=True, stop=True), reads=[PTcn, Pcn], writes=[Ppn])
            if k < 5:
                PTp, PTpn = gpp.next()
                for c in range(8):
                    P.op(T_, lambda e, c=c, PTp=PTp, PTc=PTc, Pc=Pc: e.matmul(PTp[0:64, c * 64:(c + 1) * 64], Pc[:, c, :], PTc[:, c, :], start=True, stop=True), reads=[PTcn, Pcn], writes=[PTpn])
            Pn_, Pnn = f64p.next()
            vop(lambda e, Pn_=Pn_, Pp=Pp: e.copy(Pn_[:].rearrange("p c i -> p (c i)"), Pp[0:64, :]), [Ppn], [Pnn], eng=S)
            if k < 5:
                PTn_, PTnn = f64p.next()
                vop(lambda e, PTn_=PTn_, PTp=PTp: e.tensor_copy(PTn_[:].rearrange("p c i -> p (c i)"), PTp[0:64, :]), [PTpn], [PTnn])
            Tu, Tun = gpp.next()
            for c in range(8):
                P.op(T_, lambda e, c=c, Tu=Tu, Pn_=Pn_, TTc=TTc: e.matmul(Tu[0:64, c * 64:(c + 1) * 64], Pn_[:, c, :], TTc[:, c, :], start=True, stop=True), reads=[Pnn, TTcn], writes=[Tun])
            TT2, TT2n = f64p.next()
            vop(lambda e, TT2=TT2, TTc=TTc, Tu=Tu: e.tensor_tensor(TT2[:].rearrange("p c i -> p (c i)"), TTc[:].rearrange("p c i -> p (c i)"), Tu[0:64, :], ALU.add), [TTcn, Tun], [TT2n])
            TTc, TTcn = TT2, TT2n
            Pc, Pcn = Pn_, Pnn
            if k < 5:
                PTc, PTcn = PTn_, PTnn
            if dbg is not None and h == 0 and blk == 0 and k == 1:
                P.dma("sync", dbg["d_P1"], Pc[:].rearrange("p c i -> p (c i)"), reads=[Pcn])
                P.dma("sync", dbg["d_T1"], TTc[:].rearrange("p c i -> p (c i)"), reads=[TTcn])
        TTb, TTbn = b64p.next()
        vop(lambda e, TTc=TTc: e.copy(TTb[:], TTc[:]), [TTcn], [TTbn], eng=S)
        Kw, Kwn = tokp.next(); Kd, Kdn = p_Kd[h].next(); Vb, Vbn = tokp.next()
        for c in range(8):
            P.op(T_, lambda e, c=c: e.transpose(trps[0:64, c * 128:(c + 1) * 128], kb[:, c * 64:(c + 1) * 64], identb[:]), reads=[kbn, "g_identb"], writes=["g_trps"])
        ktv = trps[0:64, :].rearrange("p (c d) -> p c d", d=128)
        vop(lambda e: e.tensor_tensor(Kw[:], ktv, bg[:].unsqueeze(2).to_broadcast([64, 8, 128]), ALU.mult), ["g_trps", bgn], [Kwn])
        vop(lambda e: e.tensor_tensor(Kd[:], ktv, ed[:].unsqueeze(2).to_broadcast([64, 8, 128]), ALU.mult), ["g_trps", edn], [Kdn])
        for c in range(8):
            P.op(T_, lambda e, c=c: e.transpose(trps[0:64, c * 128:(c + 1) * 128], vb[:, c * 64:(c + 1) * 64], identb[:]), reads=[vbn, "g_identb"], writes=["g_trps"])
        vop(lambda e: e.tensor_tensor(Vb[:], ktv, bcb.unsqueeze(2).to_broadcast([64, 8, 128]), ALU.mult), ["g_trps", bct], [Vbn])
        U, Un = p_U[h].next()
        for half in range(2):
            Ups, Upsn = gpp.next()
            for cc in range(4):
                c = half * 4 + cc
                P.op(T_, lambda e, c=c, cc=cc, Ups=Ups: e.matmul(Ups[0:64, cc * 128:(cc + 1) * 128], TTb[:, c, :], Vb[:, c, :], start=True, stop=True), reads=[TTbn, Vbn], writes=[Upsn])
            vop(lambda e, half=half, Ups=Ups: e.copy(U[:, half * 4:half * 4 + 4, :].rearrange("p c d -> p (c d)"), Ups[0:64, :]), [Upsn], [Un], eng=S)
        Wps, Wpsn = gpp.next()
        for c in range(8):
            P.op(T_, lambda e, c=c: e.matmul(Wps[:, c * 64:(c + 1) * 64], Kw[:, c, :], TTb[:, c, :], start=True, stop=True), reads=[Kwn, TTbn], writes=[Wpsn])
        WT, WTn = p_WT[h].next()
        vop(lambda e: e.tensor_copy(WT[:], Wps[:]), [Wpsn], [WTn])
        if dbg is not None and h == 0 and blk == 0:
            P.dma("sync", dbg["d_TT"], TTc[:].rearrange("p c i -> p (c i)"), reads=[TTcn])
            P.dma("sync", dbg["d_U"], U[:].rearrange("p c i -> p (c i)"), reads=[Un])
            P.dma("sync", dbg["d_Gr"], Gr[:], reads=[Grn])
            P.dma("sync", dbg["d_gcol"], gcol[:, 0, :], reads=[("gcol", 0)])
        st.update(dict(egl=egl, egln=egln, qg=qg, qgn=qgn, QKm=QKm, QKmn=QKmn, U=U, Un=Un, WT=WT, WTn=WTn, Kd=Kd, Kdn=Kdn, z=z, zn=zn))
        blkstate[(h, blk)] = st

    cur = [0, 0]
    def chunk_step(h, blk, c):
        st = blkstate[(h, blk)]
        ci = cur[h]; n = 1 - ci
        cs = slice(c * 64, (c + 1) * 64)
        ws = chps[h][0:64, 0:128]; wsn = ("chps", h, 0)
        ds = chps[h][:, 128:256]; dsn = ("chps", h, 1)
        vn, vnn = vnp[h].next()
        ob = ops_[h]; obn = ("ops", h)
        P.op(T_, lambda e: e.matmul(ws, st["WT"][:, cs], Sbf[h][ci][:], start=True, stop=True), reads=[st["WTn"], ("Sbf", h, ci)], writes=[wsn])
        vop(lambda e: e.tensor_tensor(vn[:], st["U"][:, c, :], ws, ALU.subtract), [st["Un"], wsn], [vnn])
        P.op(T_, lambda e: e.matmul(ob[:, cs], Sbf[h][ci][:], st["qg"][:, cs], start=True, stop=False), reads=[("Sbf", h, ci), st["qgn"]], writes=[obn])
        P.op(T_, lambda e: e.matmul(ob[:, cs], vn[:], st["QKm"][:, c, :], start=False, stop=True), reads=[vnn, st["QKmn"]], writes=[obn])
        P.op(T_, lambda e: e.matmul(ds, st["Kd"][:, c, :], vn[:], start=True, stop=True), reads=[st["Kdn"], vnn], writes=[dsn])
        vop(lambda e: e.scalar_tensor_tensor(Sbf[h][n][:], S32[h][ci][:], st["egl"][:, c:c + 1], ds, ALU.mult, ALU.add),
            [("S32", h, ci), st["egln"], dsn], [("Sbf", h, n)])
        vop(lambda e: e.scalar_tensor_tensor(S32[h][n][:], S32[h][ci][:], st["egl"][:, c:c + 1], ds, ALU.mult, ALU.add),
            [("S32", h, ci), st["egln"], dsn], [("S32", h, n)])
        cur[h] = n

    def finish(h, blk):
        st = blkstate[(h, blk)]
        ob = ops_[h]; obn = ("ops", h)
        sq, sn = sqp.next(); r, rn_ = rnp.next(); on, onn = onp.next(); o16, o16n = obp.next(); ps, pn = gpp.next()
        vop(lambda e: e.activation(sq[:], ob[:], AF.Square), [obn], [sn], eng=S)
        P.op(T_, lambda e: e.matmul(ps[:], ones_bf[:], sq[:], start=True, stop=True), reads=["g_ones", sn], writes=[pn])
        vop(lambda e: e.activation(r[:], ps[:], AF.Sqrt, bias=epsc[:, 0:1], scale=1.0 / 128), [pn, "g_eps"], [rn_], eng=S)
        vop(lambda e: e.reciprocal(r[:], r[:]), [rn_], [rn_])
        vop(lambda e: e.scalar_tensor_tensor(on[:], ob[:], nw[:, 0:1], r[:], ALU.mult, ALU.mult), [obn, "g_nw", rn_], [onn])
        vop(lambda e: e.tensor_tensor(o16[:], on[:], st["z"][:], ALU.mult), [onn, st["zn"]], [o16n], eng=G)
        P.dma("sync", oT[h * 128:(h + 1) * 128, blk * 512:(blk + 1) * 512], o16[:], reads=[o16n])

    for h in range(2):
        prep(h, 0)
    for blk in range(NB):
        if blk + 1 < NB:
            for h in range(2):
                prep(h, blk + 1)
        for c in range(8):
            for h in range(2):
                chunk_step(h, blk, c)
        for h in range(2):
            finish(h, blk)

EPS = 1e-6

def vop(P, fn, reads, writes, eng=V):
    P.op(eng, fn, reads=reads, writes=writes)

class Norm:
    def __init__(self, P, nfeat=4096):
        self.P = P
        self.ones = P.sbuf("n_ones", [128, 128], BF16)
        vop(P, lambda e: e.memset(self.ones[:], 1.0), [], ["n_ones"])
        self.epsc = P.sbuf("n_eps", [128, 1], F32)
        vop(P, lambda e: e.memset(self.epsc[:], EPS), [], ["n_eps"])
        self.xc = Pool(P, "n_xc", 3, [128, 4, 512], F32)
        self.xc2 = Pool(P, "n_xc2", 2, [128, 4, 512], F32)
        self.sq = Pool(P, "n_sq", 2, [128, 4, 512], BF16)
        self.ps = Pool(P, "n_ps", 1, [128, 512], F32, psum=True)
        self.r = Pool(P, "n_r", 2, [128, 512], F32)
        self.nfeat = nfeat

    def rstd(self, src, tb, KT, src_tok=None):
        P = self.P
        ps, pn = self.ps.next()
        sv = src.rearrange("(kt p) t -> p kt t", p=128)
        for c in range(KT // 4):
            xc, xn = self.xc.next(); sq, sn = self.sq.next()
            P.dma("sync", xc[:], sv[:, c * 4:c * 4 + 4, tb * 512:(tb + 1) * 512], reads=([(src_tok, c * 4 + k_, tb) for k_ in range(4)] if src_tok else []), writes=[xn])
            vop(P, lambda e, xc=xc, sq=sq: e.activation(sq[:], xc[:], AF.Square), [xn], [sn], eng=S)
            for k in range(4):
                P.op(T_, lambda e, ps=ps, sq=sq, k=k, c=c: e.matmul(ps[:], self.ones[:], sq[:, k, :], start=(c == 0 and k == 0), stop=(c == KT // 4 - 1 and k == 3)),
                     reads=["n_ones", sn], writes=[pn])
        return self.finish(ps, pn, KT * 128)

    def finish(self, ps, pn, n):
        P = self.P
        r, rn = self.r.next()
        vop(P, lambda e: e.activation(r[:], ps[:], AF.Sqrt, bias=self.epsc[:, 0:1], scale=1.0 / n), [pn, "n_eps"], [rn], eng=S)
        vop(P, lambda e: e.reciprocal(r[:], r[:]), [rn], [rn])
        return r, rn

    def to_bf16(self, src, nw, nwn, dst, dstn, T):
        P = self.P
        sv = src.rearrange("(kt p) t -> p kt t", p=128)
        for tb in range(T // 512):
            r, rn = self.rstd(src, tb, 32)
            for c in range(8):
                xc, xn = self.xc.next()
                P.dma("sync", xc[:], sv[:, c * 4:c * 4 + 4, tb * 512:(tb + 1) * 512], writes=[xn])
                for k in range(4):
                    kt = c * 4 + k
                    vop(P, lambda e, xc=xc, k=k, kt=kt, r=r, tb=tb: e.scalar_tensor_tensor(dst[:, kt, tb * 512:(tb + 1) * 512], xc[:, k, :], nw[:, kt:kt + 1], r[:], ALU.mult, ALU.mult),
                        [xn, nwn, rn], [dstn])

    def resid(self, raw, x, nw, nwn, out, T, raw_tok=None):
        P = self.P
        rv = raw.rearrange("(kt p) t -> p kt t", p=128)
        xv = x.rearrange("(kt p) t -> p kt t", p=128)
        ov = out.rearrange("(kt p) t -> p kt t", p=128)
        for tb in range(T // 512):
            r, rn = self.rstd(raw, tb, 32, src_tok=raw_tok)
            for c in range(8):
                xc, xn = self.xc.next(); x2, x2n = self.xc2.next()
                P.dma("sync", xc[:], rv[:, c * 4:c * 4 + 4, tb * 512:(tb + 1) * 512], reads=([(raw_tok, c * 4 + k_, tb) for k_ in range(4)] if raw_tok else []), writes=[xn])
                P.dma("sync", x2[:], xv[:, c * 4:c * 4 + 4, tb * 512:(tb + 1) * 512], writes=[x2n])
                for k in range(4):
                    kt = c * 4 + k
                    vop(P, lambda e, xc=xc, k=k, kt=kt, r=r: e.scalar_tensor_tensor(xc[:, k, :], xc[:, k, :], nw[:, kt:kt + 1], r[:], ALU.mult, ALU.mult), [xn, nwn, rn], [xn])
                vop(P, lambda e, xc=xc, x2=x2: e.tensor_tensor(x2[:], x2[:], xc[:], ALU.add), [xn, x2n], [x2n], eng=G)
                P.dma("sync", ov[:, c * 4:c * 4 + 4, tb * 512:(tb + 1) * 512], x2[:], reads=[x2n])

def ld_const(P, name, shape, src, dt=F32, eng="sync"):
    t = P.sbuf("sb_" + name, shape, dt)
    P.dma(eng, t[:], src, writes=[name])
    return t

NPROJ = 18560
def build_A():
    P = Prog()
    xT = P.dram("xT", [4096, 1024], F32, "ExternalInput")
    nwd = P.dram("nw", [128, 32], F32, "ExternalInput")
    w = P.dram("w", [4096, NPROJ], F32, "ExternalInput")
    out = P.dram("projT", [NPROJ, 1024], F32, "ExternalOutput")
    nw = ld_const(P, "nw", [128, 32], nwd)
    N = Norm(P)
    h = P.sbuf("hT", [128, 32, 1024], BF16)
    N.to_bf16(xT, nw, "nw", h, "hT", 1024)
    Gm = Gemm(P)
    Gm.run(h, "hT", 32, w, NPROJ, Gm.evac_to_dram(out))
    return P.build()

def build_C1a():
    P = Prog()
    yT = P.dram("yT", [2048, 1024], BF16, "ExternalInput")
    oT = P.dram("oT", [2048, 1024], BF16, "ExternalInput")
    gs = P.dram("gsT", [4096, 1024], F32, "ExternalInput")
    gd = P.dram("gdT", [4096, 1024], F32, "ExternalInput")
    wglu = P.dram("w_glu", [2048, 8192], F32, "ExternalInput")
    wgdn = P.dram("w_gdn", [2048, 4096], F32, "ExternalInput")
    mT = P.dram("mT", [4096, 1024], BF16, "ExternalOutput")
    y = P.sbuf("y_sb", [128, 16, 1024], BF16)
    o = P.sbuf("o_sb", [128, 16, 1024], BF16)
    P.dma("sync", y[:], yT.rearrange("(kt p) t -> p kt t", p=128), writes=["y_sb"])
    P.dma("sync", o[:], oT.rearrange("(kt p) t -> p kt t", p=128), writes=["o_sb"])
    Gm = Gemm(P, wbytes=8192, npsum=7)
    gsp = Pool(P, "c_gs", 2, [128, 512], F32); gdp = Pool(P, "c_gd", 2, [128, 512], F32)
    t1p = Pool(P, "c_t1", 2, [128, 512], F32); t2p = Pool(P, "c_t2", 2, [128, 512], F32)
    sbp = Pool(P, "c_sb", 2, [128, 512], F32); mp = Pool(P, "c_m", 3, [128, 512], BF16)
    for nt in range(32):
        stash = {}
        def ev_ab(i, tb, ps, ptok, stash=stash):
            stash[(i, tb)] = (ps, ptok)
        Gm.run(y, "y_sb", 16, wglu, 256, ev_ab, col_groups=[[(nt * 128, 128), (4096 + nt * 128, 128)]])
        def ev_g(i, tb, G_, Gn, stash=stash, nt=nt):
            A_, An = stash[(0, tb)]; B_, Bn = stash[(1, tb)]
            gst, gsn = gsp.next(); gdt, gdn = gdp.next(); t1, t1n = t1p.next(); t2, t2n = t2p.next(); sb, sbn = sbp.next(); m, mn = mp.next()
            rs = slice(nt * 128, (nt + 1) * 128); cs = slice(tb * 512, (tb + 1) * 512)
            P.dma("sync", gst[:], gs[rs, cs], writes=[gsn])
            P.dma("sync", gdt[:], gd[rs, cs], writes=[gdn])
            vop(P, lambda e: e.activation(sb[:], B_[:], AF.Sigmoid), [Bn], [sbn], eng=S)
            vop(P, lambda e: e.tensor_tensor(t1[:], A_[:], sb[:], ALU.mult), [An, sbn], [t1n])
            vop(P, lambda e: e.activation(gst[:], gst[:], AF.Sigmoid), [gsn], [gsn], eng=S)
            vop(P, lambda e: e.activation(gdt[:], gdt[:], AF.Sigmoid), [gdn], [gdn], eng=S)
            vop(P, lambda e: e.tensor_tensor(t1[:], t1[:], gst[:], ALU.mult), [t1n, gsn], [t1n])
            vop(P, lambda e: e.tensor_tensor(t2[:], G_[:], gdt[:], ALU.mult), [Gn, gdn], [t2n])
            vop(P, lambda e: e.tensor_tensor(m[:], t1[:], t2[:], ALU.add), [t1n, t2n], [mn])
            P.dma("sync", mT[rs, cs], m[:], reads=[mn])
        Gm.run(o, "o_sb", 16, wgdn, 128, ev_g, col_groups=[[(nt * 128, 128)]])
    return P.build()

def build_C1b():
    P = Prog()
    mT = P.dram("mT", [4096, 1024], BF16, "ExternalInput")
    xT = P.dram("xT", [4096, 1024], F32, "ExternalInput")
    w = P.dram("w_out", [4096, 4096], F32, "ExternalInput")
    nwd = P.dram("nw", [128, 32], F32, "ExternalInput")
    x1 = P.dram("x1T", [4096, 1024], F32, "ExternalOutput")
    raw = P.nc.dram_tensor("c_raw", [4096, 1024], F32).ap()
    nw = ld_const(P, "nw", [128, 32], nwd)
    m = P.sbuf("m_sb", [128, 32, 1024], BF16)
    P.dma("sync", m[:], mT.rearrange("(kt p) t -> p kt t", p=128), writes=["m_sb"])
    Gm = Gemm(P)
    Gm.run(m, "m_sb", 32, w, 4096, Gm.evac_to_dram(raw, tok="c_raw"))
    N = Norm(P)
    N.resid(raw, xT, nw, "nw", x1, 1024, raw_tok="c_raw")
    return P.build()

def build_C2(FF=16384):
    P = Prog()
    nq = FF // 4096
    x1 = P.dram("x1T", [4096, 1024], F32, "ExternalInput")
    w1 = P.dram("w1", [4096, FF], F32, "ExternalInput")
    w2 = P.dram("w2", [FF, 4096], F32, "ExternalInput")
    nw1d = P.dram("nw1", [128, 32], F32, "ExternalInput")
    nw2d = P.dram("nw2", [128, 32], F32, "ExternalInput")
    x2 = P.dram("x2T", [4096, 1024], F32, "ExternalOutput")
    hid = P.nc.dram_tensor("f_hid", [FF, 1024], BF16).ap()
    parts = [P.nc.dram_tensor("f_part%d" % q, [4096, 1024], F32).ap() for q in range(nq)]
    raw2 = P.nc.dram_tensor("f_raw2", [4096, 1024], F32).ap()
    nw1 = ld_const(P, "nw1", [128, 32], nw1d)
    nw2 = ld_const(P, "nw2", [128, 32], nw2d)
    N = Norm(P)
    h = P.sbuf("hT", [128, 32, 1024], BF16)
    N.to_bf16(x1, nw1, "nw1", h, "hT", 1024)
    Gm = Gemm(P)
    relp = Pool(P, "f_rel", 2, [128, 512], F32); hbp = Pool(P, "f_hb", 3, [128, 512], BF16)
    def ev1(nt, tb, ps, ptok):
        r, rn = relp.next(); hb, hbn = hbp.next()
        vop(P, lambda e: e.activation(r[:], ps[:], AF.Relu), [ptok], [rn], eng=S)
        vop(P, lambda e: e.tensor_tensor(hb[:], r[:], r[:], ALU.mult), [rn], [hbn])
        P.dma("sync", hid[nt * 128:(nt + 1) * 128, tb * 512:(tb + 1) * 512], hb[:], reads=[hbn], writes=[("hid", nt, tb)])
    Gm.run(h, "hT", 32, w1, FF, ev1)
    for q in range(nq):
        P.dma("sync", h[:], hid[q * 4096:(q + 1) * 4096, :].rearrange("(kt p) t -> p kt t", p=128), reads=[("hid", q * 32 + i_, tb_) for i_ in range(32) for tb_ in range(2)], writes=["hT"])
        ev0 = Gm.evac_to_dram(parts[q], tok=("part", q))
        Gm.run(h, "hT", 32, w2[q * 4096:(q + 1) * 4096, :], 4096, ev0)
    for tb in range(2):
        for c in range(8):
            acc, accn = N.xc2.next()
            sl = (slice(None), slice(c * 4, c * 4 + 4), slice(tb * 512, (tb + 1) * 512))
            P.dma("sync", acc[:], parts[0].rearrange("(kt p) t -> p kt t", p=128)[sl], reads=[(("part", 0), c * 4 + k_, tb) for k_ in range(4)], writes=[accn])
            for q in range(1, nq):
                xc, xn = N.xc.next()
                P.dma("sync", xc[:], parts[q].rearrange("(kt p) t -> p kt t", p=128)[sl], reads=[(("part", q), c * 4 + k_, tb) for k_ in range(4)], writes=[xn])
                vop(P, lambda e, acc=acc, xc=xc: e.tensor_tensor(acc[:], acc[:], xc[:], ALU.add), [accn, xn], [accn], eng=(V if q % 2 else G))
            P.dma("sync", raw2.rearrange("(kt p) t -> p kt t", p=128)[sl], acc[:], reads=[accn], writes=[("f_raw2", c * 4 + k_, tb) for k_ in range(4)])
    N.resid(raw2, x1, nw2, "nw2", x2, 1024, raw_tok="f_raw2")
    return P.build()


def build_BS():
    P = Prog()
    uT = P.dram("uT", [256, 8192], F32, "ExternalInput")
    yT = P.dram("yT", [256, 8192], BF16, "ExternalOutput")
    z16 = np.zeros((128, 64), np.float32)
    shapes = {"ssmA": [128, 5, 2, 64], "ssmB": [128, 3, 16], "ssmcc": [128, 2, 16, 16], "ssmd": [128, 2]}
    shapes.update({k: list(v.shape) for k, v in ssm_consts().items()})
    D = {k: P.dram(k, s, F32, "ExternalInput") for k, s in shapes.items()}
    emit_ssm(P, uT, yT, D, NCH=16)
    return P.build()


def build_BG():
    P = Prog()
    gin = P.dram("gin", [1028, 8192], F32, "ExternalInput")
    oT = P.dram("oT", [256, 8192], BF16, "ExternalOutput")
    shapes = {"g_cw": [128, 3, 2, 4], "g_alog": [2, 1], "g_dtb": [2, 1], "g_nw": [128, 1], "g_ident": [128, 128]}
    shapes.update({k: list(v.shape) for k, v in gdn2_consts().items()})
    D = {k: P.dram(k, s, F32, "ExternalInput") for k, s in shapes.items()}
    emit_gdn2(P, gin, oT, D, T=8192)
    return P.build()


def pack_w_in(w_in):
    ab = np.zeros((4096, 128), np.float32)
    for j in range(8):
        ab[:, 4 * j + 0] = w_in[:, 8192 + 2 * j]
        ab[:, 4 * j + 1] = w_in[:, 8192 + 2 * j + 1]
        ab[:, 4 * j + 2] = w_in[:, 8208 + 2 * j]
        ab[:, 4 * j + 3] = w_in[:, 8208 + 2 * j + 1]
    return np.ascontiguousarray(np.concatenate([w_in[:, 0:8192], w_in[:, 8224:], ab], axis=1))


_PROGS = {}


def _prog(name, builder):
    if name not in _PROGS:
        _PROGS[name] = builder()
    return _PROGS[name]


def _run(name, builder, ins):
    nc = _prog(name, builder)
    return run_bass_kernel_spmd(nc, ins, core_ids=list(range(8))).results


def _col(v):
    return np.ascontiguousarray(np.asarray(v, np.float32).reshape(32, 128).T)


def kernel(**inp):
    inp = {k: np.asarray(v) for k, v in inp.items()}
    C = np.ascontiguousarray
    x = inp["x"][0]
    xs = [C(x[c * 1024:(c + 1) * 1024].T) for c in range(8)]
    consts = ssm_consts()
    gconsts = gdn2_consts()
    for l in range(2):
        wp = pack_w_in(inp["w_in"][l])
        nw = _col(inp["mix_pre_w"][l])
        rA = _run("A", build_A, [{"xT": xs[c], "nw": nw, "w": wp} for c in range(8)])
        del wp
        projT = np.concatenate([r["projT"] for r in rA], axis=1)
        gates = [(C(r["projT"][10240:14336]), C(r["projT"][14336:18432])) for r in rA]
        del rA
        ins = []
        for j in range(8):
            d = ssm_host_inputs(j, inp["ssm_a_re"][l], inp["ssm_a_im"][l], inp["ssm_log_dt"][l], inp["ssm_b_re"][l],
                                inp["ssm_b_im"][l], inp["ssm_c_re"][l], inp["ssm_c_im"][l], inp["ssm_d"][l])
            d.update(consts)
            d["uT"] = C(projT[256 * j:256 * (j + 1)])
            ins.append(d)
        rS = _run("BS", build_BS, ins)
        yfull = np.concatenate([r["yT"] for r in rS], axis=0)
        ins = []
        for j in range(8):
            d = gdn_host_params(j, inp["conv_w"][l], inp["gdn_a_log"][l], inp["gdn_dt_bias"][l], inp["gdn_norm_w"][l])
            d.update(gconsts)
            d["gin"] = C(np.concatenate([projT[2048 + 256 * j:2048 + 256 * (j + 1)], projT[4096 + 256 * j:4096 + 256 * (j + 1)],
                                         projT[6144 + 256 * j:6144 + 256 * (j + 1)], projT[8192 + 256 * j:8192 + 256 * (j + 1)],
                                         projT[18432 + 4 * j:18432 + 4 * j + 4]], axis=0))
            ins.append(d)
        del projT
        rG = _run("BG", build_BG, ins)
        ofull = np.concatenate([r["oT"] for r in rG], axis=0)
        wglu = C(inp["w_glu"][l]); wgdn = C(inp["w_gdn_out"][l])
        rM = _run("C1a", build_C1a, [{"yT": C(yfull[:, c * 1024:(c + 1) * 1024]), "oT": C(ofull[:, c * 1024:(c + 1) * 1024]),
                                       "gsT": gates[c][0], "gdT": gates[c][1], "w_glu": wglu, "w_gdn": wgdn} for c in range(8)])
        del gates
        wout = C(inp["w_out"][l]); nwp = _col(inp["mix_post_w"][l])
        r1 = _run("C1b", build_C1b, [{"mT": rM[c]["mT"], "xT": xs[c], "w_out": wout, "nw": nwp} for c in range(8)])
        w1 = C(inp["w_ff1"][l]); w2 = C(inp["w_ff2"][l])
        n1 = _col(inp["ffn_pre_w"][l]); n2 = _col(inp["ffn_post_w"][l])
        r2 = _run("C2", build_C2, [{"x1T": r1[c]["x1T"], "w1": w1, "w2": w2, "nw1": n1, "nw2": n2} for c in range(8)])
        xs = [r2[c]["x2T"] for c in range(8)]
    out = np.concatenate([xc.T for xc in xs], axis=0)
    return np.ascontiguousarray(out.reshape(1, 8192, 4096).astype(np.float32))
```

```python
import numpy as np
from contextlib import ExitStack
V, S, G, T_ = "vector", "scalar", "gpsimd", "tensor"
import concourse.bass as bass
import concourse.mybir as mybir
from concourse.bass_utils import run_bass_kernel_spmd

F32 = mybir.dt.float32
BF16 = mybir.dt.bfloat16
ALU = mybir.AluOpType
AF = mybir.ActivationFunctionType
AX = mybir.AxisListType

SAME_ENGINE_SYNC = True
NDMASEM = 48
COMPUTE = ("tensor", "vector", "scalar", "gpsimd")
ENGS = ("tensor", "vector", "scalar", "gpsimd", "sync")


class Prog:
    def __init__(self, name="k"):
        self.nc = bass.Bass("TRN2", target_bir_lowering=False)
        self.es = ExitStack()
        self.ops = {e: [] for e in ENGS}
        self.last_w = {}
        self.readers = {}
        self.ndma = 0
        self.dma_ops = []
        self.waited = {e: {x: -1 for x in COMPUTE} for e in ENGS}
        self.dma_waited = {e: set() for e in ENGS}
        self.nbuf = 0

    def dram(self, name, shape, dt, kind):
        return self.nc.dram_tensor(name, list(shape), dt, kind=kind).ap()

    def sbuf(self, name, shape, dt):
        return self.es.enter_context(self.nc.sbuf_tensor(name, list(shape), dt))

    def psum(self, name, shape, dt=F32):
        return self.es.enter_context(self.nc.psum_tensor(name, list(shape), dt))

    def _dep(self, eng, rec, dep):
        kind, deng, didx = dep
        if kind == "c":
            if deng == eng and (eng == "tensor" or not SAME_ENGINE_SYNC):
                return
            if self.waited[eng][deng] >= didx:
                return
            self.waited[eng][deng] = didx
            rec["waits"].append(dep)
            self.ops[deng][didx]["inc"] = True
        else:
            if didx in self.dma_waited[eng]:
                return
            self.dma_waited[eng].add(didx)
            rec["waits"].append(dep)

    def op(self, eng, fn, reads=(), writes=(), dma=False):
        rec = {"fn": fn, "waits": [], "inc": False, "dma": None}
        idx = len(self.ops[eng])
        for b in reads:
            lw = self.last_w.get(b)
            if lw is not None:
                self._dep(eng, rec, lw)
        for b in writes:
            lw = self.last_w.get(b)
            if lw is not None:
                self._dep(eng, rec, lw)
            for r in self.readers.get(b, ()):
                self._dep(eng, rec, r)
        if dma:
            d = self.ndma
            self.ndma += 1
            rec["dma"] = d
            if d >= NDMASEM:
                self._dep(eng, rec, ("d", None, d - NDMASEM))
            me = ("d", eng, d)
            self.dma_ops.append((eng, idx))
        else:
            me = ("c", eng, idx)
        for b in reads:
            self.readers.setdefault(b, []).append(me)
        for b in writes:
            self.last_w[b] = me
            self.readers[b] = []
        self.ops[eng].append(rec)
        return me

    def dma(self, eng, out, in_, reads=(), writes=(), **kw):
        return self.op(eng, lambda e: e.dma_start(out=out, in_=in_, **kw), reads, writes, dma=True)

    def build(self):
        nc = self.nc
        for eng in ("sync", "gpsimd", "scalar"):
            mine = [d for (e, i) in self.dma_ops for d in [self.ops[e][i]["dma"]] if e == eng]
            if not mine:
                continue
            rec = {"fn": None, "waits": [], "inc": False, "dma": None}
            for d in mine:
                if d not in self.dma_waited[eng]:
                    rec["waits"].append(("d", None, d))
            self.ops[eng].append(rec)
        dsems = [self.es.enter_context(nc.semaphore("d%d" % i)) for i in range(NDMASEM)]
        EPOCH = 8192
        sems = {}
        for e in COMPUTE:
            c = 0
            for rec in self.ops[e]:
                if rec["inc"]:
                    rec["ticket"] = (c // EPOCH, c % EPOCH + 1)
                    c += 1
            sems[e] = [self.es.enter_context(nc.semaphore("s_%s%d" % (e, k))) for k in range(max(1, (c + EPOCH - 1) // EPOCH))]
        ops = self.ops

        def replay(ename, eh):
            for rec in ops[ename]:
                for (kind, deng, didx) in rec["waits"]:
                    if kind == "c":
                        ep, val = ops[deng][didx]["ticket"]
                        eh.wait_ge(sems[deng][ep], val)
                    else:
                        eh.wait_ge(dsems[didx % NDMASEM], 16 * (didx // NDMASEM + 1))
                if rec["fn"] is None:
                    continue
                ins = rec["fn"](eh)
                if rec["dma"] is not None:
                    ins.then_inc(dsems[rec["dma"] % NDMASEM], 16)
                elif rec["inc"]:
                    ins.then_inc(sems[ename][rec["ticket"][0]], 1)

        with nc.Block() as block:
            @block.tensor
            def _(t):
                replay("tensor", t)

            @block.vector
            def _(v):
                replay("vector", v)

            @block.scalar
            def _(s):
                replay("scalar", s)

            @block.gpsimd
            def _(g):
                replay("gpsimd", g)

            @block.sync
            def _(s):
                replay("sync", s)
        self.es.close()
        return nc


class Pool:
    def __init__(self, P, name, n, shape, dt, psum=False):
        self.bufs = [(P.psum if psum else P.sbuf)("%s%d" % (name, i), shape, dt) for i in range(n)]
        self.name = name
        self.i = 0

    def next(self):
        k = self.i % len(self.bufs)
        self.i += 1
        return self.bufs[k], (self.name, k)


class Gemm:
    def __init__(self, P, wbytes=16384, npsum=6, nstage=4, nwb=3):
        self.P = P
        self.wel = wbytes // 2
        self.wb = Pool(P, "wb", nwb, [128, self.wel], BF16)
        self.ps = Pool(P, "ps", npsum, [128, 512], F32, psum=True)
        self.stage = Pool(P, "stg", nstage, [128, 512], F32)
        self.evq = 0

    def run(self, actT, act_tok, KT, W, ncols, evac, ntb=2, col_groups=None):
        P = self.P
        gc = max(128, (self.wel // KT) // 128 * 128)
        gc = min(gc, 512)
        if col_groups is None:
            col_groups = []
            c = 0
            while c < ncols:
                w = min(gc, ncols - c)
                col_groups.append([(c, w)])
                c += w
        Wv = W.rearrange("(kt p) n -> p kt n", p=128)
        nt_idx = 0
        for grp in col_groups:
            wbuf, wtok = self.wb.next()
            tot = sum(w for _, w in grp)
            wv = wbuf[:, 0:KT * tot].rearrange("p (kt n) -> p kt n", kt=KT)
            off = 0
            for (c0, w) in grp:
                P.dma("gpsimd", wv[:, :, off:off + w], Wv[:, :, c0:c0 + w], writes=[wtok])
                off += w
            for j in range(tot // 128):
                for tb in range(ntb):
                    ps, ptok = self.ps.next()
                    for kt in range(KT):
                        P.op("tensor", (lambda e, ps=ps, kt=kt, j=j, tb=tb, wv=wv: e.matmul(
                            ps[:], wv[:, kt, j * 128:(j + 1) * 128], actT[:, kt, tb * 512:(tb + 1) * 512],
                            start=(kt == 0), stop=(kt == KT - 1))),
                            reads=[wtok, (act_tok(kt, tb) if callable(act_tok) else act_tok)], writes=[ptok])
                    evac(nt_idx, tb, ps, ptok)
                nt_idx += 1

    def evac_to_dram(self, outT, tok=None):
        P = self.P
        def ev(nt, tb, ps, ptok):
            st, stok = self.stage.next()
            eng = "vector" if (self.evq % 2 == 0) else "scalar"
            self.evq += 1
            if eng == "vector":
                P.op("vector", lambda e: e.tensor_copy(st[:], ps[:]), reads=[ptok], writes=[stok])
            else:
                P.op("scalar", lambda e: e.copy(st[:], ps[:]), reads=[ptok], writes=[stok])
            P.dma("sync", outT[nt * 128:(nt + 1) * 128, tb * 512:(tb + 1) * 512], st[:], reads=[stok], writes=([(tok, nt, tb)] if tok else []))
        return ev

import math
PI = math.pi

def ssm_host_inputs(j, a_re, a_im, log_dt, b_re, b_im, c_re, c_im, d_skip):
    G0 = 16 * j
    gs = np.arange(G0, G0 + 16)
    ssmA = np.zeros((128, 5, 2, 64), np.float32)
    for ct in range(2):
        for gp in range(8):
            g = G0 + ct * 8 + gp
            sl = slice(gp * 16, gp * 16 + 16)
            ssmA[sl, 0, ct, :] = a_re[g][None, :]
            ssmA[sl, 1, ct, :] = a_im[g][None, :]
            ssmA[sl, 2, ct, :] = log_dt[g]
            ssmA[sl, 3, ct, :] = b_re[g].T
            ssmA[sl, 4, ct, :] = b_im[g].T
    ssmB = np.zeros((128, 3, 16), np.float32)
    ssmB[:, 0, :] = np.concatenate([a_re[gs].T, a_re[gs].T], 0)
    ssmB[:, 1, :] = np.concatenate([a_im[gs].T, a_im[gs].T], 0)
    ssmB[:, 2, :] = log_dt[gs][None, :]
    cc = np.zeros((128, 2, 16, 16), np.float32)
    cre = np.transpose(c_re[gs], (2, 0, 1))
    cim = np.transpose(c_im[gs], (2, 0, 1))
    cc[0:64, 0] = cre; cc[64:128, 0] = cim
    cc[0:64, 1] = cim; cc[64:128, 1] = cre
    dsk = np.ascontiguousarray(d_skip[G0 * 16:G0 * 16 + 256].reshape(2, 128).T)
    return {"ssmA": ssmA, "ssmB": ssmB, "ssmcc": cc, "ssmd": dsk}

def ssm_consts():
    iota = np.tile(np.arange(512, dtype=np.float32)[None, :], (128, 1))
    ident = np.eye(128, dtype=np.float32)
    Jm = np.zeros((128, 128), np.float32)
    for n in range(64):
        Jm[64 + n, n] = -1.0
        Jm[n, 64 + n] = 1.0
    sgn = np.ones((128, 2), np.float32); sgn[64:, 0] = -1.0; sgn[:, 1] = -1.0
    mask = np.zeros((128, 16), np.float32)
    for p in range(128):
        mask[p, p // 16] = 1.0
        mask[p, 8 + p // 16] = -1.0
    return {"c_iota": iota, "c_ident": ident, "c_J": Jm, "c_sgn": sgn, "c_mask": mask}

def emit_ssm(P, uT, yT, D, NCH=16):
    V, S, G, T_ = "vector", "scalar", "gpsimd", "tensor"
    def ld(name, shape, src, dt=F32):
        t = P.sbuf("sb_" + name, shape, dt)
        P.dma("sync", t[:], src, writes=[name])
        return t
    sA = ld("ssmA", [128, 5, 2, 64], D["ssmA"])
    sB = ld("ssmB", [128, 3, 16], D["ssmB"])
    cc = ld("ssmcc", [128, 2, 16, 16], D["ssmcc"])
    dsk = ld("ssmd", [128, 2], D["ssmd"])
    iota = ld("c_iota", [128, 512], D["c_iota"])
    ident = ld("c_ident", [128, 128], D["c_ident"])
    Jm = ld("c_J", [128, 128], D["c_J"])
    sgn = ld("c_sgn", [128, 2], D["c_sgn"])
    mask = ld("c_mask", [128, 16], D["c_mask"])
    negpi = P.sbuf("negpi", [128, 1], F32)
    P.op(V, lambda e: e.memset(negpi[:], -PI), writes=["negpi"])

    cnt = [0]
    def tmp(shape, dt=F32):
        cnt[0] += 1
        n = "tmp%d" % cnt[0]
        return P.sbuf(n, shape, dt), n
    def vop(fn, reads, writes, eng=V):
        P.op(eng, fn, reads=reads, writes=writes)
    I32 = mybir.dt.int32
    C1 = 6.28125
    C2 = 2 * PI - C1
    sc_tmp = {}
    def sincos_into(x_ap, xn, s_ap, sn, c_ap, cn, shape):
        key = tuple(shape)
        if key not in sc_tmp:
            sc_tmp[key] = (tmp(shape), tmp(shape), tmp(shape), tmp(shape, I32), tmp(shape))
        (y, yn), (kf, kfn), (r, rn), (ki, kin), (m, mn) = sc_tmp[key]
        vop(lambda e: e.tensor_scalar(y[:], x_ap, PI, None, ALU.add), [xn], [yn])
        vop(lambda e: e.tensor_scalar(kf[:], y[:], 1.0 / (2 * PI), None, ALU.mult), [yn], [kfn])
        vop(lambda e: e.tensor_copy(ki[:], kf[:]), [kfn], [kin])
        vop(lambda e: e.tensor_copy(kf[:], ki[:]), [kin], [kfn])
        vop(lambda e: e.scalar_tensor_tensor(r[:], kf[:], -C1, y[:], ALU.mult, ALU.add), [kfn, yn], [rn])
        vop(lambda e: e.scalar_tensor_tensor(r[:], kf[:], -C2, r[:], ALU.mult, ALU.add), [kfn, rn], [rn])
        vop(lambda e: e.tensor_scalar(m[:], r[:], 0.0, 2 * PI, ALU.is_lt, ALU.mult), [rn], [mn])
        vop(lambda e: e.tensor_tensor(r[:], r[:], m[:], ALU.add), [rn, mn], [rn])
        vop(lambda e: e.tensor_scalar(m[:], r[:], 2 * PI, -2 * PI, ALU.is_ge, ALU.mult), [rn], [mn])
        vop(lambda e: e.tensor_tensor(r[:], r[:], m[:], ALU.add), [rn, mn], [rn])
        vop(lambda e: e.activation(s_ap, r[:], AF.Sin, bias=negpi[:, 0:1], scale=1.0), [rn, "negpi"], [sn], eng=S)
        vop(lambda e: e.tensor_scalar(y[:], r[:], 0.5 * PI, None, ALU.add), [rn], [yn])
        vop(lambda e: e.tensor_scalar(m[:], y[:], 2 * PI, -2 * PI, ALU.is_ge, ALU.mult), [yn], [mn])
        vop(lambda e: e.tensor_tensor(y[:], y[:], m[:], ALU.add), [yn, mn], [yn])
        vop(lambda e: e.activation(c_ap, y[:], AF.Sin, bias=negpi[:, 0:1], scale=1.0), [yn, "negpi"], [cn], eng=S)
    def sincos(x, xn, shape):
        s_, sn = tmp(shape); c_, cn = tmp(shape)
        sincos_into(x[:], xn, s_[:], sn, c_[:], cn, shape)
        return (s_, sn), (c_, cn)

    shA = [128, 128]
    def A(i):
        return sA[:, i].rearrange("p a n -> p (a n)")
    dtA, dtAn = tmp(shA); ardt, ardtn = tmp(shA); aidt, aidtn = tmp(shA); mag, magn = tmp(shA)
    vop(lambda e: e.activation(dtA[:], A(2), AF.Exp), ["ssmA"], [dtAn], eng=S)
    vop(lambda e: e.tensor_tensor(ardt[:], A(0), dtA[:], ALU.mult), ["ssmA", dtAn], [ardtn])
    vop(lambda e: e.tensor_tensor(aidt[:], A(1), dtA[:], ALU.mult), ["ssmA", dtAn], [aidtn])
    vop(lambda e: e.activation(mag[:], ardt[:], AF.Exp), [ardtn], [magn], eng=S)
    (sn_, snn), (cs_, csn) = sincos(aidt, aidtn, shA)
    abr, abrn = tmp(shA); abi, abin = tmp(shA); den, denn = tmp(shA); t0, t0n = tmp(shA); t1, t1n = tmp(shA)
    fr, frn = tmp(shA); fi, fin = tmp(shA); bbr, bbrn = tmp(shA); bbi, bbin = tmp(shA)
    vop(lambda e: e.tensor_tensor(abr[:], mag[:], cs_[:], ALU.mult), [magn, csn], [abrn])
    vop(lambda e: e.tensor_tensor(abi[:], mag[:], sn_[:], ALU.mult), [magn, snn], [abin])
    vop(lambda e: e.tensor_tensor(t0[:], A(0), A(0), ALU.mult), ["ssmA"], [t0n])
    vop(lambda e: e.tensor_tensor(t1[:], A(1), A(1), ALU.mult), ["ssmA"], [t1n])
    vop(lambda e: e.tensor_tensor(den[:], t0[:], t1[:], ALU.add), [t0n, t1n], [denn])
    vop(lambda e: e.reciprocal(den[:], den[:]), [denn], [denn])
    vop(lambda e: e.tensor_scalar(abr[:], abr[:], -1.0, None, ALU.add), [abrn], [abrn])
    vop(lambda e: e.tensor_tensor(t0[:], abr[:], A(0), ALU.mult), [abrn, "ssmA"], [t0n])
    vop(lambda e: e.tensor_tensor(t1[:], abi[:], A(1), ALU.mult), [abin, "ssmA"], [t1n])
    vop(lambda e: e.tensor_tensor(fr[:], t0[:], t1[:], ALU.add), [t0n, t1n], [frn])
    vop(lambda e: e.tensor_tensor(fr[:], fr[:], den[:], ALU.mult), [frn, denn], [frn])
    vop(lambda e: e.tensor_tensor(t0[:], abi[:], A(0), ALU.mult), [abin, "ssmA"], [t0n])
    vop(lambda e: e.tensor_tensor(t1[:], abr[:], A(1), ALU.mult), [abrn, "ssmA"], [t1n])
    vop(lambda e: e.tensor_tensor(fi[:], t0[:], t1[:], ALU.subtract), [t0n, t1n], [fin])
    vop(lambda e: e.tensor_tensor(fi[:], fi[:], den[:], ALU.mult), [fin, denn], [fin])
    vop(lambda e: e.tensor_tensor(t0[:], fr[:], A(3), ALU.mult), [frn, "ssmA"], [t0n])
    vop(lambda e: e.tensor_tensor(t1[:], fi[:], A(4), ALU.mult), [fin, "ssmA"], [t1n])
    vop(lambda e: e.tensor_tensor(bbr[:], t0[:], t1[:], ALU.subtract), [t0n, t1n], [bbrn])
    vop(lambda e: e.tensor_tensor(t0[:], fr[:], A(4), ALU.mult), [frn, "ssmA"], [t0n])
    vop(lambda e: e.tensor_tensor(t1[:], fi[:], A(3), ALU.mult), [fin, "ssmA"], [t1n])
    vop(lambda e: e.tensor_tensor(bbi[:], t0[:], t1[:], ALU.add), [t0n, t1n], [bbin])
    LB1 = P.sbuf("LB1", [128, 16, 128], BF16); LB2 = P.sbuf("LB2", [128, 16, 128], BF16)
    for g in range(16):
        ct, gp = g // 8, g % 8
        cs = slice(ct * 64, ct * 64 + 64)
        vop(lambda e, g=g, gp=gp, cs=cs: e.tensor_scalar(LB1[:, g, 0:64], bbr[:, cs], mask[:, gp:gp + 1], None, ALU.mult), [bbrn, "c_mask"], ["LB"])
        vop(lambda e, g=g, gp=gp, cs=cs: e.tensor_scalar(LB1[:, g, 64:128], bbi[:, cs], mask[:, gp:gp + 1], None, ALU.mult), [bbin, "c_mask"], ["LB"])
        vop(lambda e, g=g, gp=gp, cs=cs: e.tensor_scalar(LB2[:, g, 0:64], bbi[:, cs], mask[:, gp:gp + 1], None, ALU.mult), [bbin, "c_mask"], ["LB"])
        vop(lambda e, g=g, gp=gp, cs=cs: e.tensor_scalar(LB2[:, g, 64:128], bbr[:, cs], mask[:, 8 + gp:9 + gp], None, ALU.mult), [bbrn, "c_mask"], ["LB"])
    shB = [128, 16]
    dtB, dtBn = tmp(shB); th, thn = tmp(shB); rho, rhon = tmp(shB); thT, thTn = tmp(shB)
    vop(lambda e: e.activation(dtB[:], sB[:, 2, :], AF.Exp), ["ssmB"], [dtBn], eng=S)
    vop(lambda e: e.tensor_tensor(th[:], sB[:, 1, :], dtB[:], ALU.mult), ["ssmB", dtBn], [thn])
    vop(lambda e: e.tensor_tensor(rho[:], sB[:, 0, :], dtB[:], ALU.mult), ["ssmB", dtBn], [rhon])
    vop(lambda e: e.activation(rho[:], rho[:], AF.Exp), [rhon], [rhon], eng=S)
    vop(lambda e: e.tensor_scalar(thT[:], th[:], 512.0, None, ALU.mult), [thn], [thTn])
    (sT, sTn), (cT, cTn) = sincos(thT, thTn, shB)
    St = P.sbuf("St", [128, 16, 512], F32); Ct = P.sbuf("Ct", [128, 16, 512], F32)
    ang = P.sbuf("ang", [128, 512], F32)
    for g in range(16):
        vop(lambda e, g=g: e.tensor_scalar(ang[:], iota[:], th[:, g:g + 1], None, ALU.mult), ["c_iota", thn], ["ang"])
        sincos_into(ang[:], "ang", St[:, g, :], ("St", g), Ct[:, g, :], ("Ct", g), [128, 512])
    Rm = P.sbuf("Rm", [128, 16, 128], F32)
    rt, rtn = tmp([128, 128])
    for g in range(16):
        vop(lambda e, g=g: e.tensor_scalar(rt[:], Jm[:], sT[:, g:g + 1], None, ALU.mult), ["c_J", sTn], [rtn])
        vop(lambda e, g=g: e.scalar_tensor_tensor(Rm[:, g, :], ident[:], cT[:, g:g + 1], rt[:], ALU.mult, ALU.add), ["c_ident", cTn, rtn], [("Rm", g)])
    LC1 = P.sbuf("LC1", [128, 16, 128], BF16); LC2 = P.sbuf("LC2", [128, 16, 128], BF16)
    vop(lambda e: e.memset(LC1[:], 0.0), [], ["LC"])
    vop(lambda e: e.memset(LC2[:], 0.0), [], ["LC"])
    for g in range(16):
        gp = g % 8
        vop(lambda e, g=g, gp=gp: e.tensor_scalar(LC1[:, g, gp * 16:gp * 16 + 16], cc[:, 0, g, :], sgn[:, 0:1], None, ALU.mult), ["ssmcc", "c_sgn"], ["LC"])
        vop(lambda e, g=g, gp=gp: e.tensor_scalar(LC2[:, g, gp * 16:gp * 16 + 16], cc[:, 1, g, :], sgn[:, 1:2], None, ALU.mult), ["ssmcc", "c_sgn"], ["LC"])

    u32p = Pool(P, "u32", 3, [128, 512], F32)
    ubfp = Pool(P, "ubf", 3, [128, 512], BF16)
    abp = Pool(P, "abps", 4, [128, 512], F32, psum=True)
    yps = Pool(P, "yps", 2, [128, 512], F32, psum=True)
    ips = Pool(P, "ips", 1, [128, 16], F32, psum=True)
    t1p = Pool(P, "st1", 2, [128, 512], F32)
    t2p = Pool(P, "st2", 2, [128, 512], F32)
    winp = Pool(P, "win", 2, [128, 512], F32)
    wstp = Pool(P, "wst", 3, [128, 512], F32)
    p1p = Pool(P, "p1", 2, [128, 512], BF16)
    p2p = Pool(P, "p2", 2, [128, 512], BF16)
    ytp = Pool(P, "yt", 2, [128, 512], F32)
    ybp = Pool(P, "yb", 2, [128, 512], BF16)
    init = P.sbuf("init", [128, 16], F32)
    wlast = P.sbuf("wlast", [128, 16], F32)
    for c in range(NCH):
        for ct in range(2):
            u32, u32n = u32p.next(); ubf, ubfn = ubfp.next()
            P.dma("sync", u32[:], uT[ct * 128:(ct + 1) * 128, c * 512:(c + 1) * 512], writes=[u32n])
            vop(lambda e, ubf=ubf, u32=u32: e.copy(ubf[:], u32[:]), [u32n], [ubfn], eng=S)
            Y, Yn = yps.next()
            for gp in range(8):
                g = ct * 8 + gp
                Aps, An = abp.next(); Bps, Bn = abp.next()
                P.op(T_, lambda e, Aps=Aps, g=g, ubf=ubf: e.matmul(Aps[:], LB1[:, g, :], ubf[:], start=True, stop=True), reads=["LB", ubfn], writes=[An])
                P.op(T_, lambda e, Bps=Bps, g=g, ubf=ubf: e.matmul(Bps[:], LB2[:, g, :], ubf[:], start=True, stop=True), reads=["LB", ubfn], writes=[Bn])
                a1, a1n = t1p.next(); a2, a2n = t2p.next(); win, winn = winp.next(); wst, wstn = wstp.next()
                vop(lambda e, a1=a1, Aps=Aps, g=g: e.tensor_tensor(a1[:], Ct[:, g, :], Aps[:], ALU.mult), [("Ct", g), An], [a1n])
                vop(lambda e, a2=a2, Bps=Bps, g=g: e.tensor_tensor(a2[:], St[:, g, :], Bps[:], ALU.mult), [("St", g), Bn], [a2n])
                vop(lambda e, a1=a1, a2=a2, win=win: e.tensor_tensor(win[:], a1[:], a2[:], ALU.add), [a1n, a2n], [winn])
                if c > 0:
                    ip, ipn = ips.next()
                    P.op(T_, lambda e, ip=ip, g=g: e.matmul(ip[:, g:g + 1], Rm[:, g, :], wlast[:, g:g + 1], start=True, stop=True), reads=[("Rm", g), ("wlast", g)], writes=[ipn])
                    vop(lambda e, ip=ip, g=g: e.copy(init[:, g:g + 1], ip[:, g:g + 1]), [ipn], [("init", g)], eng=S)
                    vop(lambda e, wst=wst, win=win, g=g: e.tensor_tensor_scan(wst[:], rho[:, g:g + 1].to_broadcast([128, 512]), win[:], init[:, g:g + 1], ALU.mult, ALU.add),
                        [rhon, winn, ("init", g)], [wstn])
                else:
                    vop(lambda e, wst=wst, win=win, g=g: e.tensor_tensor_scan(wst[:], rho[:, g:g + 1].to_broadcast([128, 512]), win[:], 0.0, ALU.mult, ALU.add),
                        [rhon, winn], [wstn])
                vop(lambda e, wst=wst, g=g: e.copy(wlast[:, g:g + 1], wst[:, 511:512]), [wstn], [("wlast", g)], eng=S)
                p1, p1n = p1p.next(); p2, p2n = p2p.next()
                vop(lambda e, p1=p1, wst=wst, g=g: e.tensor_tensor(p1[:], Ct[:, g, :], wst[:], ALU.mult), [("Ct", g), wstn], [p1n])
                vop(lambda e, p2=p2, wst=wst, g=g: e.tensor_tensor(p2[:], St[:, g, :], wst[:], ALU.mult), [("St", g), wstn], [p2n], eng=G)
                P.op(T_, lambda e, Y=Y, g=g, p1=p1, gp=gp: e.matmul(Y[:], LC1[:, g, :], p1[:], start=(gp == 0), stop=False), reads=["LC", p1n], writes=[Yn])
                P.op(T_, lambda e, Y=Y, g=g, p2=p2, gp=gp: e.matmul(Y[:], LC2[:, g, :], p2[:], start=False, stop=(gp == 7)), reads=["LC", p2n], writes=[Yn])
            yt, ytn = ytp.next(); yb, ybn = ybp.next()
            vop(lambda e, yt=yt, u32=u32, Y=Y, ct=ct: e.scalar_tensor_tensor(yt[:], u32[:], dsk[:, ct:ct + 1], Y[:], ALU.mult, ALU.add), [u32n, "ssmd", Yn], [ytn])
            vop(lambda e, yt=yt, yb=yb: e.activation(yb[:], yt[:], AF.Gelu), [ytn], [ybn], eng=S)
            P.dma("sync", yT[ct * 128:(ct + 1) * 128, c * 512:(c + 1) * 512], yb[:], reads=[ybn])


def gdn_host_params(j, conv_w, a_log, dt_bias, norm_w):
    cw = np.zeros((128, 3, 2, 4), np.float32)
    for s in range(3):
        for h in range(2):
            c0 = s * 2048 + (2 * j + h) * 128
            cw[:, s, h, :] = conv_w[:, c0:c0 + 128].T
    return {"g_cw": cw,
            "g_alog": np.ascontiguousarray(a_log[2 * j:2 * j + 2].reshape(2, 1)),
            "g_dtb": np.ascontiguousarray(dt_bias[2 * j:2 * j + 2].reshape(2, 1)),
            "g_nw": np.ascontiguousarray(norm_w.reshape(128, 1)),
            "g_ident": np.eye(128, dtype=np.float32)}

def emit_gdn(P, gin, oT, D, T=8192, dbg=None):
    nc = P.nc
    NB = T // 512
    def vop(fn, reads, writes, eng=V):
        P.op(eng, fn, reads=reads, writes=writes)
    def ld(name, shape, src):
        t = P.sbuf("sb_" + name, shape, F32)
        P.dma("sync", t[:], src, writes=[name])
        return t
    cw = ld("g_cw", [128, 3, 2, 4], D["g_cw"])
    alog = ld("g_alog", [2, 1], D["g_alog"])
    dtb = ld("g_dtb", [2, 1], D["g_dtb"])
    nw = ld("g_nw", [128, 1], D["g_nw"])
    ident = ld("g_ident", [128, 128], D["g_ident"])
    ones_bf = P.sbuf("g_ones", [128, 128], BF16)
    vop(lambda e: e.memset(ones_bf[:], 1.0), [], ["g_ones"])
    epsc = P.sbuf("g_eps", [128, 1], F32)
    vop(lambda e: e.memset(epsc[:], 1e-6), [], ["g_eps"])
    onec = P.sbuf("g_one", [128, 1], F32)
    vop(lambda e: e.memset(onec[:], 1.0), [], ["g_one"])
    RC = min(T, 2048)
    arow = P.sbuf("g_arow", [2, RC], F32); brow = P.sbuf("g_brow", [2, RC], F32); tr = P.sbuf("g_tr", [2, RC], F32)
    nega = P.sbuf("g_nega", [2, 1], F32)
    vop(lambda e: e.activation(nega[:], alog[:], AF.Exp), ["g_alog"], ["g_nega"], eng=S)
    vop(lambda e: e.tensor_scalar(nega[:], nega[:], -1.0, None, ALU.mult), ["g_nega"], ["g_nega"])
    scr = nc.dram_tensor("g_scr", [3, 2, T], F32).ap()
    for rc in range(T // RC):
        cs = slice(rc * RC, (rc + 1) * RC)
        P.dma("sync", arow[:], gin[1024:1026, cs], writes=["g_arow"])
        P.dma("sync", brow[:], gin[1026:1028, cs], writes=["g_brow"])
        vop(lambda e: e.activation(tr[:], arow[:], AF.Exp, bias=dtb[:, 0:1], scale=1.0), ["g_arow", "g_dtb"], ["g_tr"], eng=S)
        vop(lambda e: e.activation(tr[:], tr[:], AF.Ln, bias=onec[0:2, 0:1], scale=1.0), ["g_tr", "g_one"], ["g_tr"], eng=S)
        vop(lambda e: e.tensor_scalar(tr[:], tr[:], nega[:, 0:1], None, ALU.mult), ["g_tr", "g_nega"], ["g_tr"])
        vop(lambda e: e.activation(arow[:], tr[:], AF.Exp), ["g_tr"], ["g_arow"], eng=S)
        vop(lambda e: e.activation(brow[:], brow[:], AF.Sigmoid), ["g_brow"], ["g_brow"], eng=S)
        vop(lambda e: e.scalar_tensor_tensor(tr[:], arow[:], -1.0, brow[:], ALU.mult, ALU.mult), ["g_arow", "g_brow"], ["g_tr"])
        P.dma("sync", scr[0, :, cs], arow[:], reads=["g_arow"], writes=[("scr", 0, rc)])
        P.dma("sync", scr[1, :, cs], tr[:], reads=["g_tr"], writes=[("scr", 1, rc)])
        P.dma("sync", scr[2, :, cs], brow[:], reads=["g_brow"], writes=[("scr", 2, rc)])
    NRC = T // RC
    natok = P.sbuf("g_natok", [128, 2, T // 128], F32)
    for h in range(2):
        P.dma("sync", natok[:, h, :], scr[1, h].rearrange("(b p) -> p b", p=128), reads=[("scr", 1, r_) for r_ in range(NRC)], writes=[("natok", h)],
              allow_slow_non_contiguous=True)
    kcol = [P.sbuf("g_kcol%d" % h, [128, T], BF16) for h in range(2)]
    qcol = [P.sbuf("g_qcol%d" % h, [128, T], BF16) for h in range(2)]
    S32 = [[P.sbuf("g_S32_%d_%d" % (h, i), [128, 128], F32) for i in range(2)] for h in range(2)]
    Sbf = [[P.sbuf("g_Sbf_%d_%d" % (h, i), [128, 128], BF16) for i in range(2)] for h in range(2)]
    for h in range(2):
        vop(lambda e, h=h: e.memset(S32[h][0][:], 0.0), [], [("S32", h, 0)])
        vop(lambda e, h=h: e.memset(Sbf[h][0][:], 0.0), [], [("Sbf", h, 0)])
    rawp = Pool(P, "g_raw", 3, [128, 515], F32)
    accp = Pool(P, "g_acc", 3, [128, 512], F32)
    sqp = Pool(P, "g_sq", 2, [128, 512], BF16)
    rnp = Pool(P, "g_rn", 2, [128, 512], F32)
    bcp = Pool(P, "g_bc", 2, [128, 512], F32)
    A128p = [Pool(P, "g_A%d" % h, 2, [128, 512], F32) for h in range(2)]
    ktokp = [Pool(P, "g_kt%d" % h, 2, [128, 4, 128], BF16) for h in range(2)]
    bvtokp = [Pool(P, "g_bv%d" % h, 2, [128, 4, 128], F32) for h in range(2)]
    knp = Pool(P, "g_kn", 2, [128, 512], F32)
    bvp = Pool(P, "g_bvf", 2, [128, 512], F32)
    zp = [Pool(P, "g_z%d" % h, 2, [128, 512], F32) for h in range(2)]
    kmp = [Pool(P, "g_km%d" % h, 3, [128, 128], BF16) for h in range(2)]
    tmpp = [Pool(P, "g_tmp%d" % h, 2, [128, 128], BF16) for h in range(2)]
    pb = [P.psum("g_pb%d" % i, [128, 512], F32) for i in range(8)]
    ksps = [[(pb[0 + h][:, i * 128:(i + 1) * 128], ("ksps", h, i)) for i in range(2)] for h in range(2)]
    dsps = [[(pb[4 + h][:, i * 128:(i + 1) * 128], ("dsps", h, i)) for i in range(2)] for h in range(2)]
    ops_ = [[(pb[2 + h], ("ops", h, 0)) for i in range(2)] for h in range(2)]
    ssps = (pb[6], "ssps")
    trps = (pb[7], "trps")

    def conv_silu(h, s, blk):
        raw, rn_ = rawp.next(); acc, an = accp.next()
        r0 = s * 256 + h * 128
        t0 = blk * 512
        if blk == 0:
            vop(lambda e: e.memset(raw[:, 0:3], 0.0), [], [rn_])
            P.dma("sync", raw[:, 3:515], gin[r0:r0 + 128, 0:512], writes=[rn_])
        else:
            P.dma("sync", raw[:], gin[r0:r0 + 128, t0 - 3:t0 + 512], writes=[rn_])
        vop(lambda e: e.tensor_scalar(acc[:], raw[:, 0:512], cw[:, s, h, 0:1], None, ALU.mult), [rn_, "g_cw"], [an])
        for j in range(1, 4):
            vop(lambda e, j=j: e.scalar_tensor_tensor(acc[:], raw[:, j:j + 512], cw[:, s, h, j:j + 1], acc[:], ALU.mult, ALU.add), [rn_, "g_cw", an], [an])
        vop(lambda e: e.activation(acc[:], acc[:], AF.Silu), [an], [an], eng=S)
        return acc, an

    def rnorm(acc, an, scale):
        sq, sn = sqp.next(); r, rn_ = rnp.next()
        vop(lambda e: e.activation(sq[:], acc[:], AF.Square), [an], [sn], eng=S)
        P.op(T_, lambda e: e.matmul(ssps[0][:], ones_bf[:], sq[:], start=True, stop=True), reads=["g_ones", sn], writes=[ssps[1]])
        vop(lambda e: e.activation(r[:], ssps[0][:], AF.Sqrt, bias=epsc[:, 0:1], scale=1.0), [ssps[1], "g_eps"], [rn_], eng=S)
        vop(lambda e: e.reciprocal(r[:], r[:]), [rn_], [rn_])
        if scale != 1.0:
            vop(lambda e: e.tensor_scalar(r[:], r[:], scale, None, ALU.mult), [rn_], [rn_])
        return r, rn_

    blkstate = {}
    def prep(h, blk):
        t0 = blk * 512
        acc, an = conv_silu(h, 0, blk)
        r, rn_ = rnorm(acc, an, 128.0 ** -0.5)
        vop(lambda e, acc=acc, r=r: e.tensor_tensor(qcol[h][:, t0:t0 + 512], acc[:], r[:], ALU.mult), [an, rn_], [("qcol", h, blk)])
        acc, an = conv_silu(h, 1, blk)
        r, rn_ = rnorm(acc, an, 1.0)
        kn, knn = knp.next()
        vop(lambda e, acc=acc, r=r, kn=kn: e.tensor_tensor(kn[:], acc[:], r[:], ALU.mult), [an, rn_], [knn])
        vop(lambda e, kn=kn: e.copy(kcol[h][:, t0:t0 + 512], kn[:]), [knn], [("kcol", h, blk)], eng=S)
        acc, an = conv_silu(h, 2, blk)
        bc, bcn = bcp.next()
        P.dma("sync", bc[:], scr[2, h:h + 1, t0:t0 + 512].partition_broadcast(128), reads=[("scr", 2, t0 // RC)], writes=[bcn])
        bv, bvn = bvp.next()
        vop(lambda e, acc=acc, bv=bv, bc=bc: e.tensor_tensor(bv[:], acc[:], bc[:], ALU.mult), [an, bcn], [bvn], eng=G)
        A1, A1n = A128p[h].next()
        P.dma("sync", A1[:], scr[0, h:h + 1, t0:t0 + 512].partition_broadcast(128), reads=[("scr", 0, t0 // RC)], writes=[A1n])
        z, zn = zp[h].next()
        P.dma("sync", z[:], gin[768 + h * 128:768 + (h + 1) * 128, t0:t0 + 512], writes=[zn])
        vop(lambda e, z=z: e.activation(z[:], z[:], AF.Silu), [zn], [zn], eng=S)
        kt, ktn = ktokp[h].next(); bvt, bvtn = bvtokp[h].next()
        for src, srcn, dst, dstn in ((kn, knn, kt, ktn), (bv, bvn, bvt, bvtn)):
            for i in range(4):
                P.op(T_, lambda e, src=src, i=i: e.transpose(trps[0][:, i * 128:(i + 1) * 128], src[:, i * 128:(i + 1) * 128], ident[:]),
                     reads=[srcn, "g_ident"], writes=[trps[1]])
            vop(lambda e, dst=dst: e.tensor_copy(dst[:].rearrange("p a b -> p (a b)"), trps[0][:]), [trps[1]], [dstn])
        blkstate[(h, blk)] = dict(A=A1, An=A1n, kt=kt, ktn=ktn, bvt=bvt, bvtn=bvtn, z=z, zn=zn)

    cur = [0, 0]
    def token_step(h, t):
        blk, b4, p = t // 512, (t % 512) // 128, t % 128
        st = blkstate[(h, blk)]
        c = cur[h]; n = 1 - c
        km, kmn = kmp[h].next(); tm, tmn = tmpp[h].next()
        ks, ksn = ksps[h][t % 2]; ds, dsn = dsps[h][t % 2]
        ob, obn = ops_[h][blk % 2]
        b128 = t // 128
        vop(lambda e: e.tensor_scalar(km[:], st["kt"][:, b4, :], ident[:, p:p + 1], None, ALU.mult), [st["ktn"], "g_ident"], [kmn], eng=G)
        P.op(T_, lambda e: e.matmul(ks, kcol[h][:, b128 * 128:(b128 + 1) * 128], Sbf[h][c][:], start=True, stop=True),
             reads=[("kcol", h, blk), ("Sbf", h, c)], writes=[ksn])
        vop(lambda e: e.scalar_tensor_tensor(tm[:], ks, natok[:, h, b128:b128 + 1], st["bvt"][:, b4, :], ALU.mult, ALU.add),
            [ksn, ("natok", h), st["bvtn"]], [tmn])
        P.op(T_, lambda e: e.matmul(ds, km[:], tm[:], start=True, stop=True), reads=[kmn, tmn], writes=[dsn])
        tl = t % 512
        vop(lambda e: e.scalar_tensor_tensor(Sbf[h][n][:], S32[h][c][:], st["A"][:, tl:tl + 1], ds, ALU.mult, ALU.add),
            [("S32", h, c), st["An"], dsn], [("Sbf", h, n)])
        vop(lambda e: e.scalar_tensor_tensor(S32[h][n][:], S32[h][c][:], st["A"][:, tl:tl + 1], ds, ALU.mult, ALU.add),
            [("S32", h, c), st["An"], dsn], [("S32", h, n)])
        P.op(T_, lambda e: e.matmul(ob[:, tl:tl + 1], Sbf[h][n][:], qcol[h][:, t:t + 1], start=True, stop=True),
             reads=[("Sbf", h, n), ("qcol", h, blk)], writes=[obn])
        cur[h] = n

    onp = Pool(P, "g_on", 2, [128, 512], F32)
    obp = Pool(P, "g_ob", 2, [128, 512], BF16)
    def finish(h, blk):
        st = blkstate[(h, blk)]
        ob, obn = ops_[h][blk % 2]
        sq, sn = sqp.next(); r, rn_ = rnp.next(); on, onn = onp.next(); o16, o16n = obp.next()
        vop(lambda e: e.activation(sq[:], ob[:], AF.Square), [obn], [sn], eng=S)
        P.op(T_, lambda e: e.matmul(ssps[0][:], ones_bf[:], sq[:], start=True, stop=True), reads=["g_ones", sn], writes=[ssps[1]])
        vop(lambda e: e.activation(r[:], ssps[0][:], AF.Sqrt, bias=epsc[:, 0:1], scale=1.0 / 128), [ssps[1], "g_eps"], [rn_], eng=S)
        vop(lambda e: e.reciprocal(r[:], r[:]), [rn_], [rn_])
        vop(lambda e: e.scalar_tensor_tensor(on[:], ob[:], nw[:, 0:1], r[:], ALU.mult, ALU.mult), [obn, "g_nw", rn_], [onn])
        vop(lambda e: e.tensor_tensor(o16[:], on[:], st["z"][:], ALU.mult), [onn, st["zn"]], [o16n], eng=G)
        P.dma("sync", oT[h * 128:(h + 1) * 128, blk * 512:(blk + 1) * 512], o16[:], reads=[o16n])

    for h in range(2):
        prep(h, 0)
    if dbg is not None:
        st = blkstate[(0, 0)]
        P.dma("sync", dbg["d_q"], qcol[0][:, 0:512], reads=[("qcol", 0, 0)])
        P.dma("sync", dbg["d_k"], kcol[0][:, 0:512], reads=[("kcol", 0, 0)])
        P.dma("sync", dbg["d_kt"], st["kt"][:].rearrange("p a b -> p (a b)"), reads=[st["ktn"]])
        P.dma("sync", dbg["d_bvt"], st["bvt"][:].rearrange("p a b -> p (a b)"), reads=[st["bvtn"]])
        P.dma("sync", dbg["d_A"], st["A"][:], reads=[st["An"]])
        P.dma("sync", dbg["d_na"], natok[:, 0, :], reads=[("natok", 0)])
    for blk in range(NB):
        if blk + 1 < NB:
            for h in range(2):
                prep(h, blk + 1)
        for t in range(blk * 512, (blk + 1) * 512):
            for h in range(2):
                token_step(h, t)
        for h in range(2):
            finish(h, blk)

NEG = -30000.0

def gdn2_consts():
    i = np.arange(64)
    negs = np.where(i[None, :] > i[:, None], 0.0, NEG).astype(np.float32)
    nega = np.where(i[:, None] > i[None, :], 0.0, NEG).astype(np.float32)
    return {"g_negs": np.ascontiguousarray(np.tile(negs[:, None, :], (1, 8, 1))),
            "g_nega": np.ascontiguousarray(np.tile(nega[:, None, :], (1, 8, 1))),
            "g_id8": np.ascontiguousarray(np.tile(np.eye(64, dtype=np.float32)[:, None, :], (1, 8, 1)))}

def emit_gdn2(P, gin, oT, D, T=8192, dbg=None):
    nc = P.nc
    NB = T // 512
    NCH = T // 64
    def vop(fn, reads, writes, eng=V):
        P.op(eng, fn, reads=reads, writes=writes)
    def ld(name, shape, src):
        t = P.sbuf("sb_" + name, shape, F32)
        P.dma("sync", t[:], src, writes=[name])
        return t
    cw = ld("g_cw", [128, 3, 2, 4], D["g_cw"])
    alog = ld("g_alog", [2, 1], D["g_alog"])
    dtb = ld("g_dtb", [2, 1], D["g_dtb"])
    nw = ld("g_nw", [128, 1], D["g_nw"])
    ident = ld("g_ident", [128, 128], D["g_ident"])
    negs = ld("g_negs", [64, 8, 64], D["g_negs"])
    negA = ld("g_nega", [64, 8, 64], D["g_nega"])
    id8 = ld("g_id8", [64, 8, 64], D["g_id8"])
    identb = P.sbuf("g_identb", [128, 128], BF16)
    vop(lambda e: e.tensor_copy(identb[:], ident[:]), ["g_ident"], ["g_identb"])
    ones_bf = P.sbuf("g_ones", [128, 128], BF16)
    vop(lambda e: e.memset(ones_bf[:], 1.0), [], ["g_ones"])
    epsc = P.sbuf("g_eps", [128, 1], F32)
    vop(lambda e: e.memset(epsc[:], 1e-6), [], ["g_eps"])
    onec = P.sbuf("g_one", [128, 1], F32)
    vop(lambda e: e.memset(onec[:], 1.0), [], ["g_one"])
    RC = min(T, 2048)
    NRC = T // RC
    arow = P.sbuf("g_arow", [2, RC], F32); brow = P.sbuf("g_brow", [2, RC], F32); tr = P.sbuf("g_tr", [2, RC], F32)
    cmask = P.sbuf("g_cmask", [2, RC], F32)
    vop(lambda e: e.memset(cmask[:], 1.0), [], ["g_cmask"])
    vop(lambda e: e.memset(cmask[:].rearrange("p (c i) -> p c i", i=64)[:, :, 0:1], 0.0), ["g_cmask"], ["g_cmask"])
    nega_ = P.sbuf("g_negal", [2, 1], F32)
    vop(lambda e: e.activation(nega_[:], alog[:], AF.Exp), ["g_alog"], ["g_negal"], eng=S)
    vop(lambda e: e.tensor_scalar(nega_[:], nega_[:], -1.0, None, ALU.mult), ["g_negal"], ["g_negal"])
    scr = nc.dram_tensor("g_scr", [2, 2, T], F32).ap()
    for rc in range(NRC):
        cs = slice(rc * RC, (rc + 1) * RC)
        P.dma("sync", arow[:], gin[1024:1026, cs], writes=["g_arow"])
        P.dma("sync", brow[:], gin[1026:1028, cs], writes=["g_brow"])
        vop(lambda e: e.activation(tr[:], arow[:], AF.Exp, bias=dtb[:, 0:1], scale=1.0), ["g_arow", "g_dtb"], ["g_tr"], eng=S)
        vop(lambda e: e.activation(tr[:], tr[:], AF.Ln, bias=onec[0:2, 0:1], scale=1.0), ["g_tr", "g_one"], ["g_tr"], eng=S)
        vop(lambda e: e.tensor_scalar(tr[:], tr[:], nega_[:, 0:1], None, ALU.mult), ["g_tr", "g_negal"], ["g_tr"])
        vop(lambda e: e.tensor_tensor_scan(arow[:], cmask[:], tr[:], 0.0, ALU.mult, ALU.add), ["g_cmask", "g_tr", "g_arow"], ["g_arow"])
        vop(lambda e: e.activation(brow[:], brow[:], AF.Sigmoid), ["g_brow"], ["g_brow"], eng=S)
        P.dma("sync", scr[0, :, cs], arow[:], reads=["g_arow"], writes=[("scr", 0, rc)])
        P.dma("sync", scr[1, :, cs], brow[:], reads=["g_brow"], writes=[("scr", 1, rc)])
    gcol = P.sbuf("g_gcol", [64, 2, NCH], F32); bcol = P.sbuf("g_bcol", [64, 2, NCH], F32)
    for h in range(2):
        P.dma("sync", gcol[:, h, :], scr[0, h].rearrange("(c i) -> i c", i=64), reads=[("scr", 0, r_) for r_ in range(NRC)], writes=[("gcol", h)], allow_slow_non_contiguous=True)
        P.dma("sync", bcol[:, h, :], scr[1, h].rearrange("(c i) -> i c", i=64), reads=[("scr", 1, r_) for r_ in range(NRC)], writes=[("bcol", h)], allow_slow_non_contiguous=True)
    S32 = [[P.sbuf("g_S32_%d_%d" % (h, i), [128, 128], F32) for i in range(2)] for h in range(2)]
    Sbf = [[P.sbuf("g_Sbf_%d_%d" % (h, i), [128, 128], BF16) for i in range(2)] for h in range(2)]
    for h in range(2):
        vop(lambda e, h=h: e.memset(S32[h][0][:], 0.0), [], [("S32", h, 0)])
        vop(lambda e, h=h: e.memset(Sbf[h][0][:], 0.0), [], [("Sbf", h, 0)])
    def mk(name, n, shape, dt=F32):
        return Pool(P, name, n, shape, dt)
    rawp = mk("g_raw", 3, [128, 515]); accp = mk("g_acc", 4, [128, 512]); sqp = mk("g_sq", 2, [128, 512], BF16); rnp = mk("g_rn", 2, [128, 512])
    f64p = mk("g_f64", 14, [64, 8, 64])
    b64p = mk("g_b64", 4, [64, 8, 64], BF16)
    bigp = mk("g_big", 5, [128, 512])
    hbfp = mk("g_hbf", 4, [128, 512], BF16)
    tokp = mk("g_tok", 3, [64, 8, 128], BF16)
    smallp = mk("g_small", 8, [64, 8])
    vnp = [mk("g_vn%d" % h, 2, [64, 128], BF16) for h in range(2)]
    p_qg = [mk("g_pqg%d" % h, 2, [128, 512], BF16) for h in range(2)]
    p_WT = [mk("g_pWT%d" % h, 2, [128, 512], BF16) for h in range(2)]
    p_z = [mk("g_pz%d" % h, 2, [128, 512]) for h in range(2)]
    p_QKm = [mk("g_pQK%d" % h, 2, [64, 8, 64], BF16) for h in range(2)]
    p_U = [mk("g_pU%d" % h, 2, [64, 8, 128]) for h in range(2)]
    p_Kd = [mk("g_pKd%d" % h, 2, [64, 8, 128], BF16) for h in range(2)]
    p_egl = [mk("g_pegl%d" % h, 2, [128, 8]) for h in range(2)]
    onp = mk("g_on", 2, [128, 512]); obp = mk("g_ob", 2, [128, 512], BF16)
    gpp = Pool(P, "g_gp", 3, [128, 512], F32, psum=True)
    trps = P.psum("g_trps", [128, 1024], BF16)
    chps = [P.psum("g_chps%d" % h, [128, 512], F32) for h in range(2)]
    ops_ = [P.psum("g_ops%d" % h, [128, 512], F32) for h in range(2)]

    def conv_silu(h, s, blk):
        raw, rn_ = rawp.next(); acc, an = accp.next()
        r0 = s * 256 + h * 128
        t0 = blk * 512
        if blk == 0:
            vop(lambda e: e.memset(raw[:, 0:3], 0.0), [], [rn_])
            P.dma("sync", raw[:, 3:515], gin[r0:r0 + 128, 0:512], writes=[rn_])
        else:
            P.dma("sync", raw[:], gin[r0:r0 + 128, t0 - 3:t0 + 512], writes=[rn_])
        vop(lambda e: e.tensor_scalar(acc[:], raw[:, 0:512], cw[:, s, h, 0:1], None, ALU.mult), [rn_, "g_cw"], [an])
        for j in range(1, 4):
            vop(lambda e, j=j: e.scalar_tensor_tensor(acc[:], raw[:, j:j + 512], cw[:, s, h, j:j + 1], acc[:], ALU.mult, ALU.add), [rn_, "g_cw", an], [an])
        vop(lambda e: e.activation(acc[:], acc[:], AF.Silu), [an], [an], eng=S)
        return acc, an

    def rnorm(acc, an, scale):
        sq, sn = sqp.next(); r, rn_ = rnp.next(); ps, pn = gpp.next()
        vop(lambda e: e.activation(sq[:], acc[:], AF.Square), [an], [sn], eng=S)
        P.op(T_, lambda e: e.matmul(ps[:], ones_bf[:], sq[:], start=True, stop=True), reads=["g_ones", sn], writes=[pn])
        vop(lambda e: e.activation(r[:], ps[:], AF.Sqrt, bias=epsc[:, 0:1], scale=1.0), [pn, "g_eps"], [rn_], eng=S)
        vop(lambda e: e.reciprocal(r[:], r[:]), [rn_], [rn_])
        if scale != 1.0:
            vop(lambda e: e.tensor_scalar(r[:], r[:], scale, None, ALU.mult), [rn_], [rn_])
        return r, rn_

    blkstate = {}
    def prep(h, blk):
        t0 = blk * 512
        c0 = blk * 8
        st = {}
        Gr, Grn = bigp.next()
        P.dma("sync", Gr[:], scr[0, h:h + 1, t0:t0 + 512].partition_broadcast(128), reads=[("scr", 0, t0 // RC)], writes=[Grn])
        Br, Brn = bigp.next()
        P.dma("sync", Br[0:64, :], scr[1, h:h + 1, t0:t0 + 512].partition_broadcast(64), reads=[("scr", 1, t0 // RC)], writes=[Brn])
        Gr3 = Gr[0:64, :].rearrange("p (c i) -> p c i", i=64)
        Br3 = Br[0:64, :].rearrange("p (c i) -> p c i", i=64)
        gcb = gcol[:, h, c0:c0 + 8]; bcb = bcol[:, h, c0:c0 + 8]
        gct, bct = ("gcol", h), ("bcol", h)
        eG, eGn = bigp.next()
        vop(lambda e: e.activation(eG[:], Gr[:], AF.Exp), [Grn], [eGn], eng=S)
        egl, egln = p_egl[h].next()
        vop(lambda e: e.tensor_copy(egl[:], eG[:].rearrange("p (c i) -> p c i", i=64)[:, :, 63]), [eGn], [egln])
        bg, bgn = smallp.next(); ed, edn = smallp.next()
        vop(lambda e: e.activation(bg[:], gcb, AF.Exp), [gct], [bgn], eng=S)
        vop(lambda e: e.tensor_tensor(bg[:], bg[:], bcb, ALU.mult), [bgn, bct], [bgn])
        vop(lambda e: e.tensor_tensor(ed[:], Gr3[:, :, 63], gcb, ALU.subtract), [Grn, gct], [edn])
        vop(lambda e: e.activation(ed[:], ed[:], AF.Exp), [edn], [edn], eng=S)
        acc, an = conv_silu(h, 0, blk)
        r, rn_ = rnorm(acc, an, 128.0 ** -0.5)
        qn, qnn = bigp.next()
        vop(lambda e, acc=acc, r=r: e.tensor_tensor(qn[:], acc[:], r[:], ALU.mult), [an, rn_], [qnn])
        qb, qbn = hbfp.next(); qg, qgn = p_qg[h].next()
        vop(lambda e: e.copy(qb[:], qn[:]), [qnn], [qbn], eng=S)
        vop(lambda e: e.tensor_tensor(qg[:], qn[:], eG[:], ALU.mult), [qnn, eGn], [qgn], eng=G)
        acc, an = conv_silu(h, 1, blk)
        r, rn_ = rnorm(acc, an, 1.0)
        kb, kbn = hbfp.next()
        vop(lambda e, acc=acc, r=r: e.tensor_tensor(kb[:], acc[:], r[:], ALU.mult), [an, rn_], [kbn])
        acc, an = conv_silu(h, 2, blk)
        vb, vbn = hbfp.next()
        vop(lambda e, acc=acc: e.copy(vb[:], acc[:]), [an], [vbn], eng=S)
        z, zn = p_z[h].next()
        P.dma("sync", z[:], gin[768 + h * 128:768 + (h + 1) * 128, t0:t0 + 512], writes=[zn])
        vop(lambda e: e.activation(z[:], z[:], AF.Silu), [zn], [zn], eng=S)
        KK, KKn = gpp.next(); QK, QKn = gpp.next()
        for c in range(8):
            cs = slice(c * 64, (c + 1) * 64)
            P.op(T_, lambda e, cs=cs: e.matmul(KK[0:64, cs], kb[:, cs], kb[:, cs], start=True, stop=True), reads=[kbn], writes=[KKn])
            P.op(T_, lambda e, cs=cs: e.matmul(QK[0:64, cs], kb[:, cs], qb[:, cs], start=True, stop=True), reads=[kbn, qbn], writes=[QKn])
        KK3 = KK[0:64, :].rearrange("p (c i) -> p c i", i=64)
        QK3 = QK[0:64, :].rearrange("p (c i) -> p c i", i=64)
        gcb_b = gcb.unsqueeze(2).to_broadcast([64, 8, 64])
        bcb_b = bcb.unsqueeze(2).to_broadcast([64, 8, 64])
        ES, ESn = f64p.next(); EA, EAn = f64p.next()
        vop(lambda e: e.tensor_tensor(ES[:], Gr3, negs[:], ALU.add), [Grn, "g_negs"], [ESn], eng=G)
        vop(lambda e: e.tensor_tensor(ES[:], ES[:], gcb_b, ALU.subtract), [ESn, gct], [ESn])
        vop(lambda e: e.activation(ES[:], ES[:], AF.Exp), [ESn], [ESn], eng=S)
        vop(lambda e: e.scalar_tensor_tensor(EA[:], Gr3, -1.0, negA[:], ALU.mult, ALU.add), [Grn, "g_nega"], [EAn])
        vop(lambda e: e.tensor_tensor(EA[:], EA[:], gcb_b, ALU.add), [EAn, gct], [EAn])
        vop(lambda e: e.activation(EA[:], EA[:], AF.Exp), [EAn], [EAn], eng=S)
        Pk, Pkn = f64p.next(); PTk, PTkn = f64p.next()
        vop(lambda e: e.scalar_tensor_tensor(Pk[:], KK3, -1.0, EA[:], ALU.mult, ALU.mult), [KKn, EAn], [Pkn])
        vop(lambda e: e.tensor_tensor(Pk[:], Pk[:], bcb_b, ALU.mult), [Pkn, bct], [Pkn], eng=G)
        vop(lambda e: e.scalar_tensor_tensor(PTk[:], KK3, -1.0, ES[:], ALU.mult, ALU.mult), [KKn, ESn], [PTkn])
        vop(lambda e: e.tensor_tensor(PTk[:], PTk[:], Br3, ALU.mult), [PTkn, Brn], [PTkn], eng=G)
        if dbg is not None and h == 0 and blk == 0:
            P.dma("sync", dbg["d_M"], Pk[:].rearrange("p c i -> p (c i)"), reads=[Pkn])
            P.dma("sync", dbg["d_ES"], ES[:].rearrange("p c i -> p (c i)"), reads=[ESn])
            P.dma("sync", dbg["d_EA"], EA[:].rearrange("p c i -> p (c i)"), reads=[EAn])
            P.dma("sync", dbg["d_MT"], PTk[:].rearrange("p c i -> p (c i)"), reads=[PTkn])
        QKm, QKmn = p_QKm[h].next()
        vop(lambda e: e.tensor_tensor(ES[:], ES[:], id8[:], ALU.add), [ESn, "g_id8"], [ESn], eng=G)
        vop(lambda e: e.tensor_tensor(QKm[:], QK3, ES[:], ALU.mult), [QKn, ESn], [QKmn])
        TT, TTn = f64p.next()
        vop(lambda e: e.tensor_tensor(TT[:], PTk[:], id8[:], ALU.add), [PTkn, "g_id8"], [TTn], eng=G)
        Pc, Pcn, PTc, PTcn, TTc, TTcn = Pk, Pkn, PTk, PTkn, TT, TTn
        for k in range(1, 6):
            Pp, Ppn = gpp.next()
            for c in range(8):
                P.op(T_, lambda e, c=c, Pp=Pp, PTc=PTc, Pc=Pc: e.matmul(Pp[0:64, c * 64:(c + 1) * 64], PTc[:, c, :], Pc[:, c, :], start=True, stop=True), reads=[PTcn, Pcn], writes=[Ppn])
            if k < 5:
                PTp, PTpn = gpp.next()
                for c in range(8):
                    P.op(T_, lambda e, c=c, PTp=PTp, PTc=PTc, Pc=Pc: e.matmul(PTp[0:64, c * 64:(c + 1) * 64], Pc[:, c, :], PTc[:, c, :], start=True, stop=True), reads=[PTcn, Pcn], writes=[PTpn])
            Pn_, Pnn = f64p.next()
            vop(lambda e, Pn_=Pn_, Pp=Pp: e.copy(Pn_[:].rearrange("p c i -> p (c i)"), Pp[0:64, :]), [Ppn], [Pnn], eng=S)
            if k < 5:
                PTn_, PTnn = f64p.next()
                vop(lambda e, PTn_=PTn_, PTp=PTp: e.tensor_copy(PTn_[:].rearrange("p c i -> p (c i)"), PTp[0:64, :]), [PTpn], [PTnn])
            Tu, Tun = gpp.next()
            for c in range(8):
                P.op(T_, lambda e, c=c, Tu=Tu, Pn_=Pn_, TTc=TTc: e.matmul(Tu[0:64, c * 64:(c + 1) * 64], Pn_[:, c, :], TTc[:, c, :], start=True, stop=True), reads=[Pnn, TTcn], writes=[Tun])
            TT2, TT2n = f64p.next()
            vop(lambda e, TT2=TT2, TTc=TTc, Tu=Tu: e.tensor_tensor(TT2[:].rearrange("p c i -> p (c i)"), TTc[:].rearrange("p c i -> p (c i)"), Tu[0:64, :], ALU.add), [TTcn, Tun], [TT2n])
            TTc, TTcn = TT2, TT2n
            Pc, Pcn = Pn_, Pnn
            if k < 5:
                PTc, PTcn = PTn_, PTnn
            if dbg is not None and h == 0 and blk == 0 and k == 1:
                P.dma("sync", dbg["d_P1"], Pc[:].rearrange("p c i -> p (c i)"), reads=[Pcn])
                P.dma("sync", dbg["d_T1"], TTc[:].rearrange("p c i -> p (c i)"), reads=[TTcn])
        TTb, TTbn = b64p.next()
        vop(lambda e, TTc=TTc: e.copy(TTb[:], TTc[:]), [TTcn], [TTbn], eng=S)
        Kw, Kwn = tokp.next(); Kd, Kdn = p_Kd[h].next(); Vb, Vbn = tokp.next()
        for c in range(8):
            P.op(T_, lambda e, c=c: e.transpose(trps[0:64, c * 128:(c + 1) * 128], kb[:, c * 64:(c + 1) * 64], identb[:]), reads=[kbn, "g_identb"], writes=["g_trps"])
        ktv = trps[0:64, :].rearrange("p (c d) -> p c d", d=128)
        vop(lambda e: e.tensor_tensor(Kw[:], ktv, bg[:].unsqueeze(2).to_broadcast([64, 8, 128]), ALU.mult), ["g_trps", bgn], [Kwn])
        vop(lambda e: e.tensor_tensor(Kd[:], ktv, ed[:].unsqueeze(2).to_broadcast([64, 8, 128]), ALU.mult), ["g_trps", edn], [Kdn])
        for c in range(8):
            P.op(T_, lambda e, c=c: e.transpose(trps[0:64, c * 128:(c + 1) * 128], vb[:, c * 64:(c + 1) * 64], identb[:]), reads=[vbn, "g_identb"], writes=["g_trps"])
        vop(lambda e: e.tensor_tensor(Vb[:], ktv, bcb.unsqueeze(2).to_broadcast([64, 8, 128]), ALU.mult), ["g_trps", bct], [Vbn])
        U, Un = p_U[h].next()
        for half in range(2):
            Ups, Upsn = gpp.next()
            for cc in range(4):
                c = half * 4 + cc
                P.op(T_, lambda e, c=c, cc=cc, Ups=Ups: e.matmul(Ups[0:64, cc * 128:(cc + 1) * 128], TTb[:, c, :], Vb[:, c, :], start=True, stop=True), reads=[TTbn, Vbn], writes=[Upsn])
            vop(lambda e, half=half, Ups=Ups: e.copy(U[:, half * 4:half * 4 + 4, :].rearrange("p c d -> p (c d)"), Ups[0:64, :]), [Upsn], [Un], eng=S)
        Wps, Wpsn = gpp.next()
        for c in range(8):
            P.op(T_, lambda e, c=c: e.matmul(Wps[:, c * 64:(c + 1) * 64], Kw[:, c, :], TTb[:, c, :], start=True, stop=True), reads=[Kwn, TTbn], writes=[Wpsn])
        WT, WTn = p_WT[h].next()
        vop(lambda e: e.tensor_copy(WT[:], Wps[:]), [Wpsn], [WTn])
        if dbg is not None and h == 0 and blk == 0:
            P.dma("sync", dbg["d_TT"], TTc[:].rearrange("p c i -> p (c i)"), reads=[TTcn])
            P.dma("sync", dbg["d_U"], U[:].rearrange("p c i -> p (c i)"), reads=[Un])
            P.dma("sync", dbg["d_Gr"], Gr[:], reads=[Grn])
            P.dma("sync", dbg["d_gcol"], gcol[:, 0, :], reads=[("gcol", 0)])
        st.update(dict(egl=egl, egln=egln, qg=qg, qgn=qgn, QKm=QKm, QKmn=QKmn, U=U, Un=Un, WT=WT, WTn=WTn, Kd=Kd, Kdn=Kdn, z=z, zn=zn))
        blkstate[(h, blk)] = st

    cur = [0, 0]
    def chunk_step(h, blk, c):
        st = blkstate[(h, blk)]
        ci = cur[h]; n = 1 - ci
        cs = slice(c * 64, (c + 1) * 64)
        ws = chps[h][0:64, 0:128]; wsn = ("chps", h, 0)
        ds = chps[h][:, 128:256]; dsn = ("chps", h, 1)
        vn, vnn = vnp[h].next()
        ob = ops_[h]; obn = ("ops", h)
        P.op(T_, lambda e: e.matmul(ws, st["WT"][:, cs], Sbf[h][ci][:], start=True, stop=True), reads=[st["WTn"], ("Sbf", h, ci)], writes=[wsn])
        vop(lambda e: e.tensor_tensor(vn[:], st["U"][:, c, :], ws, ALU.subtract), [st["Un"], wsn], [vnn])
        P.op(T_, lambda e: e.matmul(ob[:, cs], Sbf[h][ci][:], st["qg"][:, cs], start=True, stop=False), reads=[("Sbf", h, ci), st["qgn"]], writes=[obn])
        P.op(T_, lambda e: e.matmul(ob[:, cs], vn[:], st["QKm"][:, c, :], start=False, stop=True), reads=[vnn, st["QKmn"]], writes=[obn])
        P.op(T_, lambda e: e.matmul(ds, st["Kd"][:, c, :], vn[:], start=True, stop=True), reads=[st["Kdn"], vnn], writes=[dsn])
        vop(lambda e: e.scalar_tensor_tensor(Sbf[h][n][:], S32[h][ci][:], st["egl"][:, c:c + 1], ds, ALU.mult, ALU.add),
            [("S32", h, ci), st["egln"], dsn], [("Sbf", h, n)])
        vop(lambda e: e.scalar_tensor_tensor(S32[h][n][:], S32[h][ci][:], st["egl"][:, c:c + 1], ds, ALU.mult, ALU.add),
            [("S32", h, ci), st["egln"], dsn], [("S32", h, n)])
        cur[h] = n

    def finish(h, blk):
        st = blkstate[(h, blk)]
        ob = ops_[h]; obn = ("ops", h)
        sq, sn = sqp.next(); r, rn_ = rnp.next(); on, onn = onp.next(); o16, o16n = obp.next(); ps, pn = gpp.next()
        vop(lambda e: e.activation(sq[:], ob[:], AF.Square), [obn], [sn], eng=S)
        P.op(T_, lambda e: e.matmul(ps[:], ones_bf[:], sq[:], start=True, stop=True), reads=["g_ones", sn], writes=[pn])
        vop(lambda e: e.activation(r[:], ps[:], AF.Sqrt, bias=epsc[:, 0:1], scale=1.0 / 128), [pn, "g_eps"], [rn_], eng=S)
        vop(lambda e: e.reciprocal(r[:], r[:]), [rn_], [rn_])
        vop(lambda e: e.scalar_tensor_tensor(on[:], ob[:], nw[:, 0:1], r[:], ALU.mult, ALU.mult), [obn, "g_nw", rn_], [onn])
        vop(lambda e: e.tensor_tensor(o16[:], on[:], st["z"][:], ALU.mult), [onn, st["zn"]], [o16n], eng=G)
        P.dma("sync", oT[h * 128:(h + 1) * 128, blk * 512:(blk + 1) * 512], o16[:], reads=[o16n])

    for h in range(2):
        prep(h, 0)
    for blk in range(NB):
        if blk + 1 < NB:
            for h in range(2):
                prep(h, blk + 1)
        for c in range(8):
            for h in range(2):
                chunk_step(h, blk, c)
        for h in range(2):
            finish(h, blk)

EPS = 1e-6

def vop(P, fn, reads, writes, eng=V):
    P.op(eng, fn, reads=reads, writes=writes)

class Norm:
    def __init__(self, P, nfeat=4096, nxc=3, nsq=2, nr=2):
        self.P = P
        self.ones = P.sbuf("n_ones", [128, 128], BF16)
        vop(P, lambda e: e.memset(self.ones[:], 1.0), [], ["n_ones"])
        self.epsc = P.sbuf("n_eps", [128, 1], F32)
        vop(P, lambda e: e.memset(self.epsc[:], EPS), [], ["n_eps"])
        self.xc = Pool(P, "n_xc", nxc, [128, 4, 512], F32)
        self.xc2 = Pool(P, "n_xc2", 2, [128, 4, 512], F32)
        self.sq = Pool(P, "n_sq", nsq, [128, 4, 512], BF16)
        self.ps = Pool(P, "n_ps", 1, [128, 512], F32, psum=True)
        self.r = Pool(P, "n_r", nr, [128, 512], F32)
        self.nfeat = nfeat

    def rstd(self, src, tb, KT, src_tok=None):
        P = self.P
        ps, pn = self.ps.next()
        sv = src.rearrange("(kt p) t -> p kt t", p=128)
        for c in range(KT // 4):
            xc, xn = self.xc.next(); sq, sn = self.sq.next()
            P.dma("sync", xc[:], sv[:, c * 4:c * 4 + 4, tb * 512:(tb + 1) * 512], reads=([(src_tok, c * 4 + k_, tb) for k_ in range(4)] if src_tok else []), writes=[xn])
            vop(P, lambda e, xc=xc, sq=sq: e.activation(sq[:], xc[:], AF.Square), [xn], [sn], eng=S)
            for k in range(4):
                P.op(T_, lambda e, ps=ps, sq=sq, k=k, c=c: e.matmul(ps[:], self.ones[:], sq[:, k, :], start=(c == 0 and k == 0), stop=(c == KT // 4 - 1 and k == 3)),
                     reads=["n_ones", sn], writes=[pn])
        return self.finish(ps, pn, KT * 128)

    def finish(self, ps, pn, n):
        P = self.P
        r, rn = self.r.next()
        vop(P, lambda e: e.activation(r[:], ps[:], AF.Sqrt, bias=self.epsc[:, 0:1], scale=1.0 / n), [pn, "n_eps"], [rn], eng=S)
        vop(P, lambda e: e.reciprocal(r[:], r[:]), [rn], [rn])
        return r, rn

    def to_bf16(self, src, nw, nwn, dst, dstn, T):
        P = self.P
        sv = src.rearrange("(kt p) t -> p kt t", p=128)
        for tb in range(T // 512):
            r, rn = self.rstd(src, tb, 32)
            for c in range(8):
                xc, xn = self.xc.next()
                P.dma("sync", xc[:], sv[:, c * 4:c * 4 + 4, tb * 512:(tb + 1) * 512], writes=[xn])
                for k in range(4):
                    kt = c * 4 + k
                    vop(P, lambda e, xc=xc, k=k, kt=kt, r=r, tb=tb: e.scalar_tensor_tensor(dst[:, kt, tb * 512:(tb + 1) * 512], xc[:, k, :], nw[:, kt:kt + 1], r[:], ALU.mult, ALU.mult),
                        [xn, nwn, rn], [dstn])

    def resid(self, raw, x, nw, nwn, out, T, raw_tok=None):
        P = self.P
        rv = raw.rearrange("(kt p) t -> p kt t", p=128)
        xv = x.rearrange("(kt p) t -> p kt t", p=128)
        ov = out.rearrange("(kt p) t -> p kt t", p=128)
        for tb in range(T // 512):
            r, rn = self.rstd(raw, tb, 32, src_tok=raw_tok)
            for c in range(8):
                xc, xn = self.xc.next(); x2, x2n = self.xc2.next()
                P.dma("sync", xc[:], rv[:, c * 4:c * 4 + 4, tb * 512:(tb + 1) * 512], reads=([(raw_tok, c * 4 + k_, tb) for k_ in range(4)] if raw_tok else []), writes=[xn])
                P.dma("sync", x2[:], xv[:, c * 4:c * 4 + 4, tb * 512:(tb + 1) * 512], writes=[x2n])
                for k in range(4):
                    kt = c * 4 + k
                    vop(P, lambda e, xc=xc, k=k, kt=kt, r=r: e.scalar_tensor_tensor(xc[:, k, :], xc[:, k, :], nw[:, kt:kt + 1], r[:], ALU.mult, ALU.mult), [xn, nwn, rn], [xn])
                vop(P, lambda e, xc=xc, x2=x2: e.tensor_tensor(x2[:], x2[:], xc[:], ALU.add), [xn, x2n], [x2n], eng=G)
                P.dma("sync", ov[:, c * 4:c * 4 + 4, tb * 512:(tb + 1) * 512], x2[:], reads=[x2n])

def ld_const(P, name, shape, src, dt=F32, eng="sync"):
    t = P.sbuf("sb_" + name, shape, dt)
    P.dma(eng, t[:], src, writes=[name])
    return t

NPROJ = 18560
def build_A():
    P = Prog()
    xT = P.dram("xT", [4096, 1024], F32, "ExternalInput")
    nwd = P.dram("nw", [128, 32], F32, "ExternalInput")
    w = P.dram("w", [4096, NPROJ], F32, "ExternalInput")
    out = P.dram("projT", [NPROJ, 1024], F32, "ExternalOutput")
    nw = ld_const(P, "nw", [128, 32], nwd)
    N = Norm(P)
    h = P.sbuf("hT", [128, 32, 1024], BF16)
    N.to_bf16(xT, nw, "nw", h, "hT", 1024)
    Gm = Gemm(P)
    Gm.run(h, "hT", 32, w, NPROJ, Gm.evac_to_dram(out))
    return P.build()

def build_C1a():
    P = Prog()
    yT = P.dram("yT", [2048, 1024], BF16, "ExternalInput")
    oT = P.dram("oT", [2048, 1024], BF16, "ExternalInput")
    gs = P.dram("gsT", [4096, 1024], F32, "ExternalInput")
    gd = P.dram("gdT", [4096, 1024], F32, "ExternalInput")
    wglu = P.dram("w_glu", [2048, 8192], F32, "ExternalInput")
    wgdn = P.dram("w_gdn", [2048, 4096], F32, "ExternalInput")
    mT = P.dram("mT", [4096, 1024], BF16, "ExternalOutput")
    y = P.sbuf("y_sb", [128, 16, 1024], BF16)
    o = P.sbuf("o_sb", [128, 16, 1024], BF16)
    P.dma("sync", y[:], yT.rearrange("(kt p) t -> p kt t", p=128), writes=["y_sb"])
    P.dma("sync", o[:], oT.rearrange("(kt p) t -> p kt t", p=128), writes=["o_sb"])
    Gm = Gemm(P, wbytes=8192, npsum=7)
    gsp = Pool(P, "c_gs", 2, [128, 512], F32); gdp = Pool(P, "c_gd", 2, [128, 512], F32)
    t1p = Pool(P, "c_t1", 2, [128, 512], F32); t2p = Pool(P, "c_t2", 2, [128, 512], F32)
    sbp = Pool(P, "c_sb", 2, [128, 512], F32); mp = Pool(P, "c_m", 3, [128, 512], BF16)
    for nt in range(32):
        stash = {}
        def ev_ab(i, tb, ps, ptok, stash=stash):
            stash[(i, tb)] = (ps, ptok)
        Gm.run(y, "y_sb", 16, wglu, 256, ev_ab, col_groups=[[(nt * 128, 128), (4096 + nt * 128, 128)]])
        def ev_g(i, tb, G_, Gn, stash=stash, nt=nt):
            A_, An = stash[(0, tb)]; B_, Bn = stash[(1, tb)]
            gst, gsn = gsp.next(); gdt, gdn = gdp.next(); t1, t1n = t1p.next(); t2, t2n = t2p.next(); sb, sbn = sbp.next(); m, mn = mp.next()
            rs = slice(nt * 128, (nt + 1) * 128); cs = slice(tb * 512, (tb + 1) * 512)
            P.dma("sync", gst[:], gs[rs, cs], writes=[gsn])
            P.dma("sync", gdt[:], gd[rs, cs], writes=[gdn])
            vop(P, lambda e: e.activation(sb[:], B_[:], AF.Sigmoid), [Bn], [sbn], eng=S)
            vop(P, lambda e: e.tensor_tensor(t1[:], A_[:], sb[:], ALU.mult), [An, sbn], [t1n])
            vop(P, lambda e: e.activation(gst[:], gst[:], AF.Sigmoid), [gsn], [gsn], eng=S)
            vop(P, lambda e: e.activation(gdt[:], gdt[:], AF.Sigmoid), [gdn], [gdn], eng=S)
            vop(P, lambda e: e.tensor_tensor(t1[:], t1[:], gst[:], ALU.mult), [t1n, gsn], [t1n])
            vop(P, lambda e: e.tensor_tensor(t2[:], G_[:], gdt[:], ALU.mult), [Gn, gdn], [t2n])
            vop(P, lambda e: e.tensor_tensor(m[:], t1[:], t2[:], ALU.add), [t1n, t2n], [mn])
            P.dma("sync", mT[rs, cs], m[:], reads=[mn])
        Gm.run(o, "o_sb", 16, wgdn, 128, ev_g, col_groups=[[(nt * 128, 128)]])
    return P.build()

def build_C1b():
    P = Prog()
    mT = P.dram("mT", [4096, 1024], BF16, "ExternalInput")
    xT = P.dram("xT", [4096, 1024], F32, "ExternalInput")
    w = P.dram("w_out", [4096, 4096], F32, "ExternalInput")
    nwd = P.dram("nw", [128, 32], F32, "ExternalInput")
    x1 = P.dram("x1T", [4096, 1024], F32, "ExternalOutput")
    raw = P.nc.dram_tensor("c_raw", [4096, 1024], F32).ap()
    nw = ld_const(P, "nw", [128, 32], nwd)
    m = P.sbuf("m_sb", [128, 32, 1024], BF16)
    P.dma("sync", m[:], mT.rearrange("(kt p) t -> p kt t", p=128), writes=["m_sb"])
    Gm = Gemm(P)
    Gm.run(m, "m_sb", 32, w, 4096, Gm.evac_to_dram(raw, tok="c_raw"))
    N = Norm(P)
    N.resid(raw, xT, nw, "nw", x1, 1024, raw_tok="c_raw")
    return P.build()

def build_C2(FF=16384):
    P = Prog()
    nq = FF // 4096
    x1 = P.dram("x1T", [4096, 1024], F32, "ExternalInput")
    w1 = P.dram("w1", [4096, FF], F32, "ExternalInput")
    w2 = P.dram("w2", [FF, 4096], F32, "ExternalInput")
    nw1d = P.dram("nw1", [128, 32], F32, "ExternalInput")
    nw2d = P.dram("nw2", [128, 32], F32, "ExternalInput")
    x2 = P.dram("x2T", [4096, 1024], F32, "ExternalOutput")
    parts = [P.nc.dram_tensor("f_part%d" % q, [4096, 1024], F32).ap() for q in range(nq)]
    raw2 = P.nc.dram_tensor("f_raw2", [4096, 1024], F32).ap()
    nw1 = ld_const(P, "nw1", [128, 32], nw1d)
    nw2 = ld_const(P, "nw2", [128, 32], nw2d)
    N = Norm(P, nxc=2, nsq=1, nr=1)
    h = P.sbuf("hT", [128, 32, 1024], BF16)
    hq = P.sbuf("hqT", [128, 32, 1024], BF16)
    N.to_bf16(x1, nw1, "nw1", h, "hT", 1024)
    Gm = Gemm(P, nwb=2, nstage=2)
    relp = Pool(P, "f_rel", 2, [128, 512], F32)
    for q in range(nq):
        def ev1(nt, tb, ps, ptok):
            r, rn = relp.next()
            vop(P, lambda e: e.activation(r[:], ps[:], AF.Relu), [ptok], [rn], eng=S)
            vop(P, lambda e: e.tensor_tensor(hq[:, nt, tb * 512:(tb + 1) * 512], r[:], r[:], ALU.mult), [rn], [("hq", nt, tb)])
        Gm.run(h, "hT", 32, w1[:, q * 4096:(q + 1) * 4096], 4096, ev1)
        Gm.run(hq, (lambda kt, tb: ("hq", kt, tb)), 32, w2[q * 4096:(q + 1) * 4096, :], 4096, Gm.evac_to_dram(parts[q], tok=("part", q)))
    for tb in range(2):
        for c in range(8):
            acc, accn = N.xc2.next()
            sl = (slice(None), slice(c * 4, c * 4 + 4), slice(tb * 512, (tb + 1) * 512))
            P.dma("sync", acc[:], parts[0].rearrange("(kt p) t -> p kt t", p=128)[sl], reads=[(("part", 0), c * 4 + k_, tb) for k_ in range(4)], writes=[accn])
            for q in range(1, nq):
                xc, xn = N.xc.next()
                P.dma("sync", xc[:], parts[q].rearrange("(kt p) t -> p kt t", p=128)[sl], reads=[(("part", q), c * 4 + k_, tb) for k_ in range(4)], writes=[xn])
                vop(P, lambda e, acc=acc, xc=xc: e.tensor_tensor(acc[:], acc[:], xc[:], ALU.add), [accn, xn], [accn])
            P.dma("sync", raw2.rearrange("(kt p) t -> p kt t", p=128)[sl], acc[:], reads=[accn], writes=[("f_raw2", c * 4 + k_, tb) for k_ in range(4)])
    N.resid(raw2, x1, nw2, "nw2", x2, 1024, raw_tok="f_raw2")
    return P.build()


def build_BS():
    P = Prog()
    uT = P.dram("uT", [256, 8192], F32, "ExternalInput")
    yT = P.dram("yT", [256, 8192], BF16, "ExternalOutput")
    z16 = np.zeros((128, 64), np.float32)
    shapes = {"ssmA": [128, 5, 2, 64], "ssmB": [128, 3, 16], "ssmcc": [128, 2, 16, 16], "ssmd": [128, 2]}
    shapes.update({k: list(v.shape) for k, v in ssm_consts().items()})
    D = {k: P.dram(k, s, F32, "ExternalInput") for k, s in shapes.items()}
    emit_ssm(P, uT, yT, D, NCH=16)
    return P.build()


def build_BG():
    P = Prog()
    gin = P.dram("gin", [1028, 8192], F32, "ExternalInput")
    oT = P.dram("oT", [256, 8192], BF16, "ExternalOutput")
    shapes = {"g_cw": [128, 3, 2, 4], "g_alog": [2, 1], "g_dtb": [2, 1], "g_nw": [128, 1], "g_ident": [128, 128]}
    shapes.update({k: list(v.shape) for k, v in gdn2_consts().items()})
    D = {k: P.dram(k, s, F32, "ExternalInput") for k, s in shapes.items()}
    emit_gdn2(P, gin, oT, D, T=8192)
    return P.build()


def pack_w_in(w_in):
    ab = np.zeros((4096, 128), np.float32)
    for j in range(8):
        ab[:, 4 * j + 0] = w_in[:, 8192 + 2 * j]
        ab[:, 4 * j + 1] = w_in[:, 8192 + 2 * j + 1]
        ab[:, 4 * j + 2] = w_in[:, 8208 + 2 * j]
        ab[:, 4 * j + 3] = w_in[:, 8208 + 2 * j + 1]
    return np.ascontiguousarray(np.concatenate([w_in[:, 0:8192], w_in[:, 8224:], ab], axis=1))


_PROGS = {}


def _prog(name, builder):
    if name not in _PROGS:
        _PROGS[name] = builder()
    return _PROGS[name]


def _run(name, builder, ins):
    nc = _prog(name, builder)
    return run_bass_kernel_spmd(nc, ins, core_ids=list(range(8))).results


def _col(v):
    return np.ascontiguousarray(np.asarray(v, np.float32).reshape(32, 128).T)


def kernel(**inp):
    inp = {k: np.asarray(v) for k, v in inp.items()}
    C = np.ascontiguousarray
    x = inp["x"][0]
    xs = [C(x[c * 1024:(c + 1) * 1024].T) for c in range(8)]
    consts = ssm_consts()
    gconsts = gdn2_consts()
    for l in range(2):
        wp = pack_w_in(inp["w_in"][l])
        nw = _col(inp["mix_pre_w"][l])
        rA = _run("A", build_A, [{"xT": xs[c], "nw": nw, "w": wp} for c in range(8)])
        del wp
        projT = np.concatenate([r["projT"] for r in rA], axis=1)
        gates = [(C(r["projT"][10240:14336]), C(r["projT"][14336:18432])) for r in rA]
        del rA
        ins = []
        for j in range(8):
            d = ssm_host_inputs(j, inp["ssm_a_re"][l], inp["ssm_a_im"][l], inp["ssm_log_dt"][l], inp["ssm_b_re"][l],
                                inp["ssm_b_im"][l], inp["ssm_c_re"][l], inp["ssm_c_im"][l], inp["ssm_d"][l])
            d.update(consts)
            d["uT"] = C(projT[256 * j:256 * (j + 1)])
            ins.append(d)
        rS = _run("BS", build_BS, ins)
        yfull = np.concatenate([r["yT"] for r in rS], axis=0)
        ins = []
        for j in range(8):
            d = gdn_host_params(j, inp["conv_w"][l], inp["gdn_a_log"][l], inp["gdn_dt_bias"][l], inp["gdn_norm_w"][l])
            d.update(gconsts)
            d["gin"] = C(np.concatenate([projT[2048 + 256 * j:2048 + 256 * (j + 1)], projT[4096 + 256 * j:4096 + 256 * (j + 1)],
                                         projT[6144 + 256 * j:6144 + 256 * (j + 1)], projT[8192 + 256 * j:8192 + 256 * (j + 1)],
                                         projT[18432 + 4 * j:18432 + 4 * j + 4]], axis=0))
            ins.append(d)
        del projT
        rG = _run("BG", build_BG, ins)
        ofull = np.concatenate([r["oT"] for r in rG], axis=0)
        wglu = C(inp["w_glu"][l]); wgdn = C(inp["w_gdn_out"][l])
        rM = _run("C1a", build_C1a, [{"yT": C(yfull[:, c * 1024:(c + 1) * 1024]), "oT": C(ofull[:, c * 1024:(c + 1) * 1024]),
                                       "gsT": gates[c][0], "gdT": gates[c][1], "w_glu": wglu, "w_gdn": wgdn} for c in range(8)])
        del gates
        wout = C(inp["w_out"][l]); nwp = _col(inp["mix_post_w"][l])
        r1 = _run("C1b", build_C1b, [{"mT": rM[c]["mT"], "xT": xs[c], "w_out": wout, "nw": nwp} for c in range(8)])
        w1 = C(inp["w_ff1"][l]); w2 = C(inp["w_ff2"][l])
        n1 = _col(inp["ffn_pre_w"][l]); n2 = _col(inp["ffn_post_w"][l])
        r2 = _run("C2", build_C2, [{"x1T": r1[c]["x1T"], "w1": w1, "w2": w2, "nw1": n1, "nw2": n2} for c in range(8)])
        xs = [r2[c]["x2T"] for c in range(8)]
    out = np.concatenate([xc.T for xc in xs], axis=0)
    return np.ascontiguousarray(out.reshape(1, 8192, 4096).astype(np.float32))
```

```python
import numpy as np
from contextlib import ExitStack
V, S, G, T_ = "vector", "scalar", "gpsimd", "tensor"
import concourse.bass as bass
import concourse.mybir as mybir
from concourse.bass_utils import run_bass_kernel_spmd

F32 = mybir.dt.float32
BF16 = mybir.dt.bfloat16
ALU = mybir.AluOpType
AF = mybir.ActivationFunctionType
AX = mybir.AxisListType

SAME_ENGINE_SYNC = True
NDMASEM = 48
NO_SELF_SYNC = ("tensor",)
COMPUTE = ("tensor", "vector", "scalar", "gpsimd")
ENGS = ("tensor", "vector", "scalar", "gpsimd", "sync")


class Prog:
    def __init__(self, name="k"):
        self.nc = bass.Bass("TRN2", target_bir_lowering=False)
        self.es = ExitStack()
        self.ops = {e: [] for e in ENGS}
        self.last_w = {}
        self.readers = {}
        self.ndma = 0
        self.dma_ops = []
        self.waited = {e: {x: -1 for x in COMPUTE} for e in ENGS}
        self.dma_waited = {e: set() for e in ENGS}
        self.nbuf = 0

    def dram(self, name, shape, dt, kind):
        return self.nc.dram_tensor(name, list(shape), dt, kind=kind).ap()

    def sbuf(self, name, shape, dt):
        return self.es.enter_context(self.nc.sbuf_tensor(name, list(shape), dt))

    def psum(self, name, shape, dt=F32):
        return self.es.enter_context(self.nc.psum_tensor(name, list(shape), dt))

    def _dep(self, eng, rec, dep):
        kind, deng, didx = dep
        if kind == "c":
            if deng == eng and (eng in NO_SELF_SYNC or not SAME_ENGINE_SYNC):
                return
            if self.waited[eng][deng] >= didx:
                return
            self.waited[eng][deng] = didx
            rec["waits"].append(dep)
            self.ops[deng][didx]["inc"] = True
        else:
            if didx in self.dma_waited[eng]:
                return
            self.dma_waited[eng].add(didx)
            rec["waits"].append(dep)

    def op(self, eng, fn, reads=(), writes=(), dma=False):
        rec = {"fn": fn, "waits": [], "inc": False, "dma": None}
        idx = len(self.ops[eng])
        for b in reads:
            lw = self.last_w.get(b)
            if lw is not None:
                self._dep(eng, rec, lw)
        for b in writes:
            lw = self.last_w.get(b)
            if lw is not None:
                self._dep(eng, rec, lw)
            for r in self.readers.get(b, ()):
                self._dep(eng, rec, r)
        if dma:
            d = self.ndma
            self.ndma += 1
            rec["dma"] = d
            if d >= NDMASEM:
                self._dep(eng, rec, ("d", None, d - NDMASEM))
            me = ("d", eng, d)
            self.dma_ops.append((eng, idx))
        else:
            me = ("c", eng, idx)
        for b in reads:
            self.readers.setdefault(b, []).append(me)
        for b in writes:
            self.last_w[b] = me
            self.readers[b] = []
        self.ops[eng].append(rec)
        return me

    def dma(self, eng, out, in_, reads=(), writes=(), **kw):
        return self.op(eng, lambda e: e.dma_start(out=out, in_=in_, **kw), reads, writes, dma=True)

    def build(self):
        nc = self.nc
        for eng in ("sync", "gpsimd", "scalar"):
            mine = [d for (e, i) in self.dma_ops for d in [self.ops[e][i]["dma"]] if e == eng]
            if not mine:
                continue
            rec = {"fn": None, "waits": [], "inc": False, "dma": None}
            for d in mine:
                if d not in self.dma_waited[eng]:
                    rec["waits"].append(("d", None, d))
            self.ops[eng].append(rec)
        dsems = [self.es.enter_context(nc.semaphore("d%d" % i)) for i in range(NDMASEM)]
        EPOCH = 8192
        sems = {}
        for e in COMPUTE:
            c = 0
            for rec in self.ops[e]:
                if rec["inc"]:
                    rec["ticket"] = (c // EPOCH, c % EPOCH + 1)
                    c += 1
            sems[e] = [self.es.enter_context(nc.semaphore("s_%s%d" % (e, k))) for k in range(max(1, (c + EPOCH - 1) // EPOCH))]
        ops = self.ops

        def replay(ename, eh):
            for rec in ops[ename]:
                for (kind, deng, didx) in rec["waits"]:
                    if kind == "c":
                        ep, val = ops[deng][didx]["ticket"]
                        eh.wait_ge(sems[deng][ep], val)
                    else:
                        eh.wait_ge(dsems[didx % NDMASEM], 16 * (didx // NDMASEM + 1))
                if rec["fn"] is None:
                    continue
                ins = rec["fn"](eh)
                if rec["dma"] is not None:
                    ins.then_inc(dsems[rec["dma"] % NDMASEM], 16)
                elif rec["inc"]:
                    ins.then_inc(sems[ename][rec["ticket"][0]], 1)

        with nc.Block() as block:
            @block.tensor
            def _(t):
                replay("tensor", t)

            @block.vector
            def _(v):
                replay("vector", v)

            @block.scalar
            def _(s):
                replay("scalar", s)

            @block.gpsimd
            def _(g):
                replay("gpsimd", g)

            @block.sync
            def _(s):
                replay("sync", s)
        self.es.close()
        return nc


class Pool:
    def __init__(self, P, name, n, shape, dt, psum=False):
        self.bufs = [(P.psum if psum else P.sbuf)("%s%d" % (name, i), shape, dt) for i in range(n)]
        self.name = name
        self.i = 0

    def next(self):
        k = self.i % len(self.bufs)
        self.i += 1
        return self.bufs[k], (self.name, k)


class Gemm:
    def __init__(self, P, wbytes=16384, npsum=6, nstage=4, nwb=3):
        self.P = P
        self.wel = wbytes // 2
        self.wb = Pool(P, "wb", nwb, [128, self.wel], BF16)
        self.ps = Pool(P, "ps", npsum, [128, 512], F32, psum=True)
        self.stage = Pool(P, "stg", nstage, [128, 512], F32)
        self.evq = 0

    def run(self, actT, act_tok, KT, W, ncols, evac, ntb=2, col_groups=None):
        P = self.P
        gc = max(128, (self.wel // KT) // 128 * 128)
        gc = min(gc, 512)
        if col_groups is None:
            col_groups = []
            c = 0
            while c < ncols:
                w = min(gc, ncols - c)
                col_groups.append([(c, w)])
                c += w
        Wv = W.rearrange("(kt p) n -> p kt n", p=128)
        nt_idx = 0
        for grp in col_groups:
            wbuf, wtok = self.wb.next()
            tot = sum(w for _, w in grp)
            wv = wbuf[:, 0:KT * tot].rearrange("p (kt n) -> p kt n", kt=KT)
            off = 0
            for (c0, w) in grp:
                P.dma("gpsimd", wv[:, :, off:off + w], Wv[:, :, c0:c0 + w], writes=[wtok])
                off += w
            for j in range(tot // 128):
                for tb in range(ntb):
                    ps, ptok = self.ps.next()
                    for kt in range(KT):
                        P.op("tensor", (lambda e, ps=ps, kt=kt, j=j, tb=tb, wv=wv: e.matmul(
                            ps[:], wv[:, kt, j * 128:(j + 1) * 128], actT[:, kt, tb * 512:(tb + 1) * 512],
                            start=(kt == 0), stop=(kt == KT - 1))),
                            reads=[wtok, (act_tok(kt, tb) if callable(act_tok) else act_tok)], writes=[ptok])
                    evac(nt_idx, tb, ps, ptok)
                nt_idx += 1

    def evac_to_dram(self, outT, tok=None):
        P = self.P
        def ev(nt, tb, ps, ptok):
            st, stok = self.stage.next()
            eng = "vector" if (self.evq % 2 == 0) else "scalar"
            self.evq += 1
            if eng == "vector":
                P.op("vector", lambda e: e.tensor_copy(st[:], ps[:]), reads=[ptok], writes=[stok])
            else:
                P.op("scalar", lambda e: e.copy(st[:], ps[:]), reads=[ptok], writes=[stok])
            P.dma("sync", outT[nt * 128:(nt + 1) * 128, tb * 512:(tb + 1) * 512], st[:], reads=[stok], writes=([(tok, nt, tb)] if tok else []))
        return ev

import math
PI = math.pi

def ssm_host_inputs(j, a_re, a_im, log_dt, b_re, b_im, c_re, c_im, d_skip):
    G0 = 16 * j
    gs = np.arange(G0, G0 + 16)
    ssmA = np.zeros((128, 5, 2, 64), np.float32)
    for ct in range(2):
        for gp in range(8):
            g = G0 + ct * 8 + gp
            sl = slice(gp * 16, gp * 16 + 16)
            ssmA[sl, 0, ct, :] = a_re[g][None, :]
            ssmA[sl, 1, ct, :] = a_im[g][None, :]
            ssmA[sl, 2, ct, :] = log_dt[g]
            ssmA[sl, 3, ct, :] = b_re[g].T
            ssmA[sl, 4, ct, :] = b_im[g].T
    ssmB = np.zeros((128, 3, 16), np.float32)
    ssmB[:, 0, :] = np.concatenate([a_re[gs].T, a_re[gs].T], 0)
    ssmB[:, 1, :] = np.concatenate([a_im[gs].T, a_im[gs].T], 0)
    ssmB[:, 2, :] = log_dt[gs][None, :]
    cc = np.zeros((128, 2, 16, 16), np.float32)
    cre = np.transpose(c_re[gs], (2, 0, 1))
    cim = np.transpose(c_im[gs], (2, 0, 1))
    cc[0:64, 0] = cre; cc[64:128, 0] = cim
    cc[0:64, 1] = cim; cc[64:128, 1] = cre
    dsk = np.ascontiguousarray(d_skip[G0 * 16:G0 * 16 + 256].reshape(2, 128).T)
    return {"ssmA": ssmA, "ssmB": ssmB, "ssmcc": cc, "ssmd": dsk}

def ssm_consts():
    iota = np.tile(np.arange(512, dtype=np.float32)[None, :], (128, 1))
    ident = np.eye(128, dtype=np.float32)
    Jm = np.zeros((128, 128), np.float32)
    for n in range(64):
        Jm[64 + n, n] = -1.0
        Jm[n, 64 + n] = 1.0
    sgn = np.ones((128, 2), np.float32); sgn[64:, 0] = -1.0; sgn[:, 1] = -1.0
    mask = np.zeros((128, 16), np.float32)
    for p in range(128):
        mask[p, p // 16] = 1.0
        mask[p, 8 + p // 16] = -1.0
    return {"c_iota": iota, "c_ident": ident, "c_J": Jm, "c_sgn": sgn, "c_mask": mask}

def emit_ssm(P, uT, yT, D, NCH=16):
    V, S, G, T_ = "vector", "scalar", "gpsimd", "tensor"
    def ld(name, shape, src, dt=F32):
        t = P.sbuf("sb_" + name, shape, dt)
        P.dma("sync", t[:], src, writes=[name])
        return t
    sA = ld("ssmA", [128, 5, 2, 64], D["ssmA"])
    sB = ld("ssmB", [128, 3, 16], D["ssmB"])
    cc = ld("ssmcc", [128, 2, 16, 16], D["ssmcc"])
    dsk = ld("ssmd", [128, 2], D["ssmd"])
    iota = ld("c_iota", [128, 512], D["c_iota"])
    ident = ld("c_ident", [128, 128], D["c_ident"])
    Jm = ld("c_J", [128, 128], D["c_J"])
    sgn = ld("c_sgn", [128, 2], D["c_sgn"])
    mask = ld("c_mask", [128, 16], D["c_mask"])
    negpi = P.sbuf("negpi", [128, 1], F32)
    P.op(V, lambda e: e.memset(negpi[:], -PI), writes=["negpi"])

    cnt = [0]
    def tmp(shape, dt=F32):
        cnt[0] += 1
        n = "tmp%d" % cnt[0]
        return P.sbuf(n, shape, dt), n
    def vop(fn, reads, writes, eng=V):
        P.op(eng, fn, reads=reads, writes=writes)
    I32 = mybir.dt.int32
    C1 = 6.28125
    C2 = 2 * PI - C1
    sc_tmp = {}
    def sincos_into(x_ap, xn, s_ap, sn, c_ap, cn, shape):
        key = tuple(shape)
        if key not in sc_tmp:
            sc_tmp[key] = (tmp(shape), tmp(shape), tmp(shape), tmp(shape, I32), tmp(shape))
        (y, yn), (kf, kfn), (r, rn), (ki, kin), (m, mn) = sc_tmp[key]
        vop(lambda e: e.tensor_scalar(y[:], x_ap, PI, None, ALU.add), [xn], [yn])
        vop(lambda e: e.tensor_scalar(kf[:], y[:], 1.0 / (2 * PI), None, ALU.mult), [yn], [kfn])
        vop(lambda e: e.tensor_copy(ki[:], kf[:]), [kfn], [kin])
        vop(lambda e: e.tensor_copy(kf[:], ki[:]), [kin], [kfn])
        vop(lambda e: e.scalar_tensor_tensor(r[:], kf[:], -C1, y[:], ALU.mult, ALU.add), [kfn, yn], [rn])
        vop(lambda e: e.scalar_tensor_tensor(r[:], kf[:], -C2, r[:], ALU.mult, ALU.add), [kfn, rn], [rn])
        vop(lambda e: e.tensor_scalar(m[:], r[:], 0.0, 2 * PI, ALU.is_lt, ALU.mult), [rn], [mn])
        vop(lambda e: e.tensor_tensor(r[:], r[:], m[:], ALU.add), [rn, mn], [rn])
        vop(lambda e: e.tensor_scalar(m[:], r[:], 2 * PI, -2 * PI, ALU.is_ge, ALU.mult), [rn], [mn])
        vop(lambda e: e.tensor_tensor(r[:], r[:], m[:], ALU.add), [rn, mn], [rn])
        vop(lambda e: e.activation(s_ap, r[:], AF.Sin, bias=negpi[:, 0:1], scale=1.0), [rn, "negpi"], [sn], eng=S)
        vop(lambda e: e.tensor_scalar(y[:], r[:], 0.5 * PI, None, ALU.add), [rn], [yn])
        vop(lambda e: e.tensor_scalar(m[:], y[:], 2 * PI, -2 * PI, ALU.is_ge, ALU.mult), [yn], [mn])
        vop(lambda e: e.tensor_tensor(y[:], y[:], m[:], ALU.add), [yn, mn], [yn])
        vop(lambda e: e.activation(c_ap, y[:], AF.Sin, bias=negpi[:, 0:1], scale=1.0), [yn, "negpi"], [cn], eng=S)
    def sincos(x, xn, shape):
        s_, sn = tmp(shape); c_, cn = tmp(shape)
        sincos_into(x[:], xn, s_[:], sn, c_[:], cn, shape)
        return (s_, sn), (c_, cn)

    shA = [128, 128]
    def A(i):
        return sA[:, i].rearrange("p a n -> p (a n)")
    dtA, dtAn = tmp(shA); ardt, ardtn = tmp(shA); aidt, aidtn = tmp(shA); mag, magn = tmp(shA)
    vop(lambda e: e.activation(dtA[:], A(2), AF.Exp), ["ssmA"], [dtAn], eng=S)
    vop(lambda e: e.tensor_tensor(ardt[:], A(0), dtA[:], ALU.mult), ["ssmA", dtAn], [ardtn])
    vop(lambda e: e.tensor_tensor(aidt[:], A(1), dtA[:], ALU.mult), ["ssmA", dtAn], [aidtn])
    vop(lambda e: e.activation(mag[:], ardt[:], AF.Exp), [ardtn], [magn], eng=S)
    (sn_, snn), (cs_, csn) = sincos(aidt, aidtn, shA)
    abr, abrn = tmp(shA); abi, abin = tmp(shA); den, denn = tmp(shA); t0, t0n = tmp(shA); t1, t1n = tmp(shA)
    fr, frn = tmp(shA); fi, fin = tmp(shA); bbr, bbrn = tmp(shA); bbi, bbin = tmp(shA)
    vop(lambda e: e.tensor_tensor(abr[:], mag[:], cs_[:], ALU.mult), [magn, csn], [abrn])
    vop(lambda e: e.tensor_tensor(abi[:], mag[:], sn_[:], ALU.mult), [magn, snn], [abin])
    vop(lambda e: e.tensor_tensor(t0[:], A(0), A(0), ALU.mult), ["ssmA"], [t0n])
    vop(lambda e: e.tensor_tensor(t1[:], A(1), A(1), ALU.mult), ["ssmA"], [t1n])
    vop(lambda e: e.tensor_tensor(den[:], t0[:], t1[:], ALU.add), [t0n, t1n], [denn])
    vop(lambda e: e.reciprocal(den[:], den[:]), [denn], [denn])
    vop(lambda e: e.tensor_scalar(abr[:], abr[:], -1.0, None, ALU.add), [abrn], [abrn])
    vop(lambda e: e.tensor_tensor(t0[:], abr[:], A(0), ALU.mult), [abrn, "ssmA"], [t0n])
    vop(lambda e: e.tensor_tensor(t1[:], abi[:], A(1), ALU.mult), [abin, "ssmA"], [t1n])
    vop(lambda e: e.tensor_tensor(fr[:], t0[:], t1[:], ALU.add), [t0n, t1n], [frn])
    vop(lambda e: e.tensor_tensor(fr[:], fr[:], den[:], ALU.mult), [frn, denn], [frn])
    vop(lambda e: e.tensor_tensor(t0[:], abi[:], A(0), ALU.mult), [abin, "ssmA"], [t0n])
    vop(lambda e: e.tensor_tensor(t1[:], abr[:], A(1), ALU.mult), [abrn, "ssmA"], [t1n])
    vop(lambda e: e.tensor_tensor(fi[:], t0[:], t1[:], ALU.subtract), [t0n, t1n], [fin])
    vop(lambda e: e.tensor_tensor(fi[:], fi[:], den[:], ALU.mult), [fin, denn], [fin])
    vop(lambda e: e.tensor_tensor(t0[:], fr[:], A(3), ALU.mult), [frn, "ssmA"], [t0n])
    vop(lambda e: e.tensor_tensor(t1[:], fi[:], A(4), ALU.mult), [fin, "ssmA"], [t1n])
    vop(lambda e: e.tensor_tensor(bbr[:], t0[:], t1[:], ALU.subtract), [t0n, t1n], [bbrn])
    vop(lambda e: e.tensor_tensor(t0[:], fr[:], A(4), ALU.mult), [frn, "ssmA"], [t0n])
    vop(lambda e: e.tensor_tensor(t1[:], fi[:], A(3), ALU.mult), [fin, "ssmA"], [t1n])
    vop(lambda e: e.tensor_tensor(bbi[:], t0[:], t1[:], ALU.add), [t0n, t1n], [bbin])
    LB1 = P.sbuf("LB1", [128, 16, 128], BF16); LB2 = P.sbuf("LB2", [128, 16, 128], BF16)
    for g in range(16):
        ct, gp = g // 8, g % 8
        cs = slice(ct * 64, ct * 64 + 64)
        vop(lambda e, g=g, gp=gp, cs=cs: e.tensor_scalar(LB1[:, g, 0:64], bbr[:, cs], mask[:, gp:gp + 1], None, ALU.mult), [bbrn, "c_mask"], ["LB"])
        vop(lambda e, g=g, gp=gp, cs=cs: e.tensor_scalar(LB1[:, g, 64:128], bbi[:, cs], mask[:, gp:gp + 1], None, ALU.mult), [bbin, "c_mask"], ["LB"])
        vop(lambda e, g=g, gp=gp, cs=cs: e.tensor_scalar(LB2[:, g, 0:64], bbi[:, cs], mask[:, gp:gp + 1], None, ALU.mult), [bbin, "c_mask"], ["LB"])
        vop(lambda e, g=g, gp=gp, cs=cs: e.tensor_scalar(LB2[:, g, 64:128], bbr[:, cs], mask[:, 8 + gp:9 + gp], None, ALU.mult), [bbrn, "c_mask"], ["LB"])
    shB = [128, 16]
    dtB, dtBn = tmp(shB); th, thn = tmp(shB); rho, rhon = tmp(shB); thT, thTn = tmp(shB)
    vop(lambda e: e.activation(dtB[:], sB[:, 2, :], AF.Exp), ["ssmB"], [dtBn], eng=S)
    vop(lambda e: e.tensor_tensor(th[:], sB[:, 1, :], dtB[:], ALU.mult), ["ssmB", dtBn], [thn])
    vop(lambda e: e.tensor_tensor(rho[:], sB[:, 0, :], dtB[:], ALU.mult), ["ssmB", dtBn], [rhon])
    vop(lambda e: e.activation(rho[:], rho[:], AF.Exp), [rhon], [rhon], eng=S)
    vop(lambda e: e.tensor_scalar(thT[:], th[:], 512.0, None, ALU.mult), [thn], [thTn])
    (sT, sTn), (cT, cTn) = sincos(thT, thTn, shB)
    St = P.sbuf("St", [128, 16, 512], F32); Ct = P.sbuf("Ct", [128, 16, 512], F32)
    ang = P.sbuf("ang", [128, 512], F32)
    for g in range(16):
        vop(lambda e, g=g: e.tensor_scalar(ang[:], iota[:], th[:, g:g + 1], None, ALU.mult), ["c_iota", thn], ["ang"])
        sincos_into(ang[:], "ang", St[:, g, :], ("St", g), Ct[:, g, :], ("Ct", g), [128, 512])
    Rm = P.sbuf("Rm", [128, 16, 128], F32)
    rt, rtn = tmp([128, 128])
    for g in range(16):
        vop(lambda e, g=g: e.tensor_scalar(rt[:], Jm[:], sT[:, g:g + 1], None, ALU.mult), ["c_J", sTn], [rtn])
        vop(lambda e, g=g: e.scalar_tensor_tensor(Rm[:, g, :], ident[:], cT[:, g:g + 1], rt[:], ALU.mult, ALU.add), ["c_ident", cTn, rtn], [("Rm", g)])
    LC1 = P.sbuf("LC1", [128, 16, 128], BF16); LC2 = P.sbuf("LC2", [128, 16, 128], BF16)
    vop(lambda e: e.memset(LC1[:], 0.0), [], ["LC"])
    vop(lambda e: e.memset(LC2[:], 0.0), [], ["LC"])
    for g in range(16):
        gp = g % 8
        vop(lambda e, g=g, gp=gp: e.tensor_scalar(LC1[:, g, gp * 16:gp * 16 + 16], cc[:, 0, g, :], sgn[:, 0:1], None, ALU.mult), ["ssmcc", "c_sgn"], ["LC"])
        vop(lambda e, g=g, gp=gp: e.tensor_scalar(LC2[:, g, gp * 16:gp * 16 + 16], cc[:, 1, g, :], sgn[:, 1:2], None, ALU.mult), ["ssmcc", "c_sgn"], ["LC"])

    u32p = Pool(P, "u32", 3, [128, 512], F32)
    ubfp = Pool(P, "ubf", 3, [128, 512], BF16)
    abp = Pool(P, "abps", 4, [128, 512], F32, psum=True)
    yps = Pool(P, "yps", 2, [128, 512], F32, psum=True)
    ips = Pool(P, "ips", 1, [128, 16], F32, psum=True)
    t1p = Pool(P, "st1", 4, [128, 512], F32)
    t2p = Pool(P, "st2", 4, [128, 512], F32)
    winp = Pool(P, "win", 4, [128, 512], F32)
    wstp = Pool(P, "wst", 4, [128, 512], F32)
    p1p = Pool(P, "p1", 4, [128, 512], BF16)
    p2p = Pool(P, "p2", 4, [128, 512], BF16)
    ytp = Pool(P, "yt", 2, [128, 512], F32)
    ybp = Pool(P, "yb", 2, [128, 512], BF16)
    init = P.sbuf("init", [128, 16], F32)
    wlast = P.sbuf("wlast", [128, 16], F32)
    items = [(c, ct, gp) for c in range(NCH) for ct in range(2) for gp in range(8)]
    tiles = {}
    xres = {}
    def emit_x(it):
        c, ct, gp = it
        if gp == 0:
            u32, u32n = u32p.next(); ubf, ubfn = ubfp.next()
            P.dma("sync", u32[:], uT[ct * 128:(ct + 1) * 128, c * 512:(c + 1) * 512], writes=[u32n])
            vop(lambda e, ubf=ubf, u32=u32: e.copy(ubf[:], u32[:]), [u32n], [ubfn], eng=S)
            tiles[(c, ct)] = (u32, u32n, ubf, ubfn)
        u32, u32n, ubf, ubfn = tiles[(c, ct)]
        g = ct * 8 + gp
        Aps, An = abp.next(); Bps, Bn = abp.next()
        P.op(T_, lambda e, Aps=Aps, g=g, ubf=ubf: e.matmul(Aps[:], LB1[:, g, :], ubf[:], start=True, stop=True), reads=["LB", ubfn], writes=[An])
        P.op(T_, lambda e, Bps=Bps, g=g, ubf=ubf: e.matmul(Bps[:], LB2[:, g, :], ubf[:], start=True, stop=True), reads=["LB", ubfn], writes=[Bn])
        xres[it] = (Aps, An, Bps, Bn)
    ybank = {}
    emit_x(items[0])
    for idx, it in enumerate(items):
        c, ct, gp = it
        g = ct * 8 + gp
        if idx + 1 < len(items):
            emit_x(items[idx + 1])
        u32, u32n, ubf, ubfn = tiles[(c, ct)]
        Aps, An, Bps, Bn = xres.pop(it)
        if gp == 0:
            ybank[(c, ct)] = yps.next()
        Y, Yn = ybank[(c, ct)]
        a1, a1n = t1p.next(); a2, a2n = t2p.next(); win, winn = winp.next(); wst, wstn = wstp.next()
        vop(lambda e, a1=a1, Aps=Aps, g=g: e.tensor_tensor(a1[:], Ct[:, g, :], Aps[:], ALU.mult), [("Ct", g), An], [a1n])
        vop(lambda e, a2=a2, Bps=Bps, g=g: e.tensor_tensor(a2[:], St[:, g, :], Bps[:], ALU.mult), [("St", g), Bn], [a2n])
        vop(lambda e, a1=a1, a2=a2, win=win: e.tensor_tensor(win[:], a1[:], a2[:], ALU.add), [a1n, a2n], [winn])
        if c > 0:
            ip, ipn = ips.next()
            P.op(T_, lambda e, ip=ip, g=g: e.matmul(ip[:, g:g + 1], Rm[:, g, :], wlast[:, g:g + 1], start=True, stop=True), reads=[("Rm", g), ("wlast", g)], writes=[ipn])
            vop(lambda e, ip=ip, g=g: e.copy(init[:, g:g + 1], ip[:, g:g + 1]), [ipn], [("init", g)], eng=S)
            vop(lambda e, wst=wst, win=win, g=g: e.tensor_tensor_scan(wst[:], rho[:, g:g + 1].to_broadcast([128, 512]), win[:], init[:, g:g + 1], ALU.mult, ALU.add),
                [rhon, winn, ("init", g)], [wstn])
        else:
            vop(lambda e, wst=wst, win=win, g=g: e.tensor_tensor_scan(wst[:], rho[:, g:g + 1].to_broadcast([128, 512]), win[:], 0.0, ALU.mult, ALU.add),
                [rhon, winn], [wstn])
        vop(lambda e, wst=wst, g=g: e.copy(wlast[:, g:g + 1], wst[:, 511:512]), [wstn], [("wlast", g)], eng=S)
        p1, p1n = p1p.next(); p2, p2n = p2p.next()
        vop(lambda e, p1=p1, wst=wst, g=g: e.tensor_tensor(p1[:], Ct[:, g, :], wst[:], ALU.mult), [("Ct", g), wstn], [p1n])
        vop(lambda e, p2=p2, wst=wst, g=g: e.tensor_tensor(p2[:], St[:, g, :], wst[:], ALU.mult), [("St", g), wstn], [p2n], eng=G)
        P.op(T_, lambda e, Y=Y, g=g, p1=p1, gp=gp: e.matmul(Y[:], LC1[:, g, :], p1[:], start=(gp == 0), stop=False), reads=["LC", p1n], writes=[Yn])
        P.op(T_, lambda e, Y=Y, g=g, p2=p2, gp=gp: e.matmul(Y[:], LC2[:, g, :], p2[:], start=False, stop=(gp == 7)), reads=["LC", p2n], writes=[Yn])
        if gp == 7:
            yt, ytn = ytp.next(); yb, ybn = ybp.next()
            vop(lambda e, yt=yt, u32=u32, Y=Y, ct=ct: e.scalar_tensor_tensor(yt[:], u32[:], dsk[:, ct:ct + 1], Y[:], ALU.mult, ALU.add), [u32n, "ssmd", Yn], [ytn])
            vop(lambda e, yt=yt, yb=yb: e.activation(yb[:], yt[:], AF.Gelu), [ytn], [ybn], eng=S)
            P.dma("sync", yT[ct * 128:(ct + 1) * 128, c * 512:(c + 1) * 512], yb[:], reads=[ybn])


def gdn_host_params(j, conv_w, a_log, dt_bias, norm_w):
    cw = np.zeros((128, 3, 2, 4), np.float32)
    for s in range(3):
        for h in range(2):
            c0 = s * 2048 + (2 * j + h) * 128
            cw[:, s, h, :] = conv_w[:, c0:c0 + 128].T
    return {"g_cw": cw,
            "g_alog": np.ascontiguousarray(a_log[2 * j:2 * j + 2].reshape(2, 1)),
            "g_dtb": np.ascontiguousarray(dt_bias[2 * j:2 * j + 2].reshape(2, 1)),
            "g_nw": np.ascontiguousarray(norm_w.reshape(128, 1)),
            "g_ident": np.eye(128, dtype=np.float32)}

def emit_gdn(P, gin, oT, D, T=8192, dbg=None):
    nc = P.nc
    NB = T // 512
    def vop(fn, reads, writes, eng=V):
        P.op(eng, fn, reads=reads, writes=writes)
    def ld(name, shape, src):
        t = P.sbuf("sb_" + name, shape, F32)
        P.dma("sync", t[:], src, writes=[name])
        return t
    cw = ld("g_cw", [128, 3, 2, 4], D["g_cw"])
    alog = ld("g_alog", [2, 1], D["g_alog"])
    dtb = ld("g_dtb", [2, 1], D["g_dtb"])
    nw = ld("g_nw", [128, 1], D["g_nw"])
    ident = ld("g_ident", [128, 128], D["g_ident"])
    ones_bf = P.sbuf("g_ones", [128, 128], BF16)
    vop(lambda e: e.memset(ones_bf[:], 1.0), [], ["g_ones"])
    epsc = P.sbuf("g_eps", [128, 1], F32)
    vop(lambda e: e.memset(epsc[:], 1e-6), [], ["g_eps"])
    onec = P.sbuf("g_one", [128, 1], F32)
    vop(lambda e: e.memset(onec[:], 1.0), [], ["g_one"])
    RC = min(T, 2048)
    arow = P.sbuf("g_arow", [2, RC], F32); brow = P.sbuf("g_brow", [2, RC], F32); tr = P.sbuf("g_tr", [2, RC], F32)
    nega = P.sbuf("g_nega", [2, 1], F32)
    vop(lambda e: e.activation(nega[:], alog[:], AF.Exp), ["g_alog"], ["g_nega"], eng=S)
    vop(lambda e: e.tensor_scalar(nega[:], nega[:], -1.0, None, ALU.mult), ["g_nega"], ["g_nega"])
    scr = nc.dram_tensor("g_scr", [3, 2, T], F32).ap()
    for rc in range(T // RC):
        cs = slice(rc * RC, (rc + 1) * RC)
        P.dma("sync", arow[:], gin[1024:1026, cs], writes=["g_arow"])
        P.dma("sync", brow[:], gin[1026:1028, cs], writes=["g_brow"])
        vop(lambda e: e.activation(tr[:], arow[:], AF.Exp, bias=dtb[:, 0:1], scale=1.0), ["g_arow", "g_dtb"], ["g_tr"], eng=S)
        vop(lambda e: e.activation(tr[:], tr[:], AF.Ln, bias=onec[0:2, 0:1], scale=1.0), ["g_tr", "g_one"], ["g_tr"], eng=S)
        vop(lambda e: e.tensor_scalar(tr[:], tr[:], nega[:, 0:1], None, ALU.mult), ["g_tr", "g_nega"], ["g_tr"])
        vop(lambda e: e.activation(arow[:], tr[:], AF.Exp), ["g_tr"], ["g_arow"], eng=S)
        vop(lambda e: e.activation(brow[:], brow[:], AF.Sigmoid), ["g_brow"], ["g_brow"], eng=S)
        vop(lambda e: e.scalar_tensor_tensor(tr[:], arow[:], -1.0, brow[:], ALU.mult, ALU.mult), ["g_arow", "g_brow"], ["g_tr"])
        P.dma("sync", scr[0, :, cs], arow[:], reads=["g_arow"], writes=[("scr", 0, rc)])
        P.dma("sync", scr[1, :, cs], tr[:], reads=["g_tr"], writes=[("scr", 1, rc)])
        P.dma("sync", scr[2, :, cs], brow[:], reads=["g_brow"], writes=[("scr", 2, rc)])
    NRC = T // RC
    natok = P.sbuf("g_natok", [128, 2, T // 128], F32)
    for h in range(2):
        P.dma("sync", natok[:, h, :], scr[1, h].rearrange("(b p) -> p b", p=128), reads=[("scr", 1, r_) for r_ in range(NRC)], writes=[("natok", h)],
              allow_slow_non_contiguous=True)
    kcol = [P.sbuf("g_kcol%d" % h, [128, T], BF16) for h in range(2)]
    qcol = [P.sbuf("g_qcol%d" % h, [128, T], BF16) for h in range(2)]
    S32 = [[P.sbuf("g_S32_%d_%d" % (h, i), [128, 128], F32) for i in range(2)] for h in range(2)]
    Sbf = [[P.sbuf("g_Sbf_%d_%d" % (h, i), [128, 128], BF16) for i in range(2)] for h in range(2)]
    for h in range(2):
        vop(lambda e, h=h: e.memset(S32[h][0][:], 0.0), [], [("S32", h, 0)])
        vop(lambda e, h=h: e.memset(Sbf[h][0][:], 0.0), [], [("Sbf", h, 0)])
    rawp = Pool(P, "g_raw", 3, [128, 515], F32)
    accp = Pool(P, "g_acc", 3, [128, 512], F32)
    sqp = Pool(P, "g_sq", 2, [128, 512], BF16)
    rnp = Pool(P, "g_rn", 2, [128, 512], F32)
    bcp = Pool(P, "g_bc", 2, [128, 512], F32)
    A128p = [Pool(P, "g_A%d" % h, 2, [128, 512], F32) for h in range(2)]
    ktokp = [Pool(P, "g_kt%d" % h, 2, [128, 4, 128], BF16) for h in range(2)]
    bvtokp = [Pool(P, "g_bv%d" % h, 2, [128, 4, 128], F32) for h in range(2)]
    knp = Pool(P, "g_kn", 2, [128, 512], F32)
    bvp = Pool(P, "g_bvf", 2, [128, 512], F32)
    zp = [Pool(P, "g_z%d" % h, 2, [128, 512], F32) for h in range(2)]
    kmp = [Pool(P, "g_km%d" % h, 3, [128, 128], BF16) for h in range(2)]
    tmpp = [Pool(P, "g_tmp%d" % h, 2, [128, 128], BF16) for h in range(2)]
    pb = [P.psum("g_pb%d" % i, [128, 512], F32) for i in range(8)]
    ksps = [[(pb[0 + h][:, i * 128:(i + 1) * 128], ("ksps", h, i)) for i in range(2)] for h in range(2)]
    dsps = [[(pb[4 + h][:, i * 128:(i + 1) * 128], ("dsps", h, i)) for i in range(2)] for h in range(2)]
    ops_ = [[(pb[2 + h], ("ops", h, 0)) for i in range(2)] for h in range(2)]
    ssps = (pb[6], "ssps")
    trps = (pb[7], "trps")

    def conv_silu(h, s, blk):
        raw, rn_ = rawp.next(); acc, an = accp.next()
        r0 = s * 256 + h * 128
        t0 = blk * 512
        if blk == 0:
            vop(lambda e: e.memset(raw[:, 0:3], 0.0), [], [rn_])
            P.dma("sync", raw[:, 3:515], gin[r0:r0 + 128, 0:512], writes=[rn_])
        else:
            P.dma("sync", raw[:], gin[r0:r0 + 128, t0 - 3:t0 + 512], writes=[rn_])
        vop(lambda e: e.tensor_scalar(acc[:], raw[:, 0:512], cw[:, s, h, 0:1], None, ALU.mult), [rn_, "g_cw"], [an])
        for j in range(1, 4):
            vop(lambda e, j=j: e.scalar_tensor_tensor(acc[:], raw[:, j:j + 512], cw[:, s, h, j:j + 1], acc[:], ALU.mult, ALU.add), [rn_, "g_cw", an], [an])
        vop(lambda e: e.activation(acc[:], acc[:], AF.Silu), [an], [an], eng=S)
        return acc, an

    def rnorm(acc, an, scale):
        sq, sn = sqp.next(); r, rn_ = rnp.next()
        vop(lambda e: e.activation(sq[:], acc[:], AF.Square), [an], [sn], eng=S)
        P.op(T_, lambda e: e.matmul(ssps[0][:], ones_bf[:], sq[:], start=True, stop=True), reads=["g_ones", sn], writes=[ssps[1]])
        vop(lambda e: e.activation(r[:], ssps[0][:], AF.Sqrt, bias=epsc[:, 0:1], scale=1.0), [ssps[1], "g_eps"], [rn_], eng=S)
        vop(lambda e: e.reciprocal(r[:], r[:]), [rn_], [rn_])
        if scale != 1.0:
            vop(lambda e: e.tensor_scalar(r[:], r[:], scale, None, ALU.mult), [rn_], [rn_])
        return r, rn_

    blkstate = {}
    def prep(h, blk):
        t0 = blk * 512
        acc, an = conv_silu(h, 0, blk)
        r, rn_ = rnorm(acc, an, 128.0 ** -0.5)
        vop(lambda e, acc=acc, r=r: e.tensor_tensor(qcol[h][:, t0:t0 + 512], acc[:], r[:], ALU.mult), [an, rn_], [("qcol", h, blk)])
        acc, an = conv_silu(h, 1, blk)
        r, rn_ = rnorm(acc, an, 1.0)
        kn, knn = knp.next()
        vop(lambda e, acc=acc, r=r, kn=kn: e.tensor_tensor(kn[:], acc[:], r[:], ALU.mult), [an, rn_], [knn])
        vop(lambda e, kn=kn: e.copy(kcol[h][:, t0:t0 + 512], kn[:]), [knn], [("kcol", h, blk)], eng=S)
        acc, an = conv_silu(h, 2, blk)
        bc, bcn = bcp.next()
        P.dma("sync", bc[:], scr[2, h:h + 1, t0:t0 + 512].partition_broadcast(128), reads=[("scr", 2, t0 // RC)], writes=[bcn])
        bv, bvn = bvp.next()
        vop(lambda e, acc=acc, bv=bv, bc=bc: e.tensor_tensor(bv[:], acc[:], bc[:], ALU.mult), [an, bcn], [bvn], eng=G)
        A1, A1n = A128p[h].next()
        P.dma("sync", A1[:], scr[0, h:h + 1, t0:t0 + 512].partition_broadcast(128), reads=[("scr", 0, t0 // RC)], writes=[A1n])
        z, zn = zp[h].next()
        P.dma("sync", z[:], gin[768 + h * 128:768 + (h + 1) * 128, t0:t0 + 512], writes=[zn])
        vop(lambda e, z=z: e.activation(z[:], z[:], AF.Silu), [zn], [zn], eng=S)
        kt, ktn = ktokp[h].next(); bvt, bvtn = bvtokp[h].next()
        for src, srcn, dst, dstn in ((kn, knn, kt, ktn), (bv, bvn, bvt, bvtn)):
            for i in range(4):
                P.op(T_, lambda e, src=src, i=i: e.transpose(trps[0][:, i * 128:(i + 1) * 128], src[:, i * 128:(i + 1) * 128], ident[:]),
                     reads=[srcn, "g_ident"], writes=[trps[1]])
            vop(lambda e, dst=dst: e.tensor_copy(dst[:].rearrange("p a b -> p (a b)"), trps[0][:]), [trps[1]], [dstn])
        blkstate[(h, blk)] = dict(A=A1, An=A1n, kt=kt, ktn=ktn, bvt=bvt, bvtn=bvtn, z=z, zn=zn)

    cur = [0, 0]
    def token_step(h, t):
        blk, b4, p = t // 512, (t % 512) // 128, t % 128
        st = blkstate[(h, blk)]
        c = cur[h]; n = 1 - c
        km, kmn = kmp[h].next(); tm, tmn = tmpp[h].next()
        ks, ksn = ksps[h][t % 2]; ds, dsn = dsps[h][t % 2]
        ob, obn = ops_[h][blk % 2]
        b128 = t // 128
        vop(lambda e: e.tensor_scalar(km[:], st["kt"][:, b4, :], ident[:, p:p + 1], None, ALU.mult), [st["ktn"], "g_ident"], [kmn], eng=G)
        P.op(T_, lambda e: e.matmul(ks, kcol[h][:, b128 * 128:(b128 + 1) * 128], Sbf[h][c][:], start=True, stop=True),
             reads=[("kcol", h, blk), ("Sbf", h, c)], writes=[ksn])
        vop(lambda e: e.scalar_tensor_tensor(tm[:], ks, natok[:, h, b128:b128 + 1], st["bvt"][:, b4, :], ALU.mult, ALU.add),
            [ksn, ("natok", h), st["bvtn"]], [tmn])
        P.op(T_, lambda e: e.matmul(ds, km[:], tm[:], start=True, stop=True), reads=[kmn, tmn], writes=[dsn])
        tl = t % 512
        vop(lambda e: e.scalar_tensor_tensor(Sbf[h][n][:], S32[h][c][:], st["A"][:, tl:tl + 1], ds, ALU.mult, ALU.add),
            [("S32", h, c), st["An"], dsn], [("Sbf", h, n)])
        vop(lambda e: e.scalar_tensor_tensor(S32[h][n][:], S32[h][c][:], st["A"][:, tl:tl + 1], ds, ALU.mult, ALU.add),
            [("S32", h, c), st["An"], dsn], [("S32", h, n)])
        P.op(T_, lambda e: e.matmul(ob[:, tl:tl + 1], Sbf[h][n][:], qcol[h][:, t:t + 1], start=True, stop=True),
             reads=[("Sbf", h, n), ("qcol", h, blk)], writes=[obn])
        cur[h] = n

    onp = Pool(P, "g_on", 2, [128, 512], F32)
    obp = Pool(P, "g_ob", 2, [128, 512], BF16)
    def finish(h, blk):
        st = blkstate[(h, blk)]
        ob, obn = ops_[h][blk % 2]
        sq, sn = sqp.next(); r, rn_ = rnp.next(); on, onn = onp.next(); o16, o16n = obp.next()
        vop(lambda e: e.activation(sq[:], ob[:], AF.Square), [obn], [sn], eng=S)
        P.op(T_, lambda e: e.matmul(ssps[0][:], ones_bf[:], sq[:], start=True, stop=True), reads=["g_ones", sn], writes=[ssps[1]])
        vop(lambda e: e.activation(r[:], ssps[0][:], AF.Sqrt, bias=epsc[:, 0:1], scale=1.0 / 128), [ssps[1], "g_eps"], [rn_], eng=S)
        vop(lambda e: e.reciprocal(r[:], r[:]), [rn_], [rn_])
        vop(lambda e: e.scalar_tensor_tensor(on[:], ob[:], nw[:, 0:1], r[:], ALU.mult, ALU.mult), [obn, "g_nw", rn_], [onn])
        vop(lambda e: e.tensor_tensor(o16[:], on[:], st["z"][:], ALU.mult), [onn, st["zn"]], [o16n], eng=G)
        P.dma("sync", oT[h * 128:(h + 1) * 128, blk * 512:(blk + 1) * 512], o16[:], reads=[o16n])

    for h in range(2):
        prep(h, 0)
    if dbg is not None:
        st = blkstate[(0, 0)]
        P.dma("sync", dbg["d_q"], qcol[0][:, 0:512], reads=[("qcol", 0, 0)])
        P.dma("sync", dbg["d_k"], kcol[0][:, 0:512], reads=[("kcol", 0, 0)])
        P.dma("sync", dbg["d_kt"], st["kt"][:].rearrange("p a b -> p (a b)"), reads=[st["ktn"]])
        P.dma("sync", dbg["d_bvt"], st["bvt"][:].rearrange("p a b -> p (a b)"), reads=[st["bvtn"]])
        P.dma("sync", dbg["d_A"], st["A"][:], reads=[st["An"]])
        P.dma("sync", dbg["d_na"], natok[:, 0, :], reads=[("natok", 0)])
    for blk in range(NB):
        if blk + 1 < NB:
            for h in range(2):
                prep(h, blk + 1)
        for t in range(blk * 512, (blk + 1) * 512):
            for h in range(2):
                token_step(h, t)
        for h in range(2):
            finish(h, blk)

NEG = -30000.0

def gdn2_consts():
    i = np.arange(64)
    negs = np.where(i[None, :] > i[:, None], 0.0, NEG).astype(np.float32)
    nega = np.where(i[:, None] > i[None, :], 0.0, NEG).astype(np.float32)
    return {"g_negs": np.ascontiguousarray(np.tile(negs[:, None, :], (1, 8, 1))),
            "g_nega": np.ascontiguousarray(np.tile(nega[:, None, :], (1, 8, 1))),
            "g_id8": np.ascontiguousarray(np.tile(np.eye(64, dtype=np.float32)[:, None, :], (1, 8, 1)))}

def emit_gdn2(P, gin, oT, D, T=8192, dbg=None):
    nc = P.nc
    NB = T // 512
    NCH = T // 64
    def vop(fn, reads, writes, eng=V):
        P.op(eng, fn, reads=reads, writes=writes)
    def ld(name, shape, src):
        t = P.sbuf("sb_" + name, shape, F32)
        P.dma("sync", t[:], src, writes=[name])
        return t
    cw = ld("g_cw", [128, 3, 2, 4], D["g_cw"])
    alog = ld("g_alog", [2, 1], D["g_alog"])
    dtb = ld("g_dtb", [2, 1], D["g_dtb"])
    nw = ld("g_nw", [128, 1], D["g_nw"])
    ident = ld("g_ident", [128, 128], D["g_ident"])
    negs = ld("g_negs", [64, 8, 64], D["g_negs"])
    negA = ld("g_nega", [64, 8, 64], D["g_nega"])
    id8 = ld("g_id8", [64, 8, 64], D["g_id8"])
    identb = P.sbuf("g_identb", [128, 128], BF16)
    vop(lambda e: e.tensor_copy(identb[:], ident[:]), ["g_ident"], ["g_identb"])
    ones_bf = P.sbuf("g_ones", [128, 128], BF16)
    vop(lambda e: e.memset(ones_bf[:], 1.0), [], ["g_ones"])
    epsc = P.sbuf("g_eps", [128, 1], F32)
    vop(lambda e: e.memset(epsc[:], 1e-6), [], ["g_eps"])
    onec = P.sbuf("g_one", [128, 1], F32)
    vop(lambda e: e.memset(onec[:], 1.0), [], ["g_one"])
    RC = min(T, 2048)
    NRC = T // RC
    arow = P.sbuf("g_arow", [2, RC], F32); brow = P.sbuf("g_brow", [2, RC], F32); tr = P.sbuf("g_tr", [2, RC], F32)
    cmask = P.sbuf("g_cmask", [2, RC], F32)
    vop(lambda e: e.memset(cmask[:], 1.0), [], ["g_cmask"])
    vop(lambda e: e.memset(cmask[:].rearrange("p (c i) -> p c i", i=64)[:, :, 0:1], 0.0), ["g_cmask"], ["g_cmask"])
    nega_ = P.sbuf("g_negal", [2, 1], F32)
    vop(lambda e: e.activation(nega_[:], alog[:], AF.Exp), ["g_alog"], ["g_negal"], eng=S)
    vop(lambda e: e.tensor_scalar(nega_[:], nega_[:], -1.0, None, ALU.mult), ["g_negal"], ["g_negal"])
    scr = nc.dram_tensor("g_scr", [2, 2, T], F32).ap()
    for rc in range(NRC):
        cs = slice(rc * RC, (rc + 1) * RC)
        P.dma("sync", arow[:], gin[1024:1026, cs], writes=["g_arow"])
        P.dma("sync", brow[:], gin[1026:1028, cs], writes=["g_brow"])
        vop(lambda e: e.activation(tr[:], arow[:], AF.Exp, bias=dtb[:, 0:1], scale=1.0), ["g_arow", "g_dtb"], ["g_tr"], eng=S)
        vop(lambda e: e.activation(tr[:], tr[:], AF.Ln, bias=onec[0:2, 0:1], scale=1.0), ["g_tr", "g_one"], ["g_tr"], eng=S)
        vop(lambda e: e.tensor_scalar(tr[:], tr[:], nega_[:, 0:1], None, ALU.mult), ["g_tr", "g_negal"], ["g_tr"])
        vop(lambda e: e.tensor_tensor_scan(arow[:], cmask[:], tr[:], 0.0, ALU.mult, ALU.add), ["g_cmask", "g_tr", "g_arow"], ["g_arow"])
        vop(lambda e: e.activation(brow[:], brow[:], AF.Sigmoid), ["g_brow"], ["g_brow"], eng=S)
        P.dma("sync", scr[0, :, cs], arow[:], reads=["g_arow"], writes=[("scr", 0, rc)])
        P.dma("sync", scr[1, :, cs], brow[:], reads=["g_brow"], writes=[("scr", 1, rc)])
    gcol = P.sbuf("g_gcol", [64, 2, NCH], F32); bcol = P.sbuf("g_bcol", [64, 2, NCH], F32)
    for h in range(2):
        P.dma("sync", gcol[:, h, :], scr[0, h].rearrange("(c i) -> i c", i=64), reads=[("scr", 0, r_) for r_ in range(NRC)], writes=[("gcol", h)], allow_slow_non_contiguous=True)
        P.dma("sync", bcol[:, h, :], scr[1, h].rearrange("(c i) -> i c", i=64), reads=[("scr", 1, r_) for r_ in range(NRC)], writes=[("bcol", h)], allow_slow_non_contiguous=True)
    S32 = [[P.sbuf("g_S32_%d_%d" % (h, i), [128, 128], F32) for i in range(2)] for h in range(2)]
    Sbf = [[P.sbuf("g_Sbf_%d_%d" % (h, i), [128, 128], BF16) for i in range(2)] for h in range(2)]
    for h in range(2):
        vop(lambda e, h=h: e.memset(S32[h][0][:], 0.0), [], [("S32", h, 0)])
        vop(lambda e, h=h: e.memset(Sbf[h][0][:], 0.0), [], [("Sbf", h, 0)])
    def mk(name, n, shape, dt=F32):
        return Pool(P, name, n, shape, dt)
    rawp = mk("g_raw", 3, [128, 515]); accp = mk("g_acc", 4, [128, 512]); sqp = mk("g_sq", 2, [128, 512], BF16); rnp = mk("g_rn", 2, [128, 512])
    f64p = mk("g_f64", 14, [64, 8, 64])
    b64p = mk("g_b64", 4, [64, 8, 64], BF16)
    bigp = mk("g_big", 5, [128, 512])
    hbfp = mk("g_hbf", 4, [128, 512], BF16)
    tokp = mk("g_tok", 3, [64, 8, 128], BF16)
    smallp = mk("g_small", 8, [64, 8])
    vnp = [mk("g_vn%d" % h, 2, [64, 128], BF16) for h in range(2)]
    p_qg = [mk("g_pqg%d" % h, 2, [128, 512], BF16) for h in range(2)]
    p_WT = [mk("g_pWT%d" % h, 2, [128, 512], BF16) for h in range(2)]
    p_z = [mk("g_pz%d" % h, 2, [128, 512]) for h in range(2)]
    p_QKm = [mk("g_pQK%d" % h, 2, [64, 8, 64], BF16) for h in range(2)]
    p_U = [mk("g_pU%d" % h, 2, [64, 8, 128]) for h in range(2)]
    p_Kd = [mk("g_pKd%d" % h, 2, [64, 8, 128], BF16) for h in range(2)]
    p_egl = [mk("g_pegl%d" % h, 2, [128, 8]) for h in range(2)]
    onp = mk("g_on", 2, [128, 512]); obp = mk("g_ob", 2, [128, 512], BF16)
    gpp = Pool(P, "g_gp", 3, [128, 512], F32, psum=True)
    trps = P.psum("g_trps", [128, 1024], BF16)
    chps = [P.psum("g_chps%d" % h, [128, 512], F32) for h in range(2)]
    ops_ = [P.psum("g_ops%d" % h, [128, 512], F32) for h in range(2)]

    def conv_silu(h, s, blk):
        raw, rn_ = rawp.next(); acc, an = accp.next()
        r0 = s * 256 + h * 128
        t0 = blk * 512
        if blk == 0:
            vop(lambda e: e.memset(raw[:, 0:3], 0.0), [], [rn_])
            P.dma("sync", raw[:, 3:515], gin[r0:r0 + 128, 0:512], writes=[rn_])
        else:
            P.dma("sync", raw[:], gin[r0:r0 + 128, t0 - 3:t0 + 512], writes=[rn_])
        vop(lambda e: e.tensor_scalar(acc[:], raw[:, 0:512], cw[:, s, h, 0:1], None, ALU.mult), [rn_, "g_cw"], [an])
        for j in range(1, 4):
            vop(lambda e, j=j: e.scalar_tensor_tensor(acc[:], raw[:, j:j + 512], cw[:, s, h, j:j + 1], acc[:], ALU.mult, ALU.add), [rn_, "g_cw", an], [an])
        vop(lambda e: e.activation(acc[:], acc[:], AF.Silu), [an], [an], eng=S)
        return acc, an

    def rnorm(acc, an, scale):
        sq, sn = sqp.next(); r, rn_ = rnp.next(); ps, pn = gpp.next()
        vop(lambda e: e.activation(sq[:], acc[:], AF.Square), [an], [sn], eng=S)
        P.op(T_, lambda e: e.matmul(ps[:], ones_bf[:], sq[:], start=True, stop=True), reads=["g_ones", sn], writes=[pn])
        vop(lambda e: e.activation(r[:], ps[:], AF.Sqrt, bias=epsc[:, 0:1], scale=1.0), [pn, "g_eps"], [rn_], eng=S)
        vop(lambda e: e.reciprocal(r[:], r[:]), [rn_], [rn_])
        if scale != 1.0:
            vop(lambda e: e.tensor_scalar(r[:], r[:], scale, None, ALU.mult), [rn_], [rn_])
        return r, rn_

    blkstate = {}
    def prep(h, blk):
        t0 = blk * 512
        c0 = blk * 8
        st = {}
        Gr, Grn = bigp.next()
        P.dma("sync", Gr[:], scr[0, h:h + 1, t0:t0 + 512].partition_broadcast(128), reads=[("scr", 0, t0 // RC)], writes=[Grn])
        Br, Brn = bigp.next()
        P.dma("sync", Br[0:64, :], scr[1, h:h + 1, t0:t0 + 512].partition_broadcast(64), reads=[("scr", 1, t0 // RC)], writes=[Brn])
        Gr3 = Gr[0:64, :].rearrange("p (c i) -> p c i", i=64)
        Br3 = Br[0:64, :].rearrange("p (c i) -> p c i", i=64)
        gcb = gcol[:, h, c0:c0 + 8]; bcb = bcol[:, h, c0:c0 + 8]
        gct, bct = ("gcol", h), ("bcol", h)
        eG, eGn = bigp.next()
        vop(lambda e: e.activation(eG[:], Gr[:], AF.Exp), [Grn], [eGn], eng=S)
        egl, egln = p_egl[h].next()
        vop(lambda e: e.tensor_copy(egl[:], eG[:].rearrange("p (c i) -> p c i", i=64)[:, :, 63]), [eGn], [egln])
        bg, bgn = smallp.next(); ed, edn = smallp.next()
        vop(lambda e: e.activation(bg[:], gcb, AF.Exp), [gct], [bgn], eng=S)
        vop(lambda e: e.tensor_tensor(bg[:], bg[:], bcb, ALU.mult), [bgn, bct], [bgn])
        vop(lambda e: e.tensor_tensor(ed[:], Gr3[:, :, 63], gcb, ALU.subtract), [Grn, gct], [edn])
        vop(lambda e: e.activation(ed[:], ed[:], AF.Exp), [edn], [edn], eng=S)
        acc, an = conv_silu(h, 0, blk)
        r, rn_ = rnorm(acc, an, 128.0 ** -0.5)
        qn, qnn = bigp.next()
        vop(lambda e, acc=acc, r=r: e.tensor_tensor(qn[:], acc[:], r[:], ALU.mult), [an, rn_], [qnn])
        qb, qbn = hbfp.next(); qg, qgn = p_qg[h].next()
        vop(lambda e: e.copy(qb[:], qn[:]), [qnn], [qbn], eng=S)
        vop(lambda e: e.tensor_tensor(qg[:], qn[:], eG[:], ALU.mult), [qnn, eGn], [qgn], eng=G)
        acc, an = conv_silu(h, 1, blk)
        r, rn_ = rnorm(acc, an, 1.0)
        kb, kbn = hbfp.next()
        vop(lambda e, acc=acc, r=r: e.tensor_tensor(kb[:], acc[:], r[:], ALU.mult), [an, rn_], [kbn])
        acc, an = conv_silu(h, 2, blk)
        vb, vbn = hbfp.next()
        vop(lambda e, acc=acc: e.copy(vb[:], acc[:]), [an], [vbn], eng=S)
        z, zn = p_z[h].next()
        P.dma("sync", z[:], gin[768 + h * 128:768 + (h + 1) * 128, t0:t0 + 512], writes=[zn])
        vop(lambda e: e.activation(z[:], z[:], AF.Silu), [zn], [zn], eng=S)
        KK, KKn = gpp.next(); QK, QKn = gpp.next()
        for c in range(8):
            cs = slice(c * 64, (c + 1) * 64)
            P.op(T_, lambda e, cs=cs: e.matmul(KK[0:64, cs], kb[:, cs], kb[:, cs], start=True, stop=True), reads=[kbn], writes=[KKn])
            P.op(T_, lambda e, cs=cs: e.matmul(QK[0:64, cs], kb[:, cs], qb[:, cs], start=True, stop=True), reads=[kbn, qbn], writes=[QKn])
        KK3 = KK[0:64, :].rearrange("p (c i) -> p c i", i=64)
        QK3 = QK[0:64, :].rearrange("p (c i) -> p c i", i=64)
        gcb_b = gcb.unsqueeze(2).to_broadcast([64, 8, 64])
        bcb_b = bcb.unsqueeze(2).to_broadcast([64, 8, 64])
        ES, ESn = f64p.next(); EA, EAn = f64p.next()
        vop(lambda e: e.tensor_tensor(ES[:], Gr3, negs[:], ALU.add), [Grn, "g_negs"], [ESn], eng=G)
        vop(lambda e: e.tensor_tensor(ES[:], ES[:], gcb_b, ALU.subtract), [ESn, gct], [ESn])
        vop(lambda e: e.activation(ES[:], ES[:], AF.Exp), [ESn], [ESn], eng=S)
        vop(lambda e: e.scalar_tensor_tensor(EA[:], Gr3, -1.0, negA[:], ALU.mult, ALU.add), [Grn, "g_nega"], [EAn])
        vop(lambda e: e.tensor_tensor(EA[:], EA[:], gcb_b, ALU.add), [EAn, gct], [EAn])
        vop(lambda e: e.activation(EA[:], EA[:], AF.Exp), [EAn], [EAn], eng=S)
        Pk, Pkn = f64p.next(); PTk, PTkn = f64p.next()
        vop(lambda e: e.scalar_tensor_tensor(Pk[:], KK3, -1.0, EA[:], ALU.mult, ALU.mult), [KKn, EAn], [Pkn])
        vop(lambda e: e.tensor_tensor(Pk[:], Pk[:], bcb_b, ALU.mult), [Pkn, bct], [Pkn], eng=G)
        vop(lambda e: e.scalar_tensor_tensor(PTk[:], KK3, -1.0, ES[:], ALU.mult, ALU.mult), [KKn, ESn], [PTkn])
        vop(lambda e: e.tensor_tensor(PTk[:], PTk[:], Br3, ALU.mult), [PTkn, Brn], [PTkn], eng=G)
        if dbg is not None and h == 0 and blk == 0:
            P.dma("sync", dbg["d_M"], Pk[:].rearrange("p c i -> p (c i)"), reads=[Pkn])
            P.dma("sync", dbg["d_ES"], ES[:].rearrange("p c i -> p (c i)"), reads=[ESn])
            P.dma("sync", dbg["d_EA"], EA[:].rearrange("p c i -> p (c i)"), reads=[EAn])
            P.dma("sync", dbg["d_MT"], PTk[:].rearrange("p c i -> p (c i)"), reads=[PTkn])
        QKm, QKmn = p_QKm[h].next()
        vop(lambda e: e.tensor_tensor(ES[:], ES[:], id8[:], ALU.add), [ESn, "g_id8"], [ESn], eng=G)
        vop(lambda e: e.tensor_tensor(QKm[:], QK3, ES[:], ALU.mult), [QKn, ESn], [QKmn])
        TT, TTn = f64p.next()
        vop(lambda e: e.tensor_tensor(TT[:], PTk[:], id8[:], ALU.add), [PTkn, "g_id8"], [TTn], eng=G)
        Pc, Pcn, PTc, PTcn, TTc, TTcn = Pk, Pkn, PTk, PTkn, TT, TTn
        for k in range(1, 6):
            Pp, Ppn = gpp.next()
            for c in range(8):
                P.op(T_, lambda e, c=c, Pp=Pp, PTc=PTc, Pc=Pc: e.matmul(Pp[0:64, c * 64:(c + 1) * 64], PTc[:, c, :], Pc[:, c, :], start=True, stop=True), reads=[PTcn, Pcn], writes=[Ppn])
            if k < 5:
                PTp, PTpn = gpp.next()
                for c in range(8):
                    P.op(T_, lambda e, c=c, PTp=PTp, PTc=PTc, Pc=Pc: e.matmul(PTp[0:64, c * 64:(c + 1) * 64], Pc[:, c, :], PTc[:, c, :], start=True, stop=True), reads=[PTcn, Pcn], writes=[PTpn])
            Pn_, Pnn = f64p.next()
            vop(lambda e, Pn_=Pn_, Pp=Pp: e.copy(Pn_[:].rearrange("p c i -> p (c i)"), Pp[0:64, :]), [Ppn], [Pnn], eng=S)
            if k < 5:
                PTn_, PTnn = f64p.next()
                vop(lambda e, PTn_=PTn_, PTp=PTp: e.tensor_copy(PTn_[:].rearrange("p c i -> p (c i)"), PTp[0:64, :]), [PTpn], [PTnn])
            Tu, Tun = gpp.next()
            for c in range(8):
                P.op(T_, lambda e, c=c, Tu=Tu, Pn_=Pn_, TTc=TTc: e.matmul(Tu[0:64, c * 64:(c + 1) * 64], Pn_[:, c, :], TTc[:, c, :], start=True, stop=True), reads=[Pnn, TTcn], writes=[Tun])
            TT2, TT2n = f64p.next()
            vop(lambda e, TT2=TT2, TTc=TTc, Tu=Tu: e.tensor_tensor(TT2[:].rearrange("p c i -> p (c i)"), TTc[:].rearrange("p c i -> p (c i)"), Tu[0:64, :], ALU.add), [TTcn, Tun], [TT2n])
            TTc, TTcn = TT2, TT2n
            Pc, Pcn = Pn_, Pnn
            if k < 5:
                PTc, PTcn = PTn_, PTnn
            if dbg is not None and h == 0 and blk == 0 and k == 1:
                P.dma("sync", dbg["d_P1"], Pc[:].rearrange("p c i -> p (c i)"), reads=[Pcn])
                P.dma("sync", dbg["d_T1"], TTc[:].rearrange("p c i -> p (c i)"), reads=[TTcn])
        TTb, TTbn = b64p.next()
        vop(lambda e, TTc=TTc: e.copy(TTb[:], TTc[:]), [TTcn], [TTbn], eng=S)
        Kw, Kwn = tokp.next(); Kd, Kdn = p_Kd[h].next(); Vb, Vbn = tokp.next()
        for c in range(8):
            P.op(T_, lambda e, c=c: e.transpose(trps[0:64, c * 128:(c + 1) * 128], kb[:, c * 64:(c + 1) * 64], identb[:]), reads=[kbn, "g_identb"], writes=["g_trps"])
        ktv = trps[0:64, :].rearrange("p (c d) -> p c d", d=128)
        vop(lambda e: e.tensor_tensor(Kw[:], ktv, bg[:].unsqueeze(2).to_broadcast([64, 8, 128]), ALU.mult), ["g_trps", bgn], [Kwn])
        vop(lambda e: e.tensor_tensor(Kd[:], ktv, ed[:].unsqueeze(2).to_broadcast([64, 8, 128]), ALU.mult), ["g_trps", edn], [Kdn])
        for c in range(8):
            P.op(T_, lambda e, c=c: e.transpose(trps[0:64, c * 128:(c + 1) * 128], vb[:, c * 64:(c + 1) * 64], identb[:]), reads=[vbn, "g_identb"], writes=["g_trps"])
        vop(lambda e: e.tensor_tensor(Vb[:], ktv, bcb.unsqueeze(2).to_broadcast([64, 8, 128]), ALU.mult), ["g_trps", bct], [Vbn])
        U, Un = p_U[h].next()
        for half in range(2):
            Ups, Upsn = gpp.next()
            for cc in range(4):
                c = half * 4 + cc
                P.op(T_, lambda e, c=c, cc=cc, Ups=Ups: e.matmul(Ups[0:64, cc * 128:(cc + 1) * 128], TTb[:, c, :], Vb[:, c, :], start=True, stop=True), reads=[TTbn, Vbn], writes=[Upsn])
            vop(lambda e, half=half, Ups=Ups: e.copy(U[:, half * 4:half * 4 + 4, :].rearrange("p c d -> p (c d)"), Ups[0:64, :]), [Upsn], [Un], eng=S)
        Wps, Wpsn = gpp.next()
        for c in range(8):
            P.op(T_, lambda e, c=c: e.matmul(Wps[:, c * 64:(c + 1) * 64], Kw[:, c, :], TTb[:, c, :], start=True, stop=True), reads=[Kwn, TTbn], writes=[Wpsn])
        WT, WTn = p_WT[h].next()
        vop(lambda e: e.tensor_copy(WT[:], Wps[:]), [Wpsn], [WTn])
        if dbg is not None and h == 0 and blk == 0:
            P.dma("sync", dbg["d_TT"], TTc[:].rearrange("p c i -> p (c i)"), reads=[TTcn])
            P.dma("sync", dbg["d_U"], U[:].rearrange("p c i -> p (c i)"), reads=[Un])
            P.dma("sync", dbg["d_Gr"], Gr[:], reads=[Grn])
            P.dma("sync", dbg["d_gcol"], gcol[:, 0, :], reads=[("gcol", 0)])
        st.update(dict(egl=egl, egln=egln, qg=qg, qgn=qgn, QKm=QKm, QKmn=QKmn, U=U, Un=Un, WT=WT, WTn=WTn, Kd=Kd, Kdn=Kdn, z=z, zn=zn))
        blkstate[(h, blk)] = st

    cur = [0, 0]
    def chunk_step(h, blk, c):
        st = blkstate[(h, blk)]
        ci = cur[h]; n = 1 - ci
        cs = slice(c * 64, (c + 1) * 64)
        ws = chps[h][0:64, 0:128]; wsn = ("chps", h, 0)
        ds = chps[h][:, 128:256]; dsn = ("chps", h, 1)
        vn, vnn = vnp[h].next()
        ob = ops_[h]; obn = ("ops", h)
        P.op(T_, lambda e: e.matmul(ws, st["WT"][:, cs], Sbf[h][ci][:], start=True, stop=True), reads=[st["WTn"], ("Sbf", h, ci)], writes=[wsn])
        vop(lambda e: e.tensor_tensor(vn[:], st["U"][:, c, :], ws, ALU.subtract), [st["Un"], wsn], [vnn])
        P.op(T_, lambda e: e.matmul(ob[:, cs], Sbf[h][ci][:], st["qg"][:, cs], start=True, stop=False), reads=[("Sbf", h, ci), st["qgn"]], writes=[obn])
        P.op(T_, lambda e: e.matmul(ob[:, cs], vn[:], st["QKm"][:, c, :], start=False, stop=True), reads=[vnn, st["QKmn"]], writes=[obn])
        P.op(T_, lambda e: e.matmul(ds, st["Kd"][:, c, :], vn[:], start=True, stop=True), reads=[st["Kdn"], vnn], writes=[dsn])
        vop(lambda e: e.scalar_tensor_tensor(Sbf[h][n][:], S32[h][ci][:], st["egl"][:, c:c + 1], ds, ALU.mult, ALU.add),
            [("S32", h, ci), st["egln"], dsn], [("Sbf", h, n)])
        vop(lambda e: e.scalar_tensor_tensor(S32[h][n][:], S32[h][ci][:], st["egl"][:, c:c + 1], ds, ALU.mult, ALU.add),
            [("S32", h, ci), st["egln"], dsn], [("S32", h, n)])
        cur[h] = n

    def finish(h, blk):
        st = blkstate[(h, blk)]
        ob = ops_[h]; obn = ("ops", h)
        sq, sn = sqp.next(); r, rn_ = rnp.next(); on, onn = onp.next(); o16, o16n = obp.next(); ps, pn = gpp.next()
        vop(lambda e: e.activation(sq[:], ob[:], AF.Square), [obn], [sn], eng=S)
        P.op(T_, lambda e: e.matmul(ps[:], ones_bf[:], sq[:], start=True, stop=True), reads=["g_ones", sn], writes=[pn])
        vop(lambda e: e.activation(r[:], ps[:], AF.Sqrt, bias=epsc[:, 0:1], scale=1.0 / 128), [pn, "g_eps"], [rn_], eng=S)
        vop(lambda e: e.reciprocal(r[:], r[:]), [rn_], [rn_])
        vop(lambda e: e.scalar_tensor_tensor(on[:], ob[:], nw[:, 0:1], r[:], ALU.mult, ALU.mult), [obn, "g_nw", rn_], [onn])
        vop(lambda e: e.tensor_tensor(o16[:], on[:], st["z"][:], ALU.mult), [onn, st["zn"]], [o16n], eng=G)
        P.dma("sync", oT[h * 128:(h + 1) * 128, blk * 512:(blk + 1) * 512], o16[:], reads=[o16n])

    for h in range(2):
        prep(h, 0)
    for blk in range(NB):
        if blk + 1 < NB:
            for h in range(2):
                prep(h, blk + 1)
        for c in range(8):
            for h in range(2):
                chunk_step(h, blk, c)
        for h in range(2):
            finish(h, blk)


EPS = 1e-6

def vop(P, fn, reads, writes, eng=V):
    P.op(eng, fn, reads=reads, writes=writes)

class Norm:
    def __init__(self, P, nfeat=4096, nxc=3, nsq=2, nr=2):
        self.P = P
        self.ones = P.sbuf("n_ones", [128, 128], BF16)
        vop(P, lambda e: e.memset(self.ones[:], 1.0), [], ["n_ones"])
        self.epsc = P.sbuf("n_eps", [128, 1], F32)
        vop(P, lambda e: e.memset(self.epsc[:], EPS), [], ["n_eps"])
        self.xc = Pool(P, "n_xc", nxc, [128, 4, 512], F32)
        self.xc2 = Pool(P, "n_xc2", 2, [128, 4, 512], F32)
        self.sq = Pool(P, "n_sq", nsq, [128, 4, 512], BF16)
        self.ps = Pool(P, "n_ps", 1, [128, 512], F32, psum=True)
        self.r = Pool(P, "n_r", nr, [128, 512], F32)
        self.nfeat = nfeat

    def rstd(self, src, tb, KT, src_tok=None):
        P = self.P
        ps, pn = self.ps.next()
        sv = src.rearrange("(kt p) t -> p kt t", p=128)
        for c in range(KT // 4):
            xc, xn = self.xc.next(); sq, sn = self.sq.next()
            P.dma("sync", xc[:], sv[:, c * 4:c * 4 + 4, tb * 512:(tb + 1) * 512], reads=([(src_tok, c * 4 + k_, tb) for k_ in range(4)] if src_tok else []), writes=[xn])
            vop(P, lambda e, xc=xc, sq=sq: e.activation(sq[:], xc[:], AF.Square), [xn], [sn], eng=S)
            for k in range(4):
                P.op(T_, lambda e, ps=ps, sq=sq, k=k, c=c: e.matmul(ps[:], self.ones[:], sq[:, k, :], start=(c == 0 and k == 0), stop=(c == KT // 4 - 1 and k == 3)),
                     reads=["n_ones", sn], writes=[pn])
        return self.finish(ps, pn, KT * 128)

    def finish(self, ps, pn, n):
        P = self.P
        r, rn = self.r.next()
        vop(P, lambda e: e.activation(r[:], ps[:], AF.Sqrt, bias=self.epsc[:, 0:1], scale=1.0 / n), [pn, "n_eps"], [rn], eng=S)
        vop(P, lambda e: e.reciprocal(r[:], r[:]), [rn], [rn])
        return r, rn

    def to_bf16(self, src, nw, nwn, dst, dstn, T):
        P = self.P
        sv = src.rearrange("(kt p) t -> p kt t", p=128)
        for tb in range(T // 512):
            r, rn = self.rstd(src, tb, 32)
            for c in range(8):
                xc, xn = self.xc.next()
                P.dma("sync", xc[:], sv[:, c * 4:c * 4 + 4, tb * 512:(tb + 1) * 512], writes=[xn])
                for k in range(4):
                    kt = c * 4 + k
                    vop(P, lambda e, xc=xc, k=k, kt=kt, r=r, tb=tb: e.scalar_tensor_tensor(dst[:, kt, tb * 512:(tb + 1) * 512], xc[:, k, :], nw[:, kt:kt + 1], r[:], ALU.mult, ALU.mult),
                        [xn, nwn, rn], [dstn])

    def resid(self, raw, x, nw, nwn, out, T, raw_tok=None):
        P = self.P
        rv = raw.rearrange("(kt p) t -> p kt t", p=128)
        xv = x.rearrange("(kt p) t -> p kt t", p=128)
        ov = out.rearrange("(kt p) t -> p kt t", p=128)
        for tb in range(T // 512):
            r, rn = self.rstd(raw, tb, 32, src_tok=raw_tok)
            for c in range(8):
                xc, xn = self.xc.next(); x2, x2n = self.xc2.next()
                P.dma("sync", xc[:], rv[:, c * 4:c * 4 + 4, tb * 512:(tb + 1) * 512], reads=([(raw_tok, c * 4 + k_, tb) for k_ in range(4)] if raw_tok else []), writes=[xn])
                P.dma("sync", x2[:], xv[:, c * 4:c * 4 + 4, tb * 512:(tb + 1) * 512], writes=[x2n])
                for k in range(4):
                    kt = c * 4 + k
                    vop(P, lambda e, xc=xc, k=k, kt=kt, r=r: e.scalar_tensor_tensor(xc[:, k, :], xc[:, k, :], nw[:, kt:kt + 1], r[:], ALU.mult, ALU.mult), [xn, nwn, rn], [xn])
                vop(P, lambda e, xc=xc, x2=x2: e.tensor_tensor(x2[:], x2[:], xc[:], ALU.add), [xn, x2n], [x2n], eng=G)
                P.dma("sync", ov[:, c * 4:c * 4 + 4, tb * 512:(tb + 1) * 512], x2[:], reads=[x2n])

def ld_const(P, name, shape, src, dt=F32, eng="sync"):
    t = P.sbuf("sb_" + name, shape, dt)
    P.dma(eng, t[:], src, writes=[name])
    return t

NPROJ = 18560
def build_A():
    P = Prog()
    xT = P.dram("xT", [4096, 1024], F32, "ExternalInput")
    nwd = P.dram("nw", [128, 32], F32, "ExternalInput")
    w = P.dram("w", [4096, NPROJ], F32, "ExternalInput")
    out = P.dram("projT", [NPROJ, 1024], F32, "ExternalOutput")
    nw = ld_const(P, "nw", [128, 32], nwd)
    N = Norm(P)
    h = P.sbuf("hT", [128, 32, 1024], BF16)
    N.to_bf16(xT, nw, "nw", h, "hT", 1024)
    Gm = Gemm(P)
    Gm.run(h, "hT", 32, w, NPROJ, Gm.evac_to_dram(out))
    return P.build()

def build_C1a():
    P = Prog()
    yT = P.dram("yT", [2048, 1024], BF16, "ExternalInput")
    oT = P.dram("oT", [2048, 1024], BF16, "ExternalInput")
    gs = P.dram("gsT", [4096, 1024], F32, "ExternalInput")
    gd = P.dram("gdT", [4096, 1024], F32, "ExternalInput")
    wglu = P.dram("w_glu", [2048, 8192], F32, "ExternalInput")
    wgdn = P.dram("w_gdn", [2048, 4096], F32, "ExternalInput")
    mT = P.dram("mT", [4096, 1024], BF16, "ExternalOutput")
    y = P.sbuf("y_sb", [128, 16, 1024], BF16)
    o = P.sbuf("o_sb", [128, 16, 1024], BF16)
    P.dma("sync", y[:], yT.rearrange("(kt p) t -> p kt t", p=128), writes=["y_sb"])
    P.dma("sync", o[:], oT.rearrange("(kt p) t -> p kt t", p=128), writes=["o_sb"])
    Gm = Gemm(P, wbytes=8192, npsum=7)
    gsp = Pool(P, "c_gs", 2, [128, 512], F32); gdp = Pool(P, "c_gd", 2, [128, 512], F32)
    t1p = Pool(P, "c_t1", 2, [128, 512], F32); t2p = Pool(P, "c_t2", 2, [128, 512], F32)
    sbp = Pool(P, "c_sb", 2, [128, 512], F32); mp = Pool(P, "c_m", 3, [128, 512], BF16)
    for nt in range(32):
        stash = {}
        def ev_ab(i, tb, ps, ptok, stash=stash):
            stash[(i, tb)] = (ps, ptok)
        Gm.run(y, "y_sb", 16, wglu, 256, ev_ab, col_groups=[[(nt * 128, 128), (4096 + nt * 128, 128)]])
        def ev_g(i, tb, G_, Gn, stash=stash, nt=nt):
            A_, An = stash[(0, tb)]; B_, Bn = stash[(1, tb)]
            gst, gsn = gsp.next(); gdt, gdn = gdp.next(); t1, t1n = t1p.next(); t2, t2n = t2p.next(); sb, sbn = sbp.next(); m, mn = mp.next()
            rs = slice(nt * 128, (nt + 1) * 128); cs = slice(tb * 512, (tb + 1) * 512)
            P.dma("sync", gst[:], gs[rs, cs], writes=[gsn])
            P.dma("sync", gdt[:], gd[rs, cs], writes=[gdn])
            vop(P, lambda e: e.activation(sb[:], B_[:], AF.Sigmoid), [Bn], [sbn], eng=S)
            vop(P, lambda e: e.tensor_tensor(t1[:], A_[:], sb[:], ALU.mult), [An, sbn], [t1n])
            vop(P, lambda e: e.activation(gst[:], gst[:], AF.Sigmoid), [gsn], [gsn], eng=S)
            vop(P, lambda e: e.activation(gdt[:], gdt[:], AF.Sigmoid), [gdn], [gdn], eng=S)
            vop(P, lambda e: e.tensor_tensor(t1[:], t1[:], gst[:], ALU.mult), [t1n, gsn], [t1n])
            vop(P, lambda e: e.tensor_tensor(t2[:], G_[:], gdt[:], ALU.mult), [Gn, gdn], [t2n])
            vop(P, lambda e: e.tensor_tensor(m[:], t1[:], t2[:], ALU.add), [t1n, t2n], [mn])
            P.dma("sync", mT[rs, cs], m[:], reads=[mn])
        Gm.run(o, "o_sb", 16, wgdn, 128, ev_g, col_groups=[[(nt * 128, 128)]])
    return P.build()

def build_C1b():
    P = Prog()
    mT = P.dram("mT", [4096, 1024], BF16, "ExternalInput")
    xT = P.dram("xT", [4096, 1024], F32, "ExternalInput")
    w = P.dram("w_out", [4096, 4096], F32, "ExternalInput")
    nwd = P.dram("nw", [128, 32], F32, "ExternalInput")
    x1 = P.dram("x1T", [4096, 1024], F32, "ExternalOutput")
    raw = P.nc.dram_tensor("c_raw", [4096, 1024], F32).ap()
    nw = ld_const(P, "nw", [128, 32], nwd)
    m = P.sbuf("m_sb", [128, 32, 1024], BF16)
    P.dma("sync", m[:], mT.rearrange("(kt p) t -> p kt t", p=128), writes=["m_sb"])
    Gm = Gemm(P)
    Gm.run(m, "m_sb", 32, w, 4096, Gm.evac_to_dram(raw, tok="c_raw"))
    N = Norm(P)
    N.resid(raw, xT, nw, "nw", x1, 1024, raw_tok="c_raw")
    return P.build()

def build_C2(FF=16384):
    P = Prog()
    nq = FF // 4096
    x1 = P.dram("x1T", [4096, 1024], F32, "ExternalInput")
    w1 = P.dram("w1", [4096, FF], F32, "ExternalInput")
    w2 = P.dram("w2", [FF, 4096], F32, "ExternalInput")
    nw1d = P.dram("nw1", [128, 32], F32, "ExternalInput")
    nw2d = P.dram("nw2", [128, 32], F32, "ExternalInput")
    x2 = P.dram("x2T", [4096, 1024], F32, "ExternalOutput")
    parts = [P.nc.dram_tensor("f_part%d" % q, [4096, 1024], F32).ap() for q in range(nq)]
    raw2 = P.nc.dram_tensor("f_raw2", [4096, 1024], F32).ap()
    nw1 = ld_const(P, "nw1", [128, 32], nw1d)
    nw2 = ld_const(P, "nw2", [128, 32], nw2d)
    N = Norm(P, nxc=2, nsq=1, nr=1)
    h = P.sbuf("hT", [128, 32, 1024], BF16)
    hq = P.sbuf("hqT", [128, 32, 1024], BF16)
    N.to_bf16(x1, nw1, "nw1", h, "hT", 1024)
    Gm = Gemm(P, nwb=2, nstage=2)
    relp = Pool(P, "f_rel", 2, [128, 512], F32)
    for q in range(nq):
        def ev1(nt, tb, ps, ptok):
            r, rn = relp.next()
            vop(P, lambda e: e.activation(r[:], ps[:], AF.Relu), [ptok], [rn], eng=S)
            vop(P, lambda e: e.tensor_tensor(hq[:, nt, tb * 512:(tb + 1) * 512], r[:], r[:], ALU.mult), [rn], [("hq", nt, tb)])
        Gm.run(h, "hT", 32, w1[:, q * 4096:(q + 1) * 4096], 4096, ev1)
        Gm.run(hq, (lambda kt, tb: ("hq", kt, tb)), 32, w2[q * 4096:(q + 1) * 4096, :], 4096, Gm.evac_to_dram(parts[q], tok=("part", q)))
    for tb in range(2):
        for c in range(8):
            acc, accn = N.xc2.next()
            sl = (slice(None), slice(c * 4, c * 4 + 4), slice(tb * 512, (tb + 1) * 512))
            P.dma("sync", acc[:], parts[0].rearrange("(kt p) t -> p kt t", p=128)[sl], reads=[(("part", 0), c * 4 + k_, tb) for k_ in range(4)], writes=[accn])
            for q in range(1, nq):
                xc, xn = N.xc.next()
                P.dma("sync", xc[:], parts[q].rearrange("(kt p) t -> p kt t", p=128)[sl], reads=[(("part", q), c * 4 + k_, tb) for k_ in range(4)], writes=[xn])
                vop(P, lambda e, acc=acc, xc=xc: e.tensor_tensor(acc[:], acc[:], xc[:], ALU.add), [accn, xn], [accn])
            P.dma("sync", raw2.rearrange("(kt p) t -> p kt t", p=128)[sl], acc[:], reads=[accn], writes=[("f_raw2", c * 4 + k_, tb) for k_ in range(4)])
    N.resid(raw2, x1, nw2, "nw2", x2, 1024, raw_tok="f_raw2")
    return P.build()


def build_BS():
    P = Prog()
    uT = P.dram("uT", [256, 8192], F32, "ExternalInput")
    yT = P.dram("yT", [256, 8192], BF16, "ExternalOutput")
    z16 = np.zeros((128, 64), np.float32)
    shapes = {"ssmA": [128, 5, 2, 64], "ssmB": [128, 3, 16], "ssmcc": [128, 2, 16, 16], "ssmd": [128, 2]}
    shapes.update({k: list(v.shape) for k, v in ssm_consts().items()})
    D = {k: P.dram(k, s, F32, "ExternalInput") for k, s in shapes.items()}
    emit_ssm(P, uT, yT, D, NCH=16)
    return P.build()


def build_BG():
    P = Prog()
    gin = P.dram("gin", [1028, 8192], F32, "ExternalInput")
    oT = P.dram("oT", [256, 8192], BF16, "ExternalOutput")
    shapes = {"g_cw": [128, 3, 2, 4], "g_alog": [2, 1], "g_dtb": [2, 1], "g_nw": [128, 1], "g_ident": [128, 128]}
    shapes.update({k: list(v.shape) for k, v in gdn2_consts().items()})
    D = {k: P.dram(k, s, F32, "ExternalInput") for k, s in shapes.items()}
    emit_gdn2(P, gin, oT, D, T=8192)
    return P.build()


def pack_w_in(w_in):
    ab = np.zeros((4096, 128), np.float32)
    for j in range(8):
        ab[:, 4 * j + 0] = w_in[:, 8192 + 2 * j]
        ab[:, 4 * j + 1] = w_in[:, 8192 + 2 * j + 1]
        ab[:, 4 * j + 2] = w_in[:, 8208 + 2 * j]
        ab[:, 4 * j + 3] = w_in[:, 8208 + 2 * j + 1]
    return np.ascontiguousarray(np.concatenate([w_in[:, 0:8192], w_in[:, 8224:], ab], axis=1))


_PROGS = {}


def _prog(name, builder):
    if name not in _PROGS:
        _PROGS[name] = builder()
    return _PROGS[name]


def _run(name, builder, ins):
    nc = _prog(name, builder)
    return run_bass_kernel_spmd(nc, ins, core_ids=list(range(8))).results


def _col(v):
    return np.ascontiguousarray(np.asarray(v, np.float32).reshape(32, 128).T)


def kernel(**inp):
    inp = {k: np.asarray(v) for k, v in inp.items()}
    C = np.ascontiguousarray
    x = inp["x"][0]
    xs = [C(x[c * 1024:(c + 1) * 1024].T) for c in range(8)]
    consts = ssm_consts()
    gconsts = gdn2_consts()
    for l in range(2):
        wp = pack_w_in(inp["w_in"][l])
        nw = _col(inp["mix_pre_w"][l])
        rA = _run("A", build_A, [{"xT": xs[c], "nw": nw, "w": wp} for c in range(8)])
        del wp
        projT = np.concatenate([r["projT"] for r in rA], axis=1)
        gates = [(C(r["projT"][10240:14336]), C(r["projT"][14336:18432])) for r in rA]
        del rA
        ins = []
        for j in range(8):
            d = ssm_host_inputs(j, inp["ssm_a_re"][l], inp["ssm_a_im"][l], inp["ssm_log_dt"][l], inp["ssm_b_re"][l],
                                inp["ssm_b_im"][l], inp["ssm_c_re"][l], inp["ssm_c_im"][l], inp["ssm_d"][l])
            d.update(consts)
            d["uT"] = C(projT[256 * j:256 * (j + 1)])
            ins.append(d)
        rS = _run("BS", build_BS, ins)
        yfull = np.concatenate([r["yT"] for r in rS], axis=0)
        ins = []
        for j in range(8):
            d = gdn_host_params(j, inp["conv_w"][l], inp["gdn_a_log"][l], inp["gdn_dt_bias"][l], inp["gdn_norm_w"][l])
            d.update(gconsts)
            d["gin"] = C(np.concatenate([projT[2048 + 256 * j:2048 + 256 * (j + 1)], projT[4096 + 256 * j:4096 + 256 * (j + 1)],
                                         projT[6144 + 256 * j:6144 + 256 * (j + 1)], projT[8192 + 256 * j:8192 + 256 * (j + 1)],
                                         projT[18432 + 4 * j:18432 + 4 * j + 4]], axis=0))
            ins.append(d)
        del projT
        rG = _run("BG", build_BG, ins)
        ofull = np.concatenate([r["oT"] for r in rG], axis=0)
        wglu = C(inp["w_glu"][l]); wgdn = C(inp["w_gdn_out"][l])
        rM = _run("C1a", build_C1a, [{"yT": C(yfull[:, c * 1024:(c + 1) * 1024]), "oT": C(ofull[:, c * 1024:(c + 1) * 1024]),
                                       "gsT": gates[c][0], "gdT": gates[c][1], "w_glu": wglu, "w_gdn": wgdn} for c in range(8)])
        del gates
        wout = C(inp["w_out"][l]); nwp = _col(inp["mix_post_w"][l])
        r1 = _run("C1b", build_C1b, [{"mT": rM[c]["mT"], "xT": xs[c], "w_out": wout, "nw": nwp} for c in range(8)])
        w1 = C(inp["w_ff1"][l]); w2 = C(inp["w_ff2"][l])
        n1 = _col(inp["ffn_pre_w"][l]); n2 = _col(inp["ffn_post_w"][l])
        r2 = _run("C2", build_C2, [{"x1T": r1[c]["x1T"], "w1": w1, "w2": w2, "nw1": n1, "nw2": n2} for c in range(8)])
        xs = [r2[c]["x2T"] for c in range(8)]
    out = np.concatenate([xc.T for xc in xs], axis=0)
    return np.ascontiguousarray(out.reshape(1, 8192, 4096).astype(np.float32))
```

```python
import numpy as np
from contextlib import ExitStack
V, S, G, T_ = "vector", "scalar", "gpsimd", "tensor"
import concourse.bass as bass
import concourse.mybir as mybir
from concourse.bass_utils import run_bass_kernel_spmd

F32 = mybir.dt.float32
BF16 = mybir.dt.bfloat16
ALU = mybir.AluOpType
AF = mybir.ActivationFunctionType
AX = mybir.AxisListType

SAME_ENGINE_SYNC = True
NDMASEM = 48
NO_SELF_SYNC = ("tensor",)
COMPUTE = ("tensor", "vector", "scalar", "gpsimd")
ENGS = ("tensor", "vector", "scalar", "gpsimd", "sync")


class Prog:
    def __init__(self, name="k"):
        self.nc = bass.Bass("TRN2", target_bir_lowering=False)
        self.es = ExitStack()
        self.ops = {e: [] for e in ENGS}
        self.last_w = {}
        self.readers = {}
        self.ndma = 0
        self.dma_ops = []
        self.waited = {e: {x: -1 for x in COMPUTE} for e in ENGS}
        self.dma_waited = {e: set() for e in ENGS}
        self.nbuf = 0

    def dram(self, name, shape, dt, kind):
        return self.nc.dram_tensor(name, list(shape), dt, kind=kind).ap()

    def sbuf(self, name, shape, dt):
        return self.es.enter_context(self.nc.sbuf_tensor(name, list(shape), dt))

    def psum(self, name, shape, dt=F32):
        return self.es.enter_context(self.nc.psum_tensor(name, list(shape), dt))

    def _dep(self, eng, rec, dep):
        kind, deng, didx = dep
        if kind == "c":
            if deng == eng and (eng in NO_SELF_SYNC or not SAME_ENGINE_SYNC):
                return
            if self.waited[eng][deng] >= didx:
                return
            self.waited[eng][deng] = didx
            rec["waits"].append(dep)
            self.ops[deng][didx]["inc"] = True
        else:
            if didx in self.dma_waited[eng]:
                return
            self.dma_waited[eng].add(didx)
            rec["waits"].append(dep)

    def op(self, eng, fn, reads=(), writes=(), dma=False):
        rec = {"fn": fn, "waits": [], "inc": False, "dma": None}
        idx = len(self.ops[eng])
        for b in reads:
            lw = self.last_w.get(b)
            if lw is not None:
                self._dep(eng, rec, lw)
        for b in writes:
            lw = self.last_w.get(b)
            if lw is not None:
                self._dep(eng, rec, lw)
            for r in self.readers.get(b, ()):
                self._dep(eng, rec, r)
        if dma:
            d = self.ndma
            self.ndma += 1
            rec["dma"] = d
            if d >= NDMASEM:
                self._dep(eng, rec, ("d", None, d - NDMASEM))
            me = ("d", eng, d)
            self.dma_ops.append((eng, idx))
        else:
            me = ("c", eng, idx)
        for b in reads:
            self.readers.setdefault(b, []).append(me)
        for b in writes:
            self.last_w[b] = me
            self.readers[b] = []
        self.ops[eng].append(rec)
        return me

    def dma(self, eng, out, in_, reads=(), writes=(), **kw):
        return self.op(eng, lambda e: e.dma_start(out=out, in_=in_, **kw), reads, writes, dma=True)

    def build(self):
        nc = self.nc
        for eng in ("sync", "gpsimd", "scalar"):
            mine = [d for (e, i) in self.dma_ops for d in [self.ops[e][i]["dma"]] if e == eng]
            if not mine:
                continue
            rec = {"fn": None, "waits": [], "inc": False, "dma": None}
            for d in mine:
                if d not in self.dma_waited[eng]:
                    rec["waits"].append(("d", None, d))
            self.ops[eng].append(rec)
        dsems = [self.es.enter_context(nc.semaphore("d%d" % i)) for i in range(NDMASEM)]
        EPOCH = 8192
        sems = {}
        for e in COMPUTE:
            c = 0
            for rec in self.ops[e]:
                if rec["inc"]:
                    rec["ticket"] = (c // EPOCH, c % EPOCH + 1)
                    c += 1
            sems[e] = [self.es.enter_context(nc.semaphore("s_%s%d" % (e, k))) for k in range(max(1, (c + EPOCH - 1) // EPOCH))]
        ops = self.ops

        def replay(ename, eh):
            for rec in ops[ename]:
                for (kind, deng, didx) in rec["waits"]:
                    if kind == "c":
                        ep, val = ops[deng][didx]["ticket"]
                        eh.wait_ge(sems[deng][ep], val)
                    else:
                        eh.wait_ge(dsems[didx % NDMASEM], 16 * (didx // NDMASEM + 1))
                if rec["fn"] is None:
                    continue
                ins = rec["fn"](eh)
                if rec["dma"] is not None:
                    ins.then_inc(dsems[rec["dma"] % NDMASEM], 16)
                elif rec["inc"]:
                    ins.then_inc(sems[ename][rec["ticket"][0]], 1)

        with nc.Block() as block:
            @block.tensor
            def _(t):
                replay("tensor", t)

            @block.vector
            def _(v):
                replay("vector", v)

            @block.scalar
            def _(s):
                replay("scalar", s)

            @block.gpsimd
            def _(g):
                replay("gpsimd", g)

            @block.sync
            def _(s):
                replay("sync", s)
        self.es.close()
        return nc


class Pool:
    def __init__(self, P, name, n, shape, dt, psum=False):
        self.bufs = [(P.psum if psum else P.sbuf)("%s%d" % (name, i), shape, dt) for i in range(n)]
        self.name = name
        self.i = 0

    def next(self):
        k = self.i % len(self.bufs)
        self.i += 1
        return self.bufs[k], (self.name, k)


class Gemm:
    def __init__(self, P, wbytes=16384, npsum=6, nstage=4, nwb=3):
        self.P = P
        self.wel = wbytes // 2
        self.wb = Pool(P, "wb", nwb, [128, self.wel], BF16)
        self.ps = Pool(P, "ps", npsum, [128, 512], F32, psum=True)
        self.stage = Pool(P, "stg", nstage, [128, 512], F32)
        self.evq = 0

    def run(self, actT, act_tok, KT, W, ncols, evac, ntb=2, col_groups=None):
        P = self.P
        gc = max(128, (self.wel // KT) // 128 * 128)
        gc = min(gc, 512)
        if col_groups is None:
            col_groups = []
            c = 0
            while c < ncols:
                w = min(gc, ncols - c)
                col_groups.append([(c, w)])
                c += w
        Wv = W.rearrange("(kt p) n -> p kt n", p=128)
        nt_idx = 0
        for grp in col_groups:
            wbuf, wtok = self.wb.next()
            tot = sum(w for _, w in grp)
            wv = wbuf[:, 0:KT * tot].rearrange("p (kt n) -> p kt n", kt=KT)
            off = 0
            for (c0, w) in grp:
                P.dma("gpsimd", wv[:, :, off:off + w], Wv[:, :, c0:c0 + w], writes=[wtok])
                off += w
            for j in range(tot // 128):
                for tb in range(ntb):
                    ps, ptok = self.ps.next()
                    for kt in range(KT):
                        P.op("tensor", (lambda e, ps=ps, kt=kt, j=j, tb=tb, wv=wv: e.matmul(
                            ps[:], wv[:, kt, j * 128:(j + 1) * 128], actT[:, kt, tb * 512:(tb + 1) * 512],
                            start=(kt == 0), stop=(kt == KT - 1))),
                            reads=[wtok, (act_tok(kt, tb) if callable(act_tok) else act_tok)], writes=[ptok])
                    evac(nt_idx, tb, ps, ptok)
                nt_idx += 1

    def evac_to_dram(self, outT, tok=None):
        P = self.P
        def ev(nt, tb, ps, ptok):
            st, stok = self.stage.next()
            eng = "vector" if (self.evq % 2 == 0) else "scalar"
            self.evq += 1
            if eng == "vector":
                P.op("vector", lambda e: e.tensor_copy(st[:], ps[:]), reads=[ptok], writes=[stok])
            else:
                P.op("scalar", lambda e: e.copy(st[:], ps[:]), reads=[ptok], writes=[stok])
            P.dma("sync", outT[nt * 128:(nt + 1) * 128, tb * 512:(tb + 1) * 512], st[:], reads=[stok], writes=([(tok, nt, tb)] if tok else []))
        return ev

import math
PI = math.pi

def ssm_host_inputs(j, a_re, a_im, log_dt, b_re, b_im, c_re, c_im, d_skip):
    G0 = 16 * j
    gs = np.arange(G0, G0 + 16)
    ssmA = np.zeros((128, 5, 2, 64), np.float32)
    for ct in range(2):
        for gp in range(8):
            g = G0 + ct * 8 + gp
            sl = slice(gp * 16, gp * 16 + 16)
            ssmA[sl, 0, ct, :] = a_re[g][None, :]
            ssmA[sl, 1, ct, :] = a_im[g][None, :]
            ssmA[sl, 2, ct, :] = log_dt[g]
            ssmA[sl, 3, ct, :] = b_re[g].T
            ssmA[sl, 4, ct, :] = b_im[g].T
    ssmB = np.zeros((128, 3, 16), np.float32)
    ssmB[:, 0, :] = np.concatenate([a_re[gs].T, a_re[gs].T], 0)
    ssmB[:, 1, :] = np.concatenate([a_im[gs].T, a_im[gs].T], 0)
    ssmB[:, 2, :] = log_dt[gs][None, :]
    cc = np.zeros((128, 2, 16, 16), np.float32)
    cre = np.transpose(c_re[gs], (2, 0, 1))
    cim = np.transpose(c_im[gs], (2, 0, 1))
    cc[0:64, 0] = cre; cc[64:128, 0] = cim
    cc[0:64, 1] = cim; cc[64:128, 1] = cre
    dsk = np.ascontiguousarray(d_skip[G0 * 16:G0 * 16 + 256].reshape(2, 128).T)
    return {"ssmA": ssmA, "ssmB": ssmB, "ssmcc": cc, "ssmd": dsk}

def ssm_consts():
    iota = np.tile(np.arange(512, dtype=np.float32)[None, :], (128, 1))
    ident = np.eye(128, dtype=np.float32)
    Jm = np.zeros((128, 128), np.float32)
    for n in range(64):
        Jm[64 + n, n] = -1.0
        Jm[n, 64 + n] = 1.0
    sgn = np.ones((128, 2), np.float32); sgn[64:, 0] = -1.0; sgn[:, 1] = -1.0
    mask = np.zeros((128, 16), np.float32)
    for p in range(128):
        mask[p, p // 16] = 1.0
        mask[p, 8 + p // 16] = -1.0
    return {"c_iota": iota, "c_ident": ident, "c_J": Jm, "c_sgn": sgn, "c_mask": mask}

def emit_ssm(P, uT, yT, D, NCH=16):
    V, S, G, T_ = "vector", "scalar", "gpsimd", "tensor"
    def ld(name, shape, src, dt=F32):
        t = P.sbuf("sb_" + name, shape, dt)
        P.dma("sync", t[:], src, writes=[name])
        return t
    sA = ld("ssmA", [128, 5, 2, 64], D["ssmA"])
    sB = ld("ssmB", [128, 3, 16], D["ssmB"])
    cc = ld("ssmcc", [128, 2, 16, 16], D["ssmcc"])
    dsk = ld("ssmd", [128, 2], D["ssmd"])
    iota = ld("c_iota", [128, 512], D["c_iota"])
    ident = ld("c_ident", [128, 128], D["c_ident"])
    Jm = ld("c_J", [128, 128], D["c_J"])
    sgn = ld("c_sgn", [128, 2], D["c_sgn"])
    mask = ld("c_mask", [128, 16], D["c_mask"])
    negpi = P.sbuf("negpi", [128, 1], F32)
    P.op(V, lambda e: e.memset(negpi[:], -PI), writes=["negpi"])

    cnt = [0]
    def tmp(shape, dt=F32):
        cnt[0] += 1
        n = "tmp%d" % cnt[0]
        return P.sbuf(n, shape, dt), n
    def vop(fn, reads, writes, eng=V):
        P.op(eng, fn, reads=reads, writes=writes)
    I32 = mybir.dt.int32
    C1 = 6.28125
    C2 = 2 * PI - C1
    sc_tmp = {}
    def sincos_into(x_ap, xn, s_ap, sn, c_ap, cn, shape):
        key = tuple(shape)
        if key not in sc_tmp:
            sc_tmp[key] = (tmp(shape), tmp(shape), tmp(shape), tmp(shape, I32), tmp(shape))
        (y, yn), (kf, kfn), (r, rn), (ki, kin), (m, mn) = sc_tmp[key]
        vop(lambda e: e.tensor_scalar(y[:], x_ap, PI, None, ALU.add), [xn], [yn])
        vop(lambda e: e.tensor_scalar(kf[:], y[:], 1.0 / (2 * PI), None, ALU.mult), [yn], [kfn])
        vop(lambda e: e.tensor_copy(ki[:], kf[:]), [kfn], [kin])
        vop(lambda e: e.tensor_copy(kf[:], ki[:]), [kin], [kfn])
        vop(lambda e: e.scalar_tensor_tensor(r[:], kf[:], -C1, y[:], ALU.mult, ALU.add), [kfn, yn], [rn])
        vop(lambda e: e.scalar_tensor_tensor(r[:], kf[:], -C2, r[:], ALU.mult, ALU.add), [kfn, rn], [rn])
        vop(lambda e: e.tensor_scalar(m[:], r[:], 0.0, 2 * PI, ALU.is_lt, ALU.mult), [rn], [mn])
        vop(lambda e: e.tensor_tensor(r[:], r[:], m[:], ALU.add), [rn, mn], [rn])
        vop(lambda e: e.tensor_scalar(m[:], r[:], 2 * PI, -2 * PI, ALU.is_ge, ALU.mult), [rn], [mn])
        vop(lambda e: e.tensor_tensor(r[:], r[:], m[:], ALU.add), [rn, mn], [rn])
        vop(lambda e: e.activation(s_ap, r[:], AF.Sin, bias=negpi[:, 0:1], scale=1.0), [rn, "negpi"], [sn], eng=S)
        vop(lambda e: e.tensor_scalar(y[:], r[:], 0.5 * PI, None, ALU.add), [rn], [yn])
        vop(lambda e: e.tensor_scalar(m[:], y[:], 2 * PI, -2 * PI, ALU.is_ge, ALU.mult), [yn], [mn])
        vop(lambda e: e.tensor_tensor(y[:], y[:], m[:], ALU.add), [yn, mn], [yn])
        vop(lambda e: e.activation(c_ap, y[:], AF.Sin, bias=negpi[:, 0:1], scale=1.0), [yn, "negpi"], [cn], eng=S)
    def sincos(x, xn, shape):
        s_, sn = tmp(shape); c_, cn = tmp(shape)
        sincos_into(x[:], xn, s_[:], sn, c_[:], cn, shape)
        return (s_, sn), (c_, cn)

    shA = [128, 128]
    def A(i):
        return sA[:, i].rearrange("p a n -> p (a n)")
    dtA, dtAn = tmp(shA); ardt, ardtn = tmp(shA); aidt, aidtn = tmp(shA); mag, magn = tmp(shA)
    vop(lambda e: e.activation(dtA[:], A(2), AF.Exp), ["ssmA"], [dtAn], eng=S)
    vop(lambda e: e.tensor_tensor(ardt[:], A(0), dtA[:], ALU.mult), ["ssmA", dtAn], [ardtn])
    vop(lambda e: e.tensor_tensor(aidt[:], A(1), dtA[:], ALU.mult), ["ssmA", dtAn], [aidtn])
    vop(lambda e: e.activation(mag[:], ardt[:], AF.Exp), [ardtn], [magn], eng=S)
    (sn_, snn), (cs_, csn) = sincos(aidt, aidtn, shA)
    abr, abrn = tmp(shA); abi, abin = tmp(shA); den, denn = tmp(shA); t0, t0n = tmp(shA); t1, t1n = tmp(shA)
    fr, frn = tmp(shA); fi, fin = tmp(shA); bbr, bbrn = tmp(shA); bbi, bbin = tmp(shA)
    vop(lambda e: e.tensor_tensor(abr[:], mag[:], cs_[:], ALU.mult), [magn, csn], [abrn])
    vop(lambda e: e.tensor_tensor(abi[:], mag[:], sn_[:], ALU.mult), [magn, snn], [abin])
    vop(lambda e: e.tensor_tensor(t0[:], A(0), A(0), ALU.mult), ["ssmA"], [t0n])
    vop(lambda e: e.tensor_tensor(t1[:], A(1), A(1), ALU.mult), ["ssmA"], [t1n])
    vop(lambda e: e.tensor_tensor(den[:], t0[:], t1[:], ALU.add), [t0n, t1n], [denn])
    vop(lambda e: e.reciprocal(den[:], den[:]), [denn], [denn])
    vop(lambda e: e.tensor_scalar(abr[:], abr[:], -1.0, None, ALU.add), [abrn], [abrn])
    vop(lambda e: e.tensor_tensor(t0[:], abr[:], A(0), ALU.mult), [abrn, "ssmA"], [t0n])
    vop(lambda e: e.tensor_tensor(t1[:], abi[:], A(1), ALU.mult), [abin, "ssmA"], [t1n])
    vop(lambda e: e.tensor_tensor(fr[:], t0[:], t1[:], ALU.add), [t0n, t1n], [frn])
    vop(lambda e: e.tensor_tensor(fr[:], fr[:], den[:], ALU.mult), [frn, denn], [frn])
    vop(lambda e: e.tensor_tensor(t0[:], abi[:], A(0), ALU.mult), [abin, "ssmA"], [t0n])
    vop(lambda e: e.tensor_tensor(t1[:], abr[:], A(1), ALU.mult), [abrn, "ssmA"], [t1n])
    vop(lambda e: e.tensor_tensor(fi[:], t0[:], t1[:], ALU.subtract), [t0n, t1n], [fin])
    vop(lambda e: e.tensor_tensor(fi[:], fi[:], den[:], ALU.mult), [fin, denn], [fin])
    vop(lambda e: e.tensor_tensor(t0[:], fr[:], A(3), ALU.mult), [frn, "ssmA"], [t0n])
    vop(lambda e: e.tensor_tensor(t1[:], fi[:], A(4), ALU.mult), [fin, "ssmA"], [t1n])
    vop(lambda e: e.tensor_tensor(bbr[:], t0[:], t1[:], ALU.subtract), [t0n, t1n], [bbrn])
    vop(lambda e: e.tensor_tensor(t0[:], fr[:], A(4), ALU.mult), [frn, "ssmA"], [t0n])
    vop(lambda e: e.tensor_tensor(t1[:], fi[:], A(3), ALU.mult), [fin, "ssmA"], [t1n])
    vop(lambda e: e.tensor_tensor(bbi[:], t0[:], t1[:], ALU.add), [t0n, t1n], [bbin])
    LB1 = P.sbuf("LB1", [128, 16, 128], BF16); LB2 = P.sbuf("LB2", [128, 16, 128], BF16)
    for g in range(16):
        ct, gp = g // 8, g % 8
        cs = slice(ct * 64, ct * 64 + 64)
        vop(lambda e, g=g, gp=gp, cs=cs: e.tensor_scalar(LB1[:, g, 0:64], bbr[:, cs], mask[:, gp:gp + 1], None, ALU.mult), [bbrn, "c_mask"], ["LB"])
        vop(lambda e, g=g, gp=gp, cs=cs: e.tensor_scalar(LB1[:, g, 64:128], bbi[:, cs], mask[:, gp:gp + 1], None, ALU.mult), [bbin, "c_mask"], ["LB"])
        vop(lambda e, g=g, gp=gp, cs=cs: e.tensor_scalar(LB2[:, g, 0:64], bbi[:, cs], mask[:, gp:gp + 1], None, ALU.mult), [bbin, "c_mask"], ["LB"])
        vop(lambda e, g=g, gp=gp, cs=cs: e.tensor_scalar(LB2[:, g, 64:128], bbr[:, cs], mask[:, 8 + gp:9 + gp], None, ALU.mult), [bbrn, "c_mask"], ["LB"])
    shB = [128, 16]
    dtB, dtBn = tmp(shB); th, thn = tmp(shB); rho, rhon = tmp(shB); thT, thTn = tmp(shB)
    vop(lambda e: e.activation(dtB[:], sB[:, 2, :], AF.Exp), ["ssmB"], [dtBn], eng=S)
    vop(lambda e: e.tensor_tensor(th[:], sB[:, 1, :], dtB[:], ALU.mult), ["ssmB", dtBn], [thn])
    vop(lambda e: e.tensor_tensor(rho[:], sB[:, 0, :], dtB[:], ALU.mult), ["ssmB", dtBn], [rhon])
    vop(lambda e: e.activation(rho[:], rho[:], AF.Exp), [rhon], [rhon], eng=S)
    vop(lambda e: e.tensor_scalar(thT[:], th[:], 512.0, None, ALU.mult), [thn], [thTn])
    (sT, sTn), (cT, cTn) = sincos(thT, thTn, shB)
    St = P.sbuf("St", [128, 16, 512], F32); Ct = P.sbuf("Ct", [128, 16, 512], F32)
    ang = P.sbuf("ang", [128, 512], F32)
    for g in range(16):
        vop(lambda e, g=g: e.tensor_scalar(ang[:], iota[:], th[:, g:g + 1], None, ALU.mult), ["c_iota", thn], ["ang"])
        sincos_into(ang[:], "ang", St[:, g, :], ("St", g), Ct[:, g, :], ("Ct", g), [128, 512])
    Rm = P.sbuf("Rm", [128, 16, 128], F32)
    rt, rtn = tmp([128, 128])
    for g in range(16):
        vop(lambda e, g=g: e.tensor_scalar(rt[:], Jm[:], sT[:, g:g + 1], None, ALU.mult), ["c_J", sTn], [rtn])
        vop(lambda e, g=g: e.scalar_tensor_tensor(Rm[:, g, :], ident[:], cT[:, g:g + 1], rt[:], ALU.mult, ALU.add), ["c_ident", cTn, rtn], [("Rm", g)])
    LC1 = P.sbuf("LC1", [128, 16, 128], BF16); LC2 = P.sbuf("LC2", [128, 16, 128], BF16)
    vop(lambda e: e.memset(LC1[:], 0.0), [], ["LC"])
    vop(lambda e: e.memset(LC2[:], 0.0), [], ["LC"])
    for g in range(16):
        gp = g % 8
        vop(lambda e, g=g, gp=gp: e.tensor_scalar(LC1[:, g, gp * 16:gp * 16 + 16], cc[:, 0, g, :], sgn[:, 0:1], None, ALU.mult), ["ssmcc", "c_sgn"], ["LC"])
        vop(lambda e, g=g, gp=gp: e.tensor_scalar(LC2[:, g, gp * 16:gp * 16 + 16], cc[:, 1, g, :], sgn[:, 1:2], None, ALU.mult), ["ssmcc", "c_sgn"], ["LC"])

    u32p = Pool(P, "u32", 3, [128, 512], F32)
    ubfp = Pool(P, "ubf", 3, [128, 512], BF16)
    abp = Pool(P, "abps", 4, [128, 512], F32, psum=True)
    yps = Pool(P, "yps", 2, [128, 512], F32, psum=True)
    ips = Pool(P, "ips", 1, [128, 16], F32, psum=True)
    t1p = Pool(P, "st1", 4, [128, 512], F32)
    t2p = Pool(P, "st2", 4, [128, 512], F32)
    winp = Pool(P, "win", 4, [128, 512], F32)
    wstp = Pool(P, "wst", 4, [128, 512], F32)
    p1p = Pool(P, "p1", 4, [128, 512], BF16)
    p2p = Pool(P, "p2", 4, [128, 512], BF16)
    ytp = Pool(P, "yt", 2, [128, 512], F32)
    ybp = Pool(P, "yb", 2, [128, 512], BF16)
    init = P.sbuf("init", [128, 16], F32)
    wlast = P.sbuf("wlast", [128, 16], F32)
    items = [(c, ct, gp) for c in range(NCH) for ct in range(2) for gp in range(8)]
    tiles = {}
    xres = {}
    def emit_x(it):
        c, ct, gp = it
        if gp == 0:
            u32, u32n = u32p.next(); ubf, ubfn = ubfp.next()
            P.dma("sync", u32[:], uT[ct * 128:(ct + 1) * 128, c * 512:(c + 1) * 512], writes=[u32n])
            vop(lambda e, ubf=ubf, u32=u32: e.copy(ubf[:], u32[:]), [u32n], [ubfn], eng=S)
            tiles[(c, ct)] = (u32, u32n, ubf, ubfn)
        u32, u32n, ubf, ubfn = tiles[(c, ct)]
        g = ct * 8 + gp
        Aps, An = abp.next(); Bps, Bn = abp.next()
        P.op(T_, lambda e, Aps=Aps, g=g, ubf=ubf: e.matmul(Aps[:], LB1[:, g, :], ubf[:], start=True, stop=True), reads=["LB", ubfn], writes=[An])
        P.op(T_, lambda e, Bps=Bps, g=g, ubf=ubf: e.matmul(Bps[:], LB2[:, g, :], ubf[:], start=True, stop=True), reads=["LB", ubfn], writes=[Bn])
        xres[it] = (Aps, An, Bps, Bn)
        if c > 0:
            ip, ipn = ips.next()
            P.op(T_, lambda e, ip=ip, g=g: e.matmul(ip[:, g:g + 1], Rm[:, g, :], wlast[:, g:g + 1], start=True, stop=True), reads=[("Rm", g), ("wlast", g)], writes=[ipn])
            vop(lambda e, ip=ip, g=g: e.copy(init[:, g:g + 1], ip[:, g:g + 1]), [ipn], [("init", g)], eng=S)
    ybank = {}
    emit_x(items[0])
    for idx, it in enumerate(items):
        c, ct, gp = it
        g = ct * 8 + gp
        if idx + 1 < len(items):
            emit_x(items[idx + 1])
        u32, u32n, ubf, ubfn = tiles[(c, ct)]
        Aps, An, Bps, Bn = xres.pop(it)
        if gp == 0:
            ybank[(c, ct)] = yps.next()
        Y, Yn = ybank[(c, ct)]
        a1, a1n = t1p.next(); a2, a2n = t2p.next(); win, winn = winp.next(); wst, wstn = wstp.next()
        vop(lambda e, a1=a1, Aps=Aps, g=g: e.tensor_tensor(a1[:], Ct[:, g, :], Aps[:], ALU.mult), [("Ct", g), An], [a1n])
        vop(lambda e, a2=a2, Bps=Bps, g=g: e.tensor_tensor(a2[:], St[:, g, :], Bps[:], ALU.mult), [("St", g), Bn], [a2n])
        vop(lambda e, a1=a1, a2=a2, win=win: e.tensor_tensor(win[:], a1[:], a2[:], ALU.add), [a1n, a2n], [winn])
        if c > 0:
            vop(lambda e, wst=wst, win=win, g=g: e.tensor_tensor_scan(wst[:], rho[:, g:g + 1].to_broadcast([128, 512]), win[:], init[:, g:g + 1], ALU.mult, ALU.add),
                [rhon, winn, ("init", g)], [wstn])
        else:
            vop(lambda e, wst=wst, win=win, g=g: e.tensor_tensor_scan(wst[:], rho[:, g:g + 1].to_broadcast([128, 512]), win[:], 0.0, ALU.mult, ALU.add),
                [rhon, winn], [wstn])
        vop(lambda e, wst=wst, g=g: e.copy(wlast[:, g:g + 1], wst[:, 511:512]), [wstn], [("wlast", g)], eng=S)
        p1, p1n = p1p.next(); p2, p2n = p2p.next()
        vop(lambda e, p1=p1, wst=wst, g=g: e.tensor_tensor(p1[:], Ct[:, g, :], wst[:], ALU.mult), [("Ct", g), wstn], [p1n])
        vop(lambda e, p2=p2, wst=wst, g=g: e.tensor_tensor(p2[:], St[:, g, :], wst[:], ALU.mult), [("St", g), wstn], [p2n], eng=G)
        P.op(T_, lambda e, Y=Y, g=g, p1=p1, gp=gp: e.matmul(Y[:], LC1[:, g, :], p1[:], start=(gp == 0), stop=False), reads=["LC", p1n], writes=[Yn])
        P.op(T_, lambda e, Y=Y, g=g, p2=p2, gp=gp: e.matmul(Y[:], LC2[:, g, :], p2[:], start=False, stop=(gp == 7)), reads=["LC", p2n], writes=[Yn])
        if gp == 7:
            yt, ytn = ytp.next(); yb, ybn = ybp.next()
            vop(lambda e, yt=yt, u32=u32, Y=Y, ct=ct: e.scalar_tensor_tensor(yt[:], u32[:], dsk[:, ct:ct + 1], Y[:], ALU.mult, ALU.add), [u32n, "ssmd", Yn], [ytn])
            vop(lambda e, yt=yt, yb=yb: e.activation(yb[:], yt[:], AF.Gelu), [ytn], [ybn], eng=S)
            P.dma("sync", yT[ct * 128:(ct + 1) * 128, c * 512:(c + 1) * 512], yb[:], reads=[ybn])


def gdn_host_params(j, conv_w, a_log, dt_bias, norm_w):
    cw = np.zeros((128, 3, 2, 4), np.float32)
    for s in range(3):
        for h in range(2):
            c0 = s * 2048 + (2 * j + h) * 128
            cw[:, s, h, :] = conv_w[:, c0:c0 + 128].T
    return {"g_cw": cw,
            "g_alog": np.ascontiguousarray(a_log[2 * j:2 * j + 2].reshape(2, 1)),
            "g_dtb": np.ascontiguousarray(dt_bias[2 * j:2 * j + 2].reshape(2, 1)),
            "g_nw": np.ascontiguousarray(norm_w.reshape(128, 1)),
            "g_ident": np.eye(128, dtype=np.float32)}

def emit_gdn(P, gin, oT, D, T=8192, dbg=None):
    nc = P.nc
    NB = T // 512
    def vop(fn, reads, writes, eng=V):
        P.op(eng, fn, reads=reads, writes=writes)
    def ld(name, shape, src):
        t = P.sbuf("sb_" + name, shape, F32)
        P.dma("sync", t[:], src, writes=[name])
        return t
    cw = ld("g_cw", [128, 3, 2, 4], D["g_cw"])
    alog = ld("g_alog", [2, 1], D["g_alog"])
    dtb = ld("g_dtb", [2, 1], D["g_dtb"])
    nw = ld("g_nw", [128, 1], D["g_nw"])
    ident = ld("g_ident", [128, 128], D["g_ident"])
    ones_bf = P.sbuf("g_ones", [128, 128], BF16)
    vop(lambda e: e.memset(ones_bf[:], 1.0), [], ["g_ones"])
    epsc = P.sbuf("g_eps", [128, 1], F32)
    vop(lambda e: e.memset(epsc[:], 1e-6), [], ["g_eps"])
    onec = P.sbuf("g_one", [128, 1], F32)
    vop(lambda e: e.memset(onec[:], 1.0), [], ["g_one"])
    RC = min(T, 2048)
    arow = P.sbuf("g_arow", [2, RC], F32); brow = P.sbuf("g_brow", [2, RC], F32); tr = P.sbuf("g_tr", [2, RC], F32)
    nega = P.sbuf("g_nega", [2, 1], F32)
    vop(lambda e: e.activation(nega[:], alog[:], AF.Exp), ["g_alog"], ["g_nega"], eng=S)
    vop(lambda e: e.tensor_scalar(nega[:], nega[:], -1.0, None, ALU.mult), ["g_nega"], ["g_nega"])
    scr = nc.dram_tensor("g_scr", [3, 2, T], F32).ap()
    for rc in range(T // RC):
        cs = slice(rc * RC, (rc + 1) * RC)
        P.dma("sync", arow[:], gin[1024:1026, cs], writes=["g_arow"])
        P.dma("sync", brow[:], gin[1026:1028, cs], writes=["g_brow"])
        vop(lambda e: e.activation(tr[:], arow[:], AF.Exp, bias=dtb[:, 0:1], scale=1.0), ["g_arow", "g_dtb"], ["g_tr"], eng=S)
        vop(lambda e: e.activation(tr[:], tr[:], AF.Ln, bias=onec[0:2, 0:1], scale=1.0), ["g_tr", "g_one"], ["g_tr"], eng=S)
        vop(lambda e: e.tensor_scalar(tr[:], tr[:], nega[:, 0:1], None, ALU.mult), ["g_tr", "g_nega"], ["g_tr"])
        vop(lambda e: e.activation(arow[:], tr[:], AF.Exp), ["g_tr"], ["g_arow"], eng=S)
        vop(lambda e: e.activation(brow[:], brow[:], AF.Sigmoid), ["g_brow"], ["g_brow"], eng=S)
        vop(lambda e: e.scalar_tensor_tensor(tr[:], arow[:], -1.0, brow[:], ALU.mult, ALU.mult), ["g_arow", "g_brow"], ["g_tr"])
        P.dma("sync", scr[0, :, cs], arow[:], reads=["g_arow"], writes=[("scr", 0, rc)])
        P.dma("sync", scr[1, :, cs], tr[:], reads=["g_tr"], writes=[("scr", 1, rc)])
        P.dma("sync", scr[2, :, cs], brow[:], reads=["g_brow"], writes=[("scr", 2, rc)])
    NRC = T // RC
    natok = P.sbuf("g_natok", [128, 2, T // 128], F32)
    for h in range(2):
        P.dma("sync", natok[:, h, :], scr[1, h].rearrange("(b p) -> p b", p=128), reads=[("scr", 1, r_) for r_ in range(NRC)], writes=[("natok", h)],
              allow_slow_non_contiguous=True)
    kcol = [P.sbuf("g_kcol%d" % h, [128, T], BF16) for h in range(2)]
    qcol = [P.sbuf("g_qcol%d" % h, [128, T], BF16) for h in range(2)]
    S32 = [[P.sbuf("g_S32_%d_%d" % (h, i), [128, 128], F32) for i in range(2)] for h in range(2)]
    Sbf = [[P.sbuf("g_Sbf_%d_%d" % (h, i), [128, 128], BF16) for i in range(2)] for h in range(2)]
    for h in range(2):
        vop(lambda e, h=h: e.memset(S32[h][0][:], 0.0), [], [("S32", h, 0)])
        vop(lambda e, h=h: e.memset(Sbf[h][0][:], 0.0), [], [("Sbf", h, 0)])
    rawp = Pool(P, "g_raw", 3, [128, 515], F32)
    accp = Pool(P, "g_acc", 3, [128, 512], F32)
    sqp = Pool(P, "g_sq", 2, [128, 512], BF16)
    rnp = Pool(P, "g_rn", 2, [128, 512], F32)
    bcp = Pool(P, "g_bc", 2, [128, 512], F32)
    A128p = [Pool(P, "g_A%d" % h, 2, [128, 512], F32) for h in range(2)]
    ktokp = [Pool(P, "g_kt%d" % h, 2, [128, 4, 128], BF16) for h in range(2)]
    bvtokp = [Pool(P, "g_bv%d" % h, 2, [128, 4, 128], F32) for h in range(2)]
    knp = Pool(P, "g_kn", 2, [128, 512], F32)
    bvp = Pool(P, "g_bvf", 2, [128, 512], F32)
    zp = [Pool(P, "g_z%d" % h, 2, [128, 512], F32) for h in range(2)]
    kmp = [Pool(P, "g_km%d" % h, 3, [128, 128], BF16) for h in range(2)]
    tmpp = [Pool(P, "g_tmp%d" % h, 2, [128, 128], BF16) for h in range(2)]
    pb = [P.psum("g_pb%d" % i, [128, 512], F32) for i in range(8)]
    ksps = [[(pb[0 + h][:, i * 128:(i + 1) * 128], ("ksps", h, i)) for i in range(2)] for h in range(2)]
    dsps = [[(pb[4 + h][:, i * 128:(i + 1) * 128], ("dsps", h, i)) for i in range(2)] for h in range(2)]
    ops_ = [[(pb[2 + h], ("ops", h, 0)) for i in range(2)] for h in range(2)]
    ssps = (pb[6], "ssps")
    trps = (pb[7], "trps")

    def conv_silu(h, s, blk):
        raw, rn_ = rawp.next(); acc, an = accp.next()
        r0 = s * 256 + h * 128
        t0 = blk * 512
        if blk == 0:
            vop(lambda e: e.memset(raw[:, 0:3], 0.0), [], [rn_])
            P.dma("sync", raw[:, 3:515], gin[r0:r0 + 128, 0:512], writes=[rn_])
        else:
            P.dma("sync", raw[:], gin[r0:r0 + 128, t0 - 3:t0 + 512], writes=[rn_])
        vop(lambda e: e.tensor_scalar(acc[:], raw[:, 0:512], cw[:, s, h, 0:1], None, ALU.mult), [rn_, "g_cw"], [an])
        for j in range(1, 4):
            vop(lambda e, j=j: e.scalar_tensor_tensor(acc[:], raw[:, j:j + 512], cw[:, s, h, j:j + 1], acc[:], ALU.mult, ALU.add), [rn_, "g_cw", an], [an])
        vop(lambda e: e.activation(acc[:], acc[:], AF.Silu), [an], [an], eng=S)
        return acc, an

    def rnorm(acc, an, scale):
        sq, sn = sqp.next(); r, rn_ = rnp.next()
        vop(lambda e: e.activation(sq[:], acc[:], AF.Square), [an], [sn], eng=S)
        P.op(T_, lambda e: e.matmul(ssps[0][:], ones_bf[:], sq[:], start=True, stop=True), reads=["g_ones", sn], writes=[ssps[1]])
        vop(lambda e: e.activation(r[:], ssps[0][:], AF.Sqrt, bias=epsc[:, 0:1], scale=1.0), [ssps[1], "g_eps"], [rn_], eng=S)
        vop(lambda e: e.reciprocal(r[:], r[:]), [rn_], [rn_])
        if scale != 1.0:
            vop(lambda e: e.tensor_scalar(r[:], r[:], scale, None, ALU.mult), [rn_], [rn_])
        return r, rn_

    blkstate = {}
    def prep(h, blk):
        t0 = blk * 512
        acc, an = conv_silu(h, 0, blk)
        r, rn_ = rnorm(acc, an, 128.0 ** -0.5)
        vop(lambda e, acc=acc, r=r: e.tensor_tensor(qcol[h][:, t0:t0 + 512], acc[:], r[:], ALU.mult), [an, rn_], [("qcol", h, blk)])
        acc, an = conv_silu(h, 1, blk)
        r, rn_ = rnorm(acc, an, 1.0)
        kn, knn = knp.next()
        vop(lambda e, acc=acc, r=r, kn=kn: e.tensor_tensor(kn[:], acc[:], r[:], ALU.mult), [an, rn_], [knn])
        vop(lambda e, kn=kn: e.copy(kcol[h][:, t0:t0 + 512], kn[:]), [knn], [("kcol", h, blk)], eng=S)
        acc, an = conv_silu(h, 2, blk)
        bc, bcn = bcp.next()
        P.dma("sync", bc[:], scr[2, h:h + 1, t0:t0 + 512].partition_broadcast(128), reads=[("scr", 2, t0 // RC)], writes=[bcn])
        bv, bvn = bvp.next()
        vop(lambda e, acc=acc, bv=bv, bc=bc: e.tensor_tensor(bv[:], acc[:], bc[:], ALU.mult), [an, bcn], [bvn], eng=G)
        A1, A1n = A128p[h].next()
        P.dma("sync", A1[:], scr[0, h:h + 1, t0:t0 + 512].partition_broadcast(128), reads=[("scr", 0, t0 // RC)], writes=[A1n])
        z, zn = zp[h].next()
        P.dma("sync", z[:], gin[768 + h * 128:768 + (h + 1) * 128, t0:t0 + 512], writes=[zn])
        vop(lambda e, z=z: e.activation(z[:], z[:], AF.Silu), [zn], [zn], eng=S)
        kt, ktn = ktokp[h].next(); bvt, bvtn = bvtokp[h].next()
        for src, srcn, dst, dstn in ((kn, knn, kt, ktn), (bv, bvn, bvt, bvtn)):
            for i in range(4):
                P.op(T_, lambda e, src=src, i=i: e.transpose(trps[0][:, i * 128:(i + 1) * 128], src[:, i * 128:(i + 1) * 128], ident[:]),
                     reads=[srcn, "g_ident"], writes=[trps[1]])
            vop(lambda e, dst=dst: e.tensor_copy(dst[:].rearrange("p a b -> p (a b)"), trps[0][:]), [trps[1]], [dstn])
        blkstate[(h, blk)] = dict(A=A1, An=A1n, kt=kt, ktn=ktn, bvt=bvt, bvtn=bvtn, z=z, zn=zn)

    cur = [0, 0]
    def token_step(h, t):
        blk, b4, p = t // 512, (t % 512) // 128, t % 128
        st = blkstate[(h, blk)]
        c = cur[h]; n = 1 - c
        km, kmn = kmp[h].next(); tm, tmn = tmpp[h].next()
        ks, ksn = ksps[h][t % 2]; ds, dsn = dsps[h][t % 2]
        ob, obn = ops_[h][blk % 2]
        b128 = t // 128
        vop(lambda e: e.tensor_scalar(km[:], st["kt"][:, b4, :], ident[:, p:p + 1], None, ALU.mult), [st["ktn"], "g_ident"], [kmn], eng=G)
        P.op(T_, lambda e: e.matmul(ks, kcol[h][:, b128 * 128:(b128 + 1) * 128], Sbf[h][c][:], start=True, stop=True),
             reads=[("kcol", h, blk), ("Sbf", h, c)], writes=[ksn])
        vop(lambda e: e.scalar_tensor_tensor(tm[:], ks, natok[:, h, b128:b128 + 1], st["bvt"][:, b4, :], ALU.mult, ALU.add),
            [ksn, ("natok", h), st["bvtn"]], [tmn])
        P.op(T_, lambda e: e.matmul(ds, km[:], tm[:], start=True, stop=True), reads=[kmn, tmn], writes=[dsn])
        tl = t % 512
        vop(lambda e: e.scalar_tensor_tensor(Sbf[h][n][:], S32[h][c][:], st["A"][:, tl:tl + 1], ds, ALU.mult, ALU.add),
            [("S32", h, c), st["An"], dsn], [("Sbf", h, n)])
        vop(lambda e: e.scalar_tensor_tensor(S32[h][n][:], S32[h][c][:], st["A"][:, tl:tl + 1], ds, ALU.mult, ALU.add),
            [("S32", h, c), st["An"], dsn], [("S32", h, n)])
        P.op(T_, lambda e: e.matmul(ob[:, tl:tl + 1], Sbf[h][n][:], qcol[h][:, t:t + 1], start=True, stop=True),
             reads=[("Sbf", h, n), ("qcol", h, blk)], writes=[obn])
        cur[h] = n

    onp = Pool(P, "g_on", 2, [128, 512], F32)
    obp = Pool(P, "g_ob", 2, [128, 512], BF16)
    def finish(h, blk):
        st = blkstate[(h, blk)]
        ob, obn = ops_[h][blk % 2]
        sq, sn = sqp.next(); r, rn_ = rnp.next(); on, onn = onp.next(); o16, o16n = obp.next()
        vop(lambda e: e.activation(sq[:], ob[:], AF.Square), [obn], [sn], eng=S)
        P.op(T_, lambda e: e.matmul(ssps[0][:], ones_bf[:], sq[:], start=True, stop=True), reads=["g_ones", sn], writes=[ssps[1]])
        vop(lambda e: e.activation(r[:], ssps[0][:], AF.Sqrt, bias=epsc[:, 0:1], scale=1.0 / 128), [ssps[1], "g_eps"], [rn_], eng=S)
        vop(lambda e: e.reciprocal(r[:], r[:]), [rn_], [rn_])
        vop(lambda e: e.scalar_tensor_tensor(on[:], ob[:], nw[:, 0:1], r[:], ALU.mult, ALU.mult), [obn, "g_nw", rn_], [onn])
        vop(lambda e: e.tensor_tensor(o16[:], on[:], st["z"][:], ALU.mult), [onn, st["zn"]], [o16n], eng=G)
        P.dma("sync", oT[h * 128:(h + 1) * 128, blk * 512:(blk + 1) * 512], o16[:], reads=[o16n])

    for h in range(2):
        prep(h, 0)
    if dbg is not None:
        st = blkstate[(0, 0)]
        P.dma("sync", dbg["d_q"], qcol[0][:, 0:512], reads=[("qcol", 0, 0)])
        P.dma("sync", dbg["d_k"], kcol[0][:, 0:512], reads=[("kcol", 0, 0)])
        P.dma("sync", dbg["d_kt"], st["kt"][:].rearrange("p a b -> p (a b)"), reads=[st["ktn"]])
        P.dma("sync", dbg["d_bvt"], st["bvt"][:].rearrange("p a b -> p (a b)"), reads=[st["bvtn"]])
        P.dma("sync", dbg["d_A"], st["A"][:], reads=[st["An"]])
        P.dma("sync", dbg["d_na"], natok[:, 0, :], reads=[("natok", 0)])
    for blk in range(NB):
        if blk + 1 < NB:
            for h in range(2):
                prep(h, blk + 1)
        for t in range(blk * 512, (blk + 1) * 512):
            for h in range(2):
                token_step(h, t)
        for h in range(2):
            finish(h, blk)

NEG = -30000.0

def gdn2_consts():
    i = np.arange(64)
    negs = np.where(i[None, :] > i[:, None], 0.0, NEG).astype(np.float32)
    nega = np.where(i[:, None] > i[None, :], 0.0, NEG).astype(np.float32)
    return {"g_negs": np.ascontiguousarray(np.tile(negs[:, None, :], (1, 8, 1))),
            "g_nega": np.ascontiguousarray(np.tile(nega[:, None, :], (1, 8, 1))),
            "g_id8": np.ascontiguousarray(np.tile(np.eye(64, dtype=np.float32)[:, None, :], (1, 8, 1)))}

def emit_gdn2(P, gin, oT, D, T=8192, dbg=None):
    nc = P.nc
    NB = T // 512
    NCH = T // 64
    def vop(fn, reads, writes, eng=V):
        P.op(eng, fn, reads=reads, writes=writes)
    def ld(name, shape, src):
        t = P.sbuf("sb_" + name, shape, F32)
        P.dma("sync", t[:], src, writes=[name])
        return t
    cw = ld("g_cw", [128, 3, 2, 4], D["g_cw"])
    alog = ld("g_alog", [2, 1], D["g_alog"])
    dtb = ld("g_dtb", [2, 1], D["g_dtb"])
    nw = ld("g_nw", [128, 1], D["g_nw"])
    ident = ld("g_ident", [128, 128], D["g_ident"])
    negs = ld("g_negs", [64, 8, 64], D["g_negs"])
    negA = ld("g_nega", [64, 8, 64], D["g_nega"])
    id8 = ld("g_id8", [64, 8, 64], D["g_id8"])
    identb = P.sbuf("g_identb", [128, 128], BF16)
    vop(lambda e: e.tensor_copy(identb[:], ident[:]), ["g_ident"], ["g_identb"])
    ones_bf = P.sbuf("g_ones", [128, 128], BF16)
    vop(lambda e: e.memset(ones_bf[:], 1.0), [], ["g_ones"])
    epsc = P.sbuf("g_eps", [128, 1], F32)
    vop(lambda e: e.memset(epsc[:], 1e-6), [], ["g_eps"])
    onec = P.sbuf("g_one", [128, 1], F32)
    vop(lambda e: e.memset(onec[:], 1.0), [], ["g_one"])
    RC = min(T, 2048)
    NRC = T // RC
    arow = P.sbuf("g_arow", [2, RC], F32); brow = P.sbuf("g_brow", [2, RC], F32); tr = P.sbuf("g_tr", [2, RC], F32)
    cmask = P.sbuf("g_cmask", [2, RC], F32)
    vop(lambda e: e.memset(cmask[:], 1.0), [], ["g_cmask"])
    vop(lambda e: e.memset(cmask[:].rearrange("p (c i) -> p c i", i=64)[:, :, 0:1], 0.0), ["g_cmask"], ["g_cmask"])
    nega_ = P.sbuf("g_negal", [2, 1], F32)
    vop(lambda e: e.activation(nega_[:], alog[:], AF.Exp), ["g_alog"], ["g_negal"], eng=S)
    vop(lambda e: e.tensor_scalar(nega_[:], nega_[:], -1.0, None, ALU.mult), ["g_negal"], ["g_negal"])
    scr = nc.dram_tensor("g_scr", [2, 2, T], F32).ap()
    for rc in range(NRC):
        cs = slice(rc * RC, (rc + 1) * RC)
        P.dma("sync", arow[:], gin[1024:1026, cs], writes=["g_arow"])
        P.dma("sync", brow[:], gin[1026:1028, cs], writes=["g_brow"])
        vop(lambda e: e.activation(tr[:], arow[:], AF.Exp, bias=dtb[:, 0:1], scale=1.0), ["g_arow", "g_dtb"], ["g_tr"], eng=S)
        vop(lambda e: e.activation(tr[:], tr[:], AF.Ln, bias=onec[0:2, 0:1], scale=1.0), ["g_tr", "g_one"], ["g_tr"], eng=S)
        vop(lambda e: e.tensor_scalar(tr[:], tr[:], nega_[:, 0:1], None, ALU.mult), ["g_tr", "g_negal"], ["g_tr"])
        vop(lambda e: e.tensor_tensor_scan(arow[:], cmask[:], tr[:], 0.0, ALU.mult, ALU.add), ["g_cmask", "g_tr", "g_arow"], ["g_arow"])
        vop(lambda e: e.activation(brow[:], brow[:], AF.Sigmoid), ["g_brow"], ["g_brow"], eng=S)
        P.dma("sync", scr[0, :, cs], arow[:], reads=["g_arow"], writes=[("scr", 0, rc)])
        P.dma("sync", scr[1, :, cs], brow[:], reads=["g_brow"], writes=[("scr", 1, rc)])
    gcol = P.sbuf("g_gcol", [64, 2, NCH], F32); bcol = P.sbuf("g_bcol", [64, 2, NCH], F32)
    for h in range(2):
        P.dma("sync", gcol[:, h, :], scr[0, h].rearrange("(c i) -> i c", i=64), reads=[("scr", 0, r_) for r_ in range(NRC)], writes=[("gcol", h)], allow_slow_non_contiguous=True)
        P.dma("sync", bcol[:, h, :], scr[1, h].rearrange("(c i) -> i c", i=64), reads=[("scr", 1, r_) for r_ in range(NRC)], writes=[("bcol", h)], allow_slow_non_contiguous=True)
    S32 = [[P.sbuf("g_S32_%d_%d" % (h, i), [128, 128], F32) for i in range(2)] for h in range(2)]
    Sbf = [[P.sbuf("g_Sbf_%d_%d" % (h, i), [128, 128], BF16) for i in range(2)] for h in range(2)]
    for h in range(2):
        vop(lambda e, h=h: e.memset(S32[h][0][:], 0.0), [], [("S32", h, 0)])
        vop(lambda e, h=h: e.memset(Sbf[h][0][:], 0.0), [], [("Sbf", h, 0)])
    def mk(name, n, shape, dt=F32):
        return Pool(P, name, n, shape, dt)
    rawp = mk("g_raw", 3, [128, 515]); accp = mk("g_acc", 4, [128, 512]); sqp = mk("g_sq", 2, [128, 512], BF16); rnp = mk("g_rn", 2, [128, 512])
    f64p = mk("g_f64", 14, [64, 8, 64])
    b64p = mk("g_b64", 4, [64, 8, 64], BF16)
    bigp = mk("g_big", 5, [128, 512])
    hbfp = mk("g_hbf", 4, [128, 512], BF16)
    tokp = mk("g_tok", 3, [64, 8, 128], BF16)
    smallp = mk("g_small", 8, [64, 8])
    vnp = [mk("g_vn%d" % h, 2, [64, 128], BF16) for h in range(2)]
    p_qg = [mk("g_pqg%d" % h, 2, [128, 512], BF16) for h in range(2)]
    p_WT = [mk("g_pWT%d" % h, 2, [128, 512], BF16) for h in range(2)]
    p_z = [mk("g_pz%d" % h, 2, [128, 512]) for h in range(2)]
    p_QKm = [mk("g_pQK%d" % h, 2, [64, 8, 64], BF16) for h in range(2)]
    p_U = [mk("g_pU%d" % h, 2, [64, 8, 128]) for h in range(2)]
    p_Kd = [mk("g_pKd%d" % h, 2, [64, 8, 128], BF16) for h in range(2)]
    p_egl = [mk("g_pegl%d" % h, 2, [128, 8]) for h in range(2)]
    onp = mk("g_on", 2, [128, 512]); obp = mk("g_ob", 2, [128, 512], BF16)
    gpp = Pool(P, "g_gp", 3, [128, 512], F32, psum=True)
    trps = P.psum("g_trps", [128, 1024], BF16)
    chps = [P.psum("g_chps%d" % h, [128, 512], F32) for h in range(2)]
    ops_ = [P.psum("g_ops%d" % h, [128, 512], F32) for h in range(2)]

    def conv_silu(h, s, blk):
        raw, rn_ = rawp.next(); acc, an = accp.next()
        r0 = s * 256 + h * 128
        t0 = blk * 512
        if blk == 0:
            vop(lambda e: e.memset(raw[:, 0:3], 0.0), [], [rn_])
            P.dma("sync", raw[:, 3:515], gin[r0:r0 + 128, 0:512], writes=[rn_])
        else:
            P.dma("sync", raw[:], gin[r0:r0 + 128, t0 - 3:t0 + 512], writes=[rn_])
        vop(lambda e: e.tensor_scalar(acc[:], raw[:, 0:512], cw[:, s, h, 0:1], None, ALU.mult), [rn_, "g_cw"], [an])
        for j in range(1, 4):
            vop(lambda e, j=j: e.scalar_tensor_tensor(acc[:], raw[:, j:j + 512], cw[:, s, h, j:j + 1], acc[:], ALU.mult, ALU.add), [rn_, "g_cw", an], [an])
        vop(lambda e: e.activation(acc[:], acc[:], AF.Silu), [an], [an], eng=S)
        return acc, an

    def rnorm(acc, an, scale):
        sq, sn = sqp.next(); r, rn_ = rnp.next(); ps, pn = gpp.next()
        vop(lambda e: e.activation(sq[:], acc[:], AF.Square), [an], [sn], eng=S)
        P.op(T_, lambda e: e.matmul(ps[:], ones_bf[:], sq[:], start=True, stop=True), reads=["g_ones", sn], writes=[pn])
        vop(lambda e: e.activation(r[:], ps[:], AF.Sqrt, bias=epsc[:, 0:1], scale=1.0), [pn, "g_eps"], [rn_], eng=S)
        vop(lambda e: e.reciprocal(r[:], r[:]), [rn_], [rn_])
        if scale != 1.0:
            vop(lambda e: e.tensor_scalar(r[:], r[:], scale, None, ALU.mult), [rn_], [rn_])
        return r, rn_

    blkstate = {}
    def prep(h, blk):
        t0 = blk * 512
        c0 = blk * 8
        st = {}
        Gr, Grn = bigp.next()
        P.dma("sync", Gr[:], scr[0, h:h + 1, t0:t0 + 512].partition_broadcast(128), reads=[("scr", 0, t0 // RC)], writes=[Grn])
        Br, Brn = bigp.next()
        P.dma("sync", Br[0:64, :], scr[1, h:h + 1, t0:t0 + 512].partition_broadcast(64), reads=[("scr", 1, t0 // RC)], writes=[Brn])
        Gr3 = Gr[0:64, :].rearrange("p (c i) -> p c i", i=64)
        Br3 = Br[0:64, :].rearrange("p (c i) -> p c i", i=64)
        gcb = gcol[:, h, c0:c0 + 8]; bcb = bcol[:, h, c0:c0 + 8]
        gct, bct = ("gcol", h), ("bcol", h)
        eG, eGn = bigp.next()
        vop(lambda e: e.activation(eG[:], Gr[:], AF.Exp), [Grn], [eGn], eng=S)
        egl, egln = p_egl[h].next()
        vop(lambda e: e.tensor_copy(egl[:], eG[:].rearrange("p (c i) -> p c i", i=64)[:, :, 63]), [eGn], [egln])
        bg, bgn = smallp.next(); ed, edn = smallp.next()
        vop(lambda e: e.activation(bg[:], gcb, AF.Exp), [gct], [bgn], eng=S)
        vop(lambda e: e.tensor_tensor(bg[:], bg[:], bcb, ALU.mult), [bgn, bct], [bgn])
        vop(lambda e: e.tensor_tensor(ed[:], Gr3[:, :, 63], gcb, ALU.subtract), [Grn, gct], [edn])
        vop(lambda e: e.activation(ed[:], ed[:], AF.Exp), [edn], [edn], eng=S)
        acc, an = conv_silu(h, 0, blk)
        r, rn_ = rnorm(acc, an, 128.0 ** -0.5)
        qn, qnn = bigp.next()
        vop(lambda e, acc=acc, r=r: e.tensor_tensor(qn[:], acc[:], r[:], ALU.mult), [an, rn_], [qnn])
        qb, qbn = hbfp.next(); qg, qgn = p_qg[h].next()
        vop(lambda e: e.copy(qb[:], qn[:]), [qnn], [qbn], eng=S)
        vop(lambda e: e.tensor_tensor(qg[:], qn[:], eG[:], ALU.mult), [qnn, eGn], [qgn], eng=G)
        acc, an = conv_silu(h, 1, blk)
        r, rn_ = rnorm(acc, an, 1.0)
        kb, kbn = hbfp.next()
        vop(lambda e, acc=acc, r=r: e.tensor_tensor(kb[:], acc[:], r[:], ALU.mult), [an, rn_], [kbn])
        acc, an = conv_silu(h, 2, blk)
        vb, vbn = hbfp.next()
        vop(lambda e, acc=acc: e.copy(vb[:], acc[:]), [an], [vbn], eng=S)
        z, zn = p_z[h].next()
        P.dma("sync", z[:], gin[768 + h * 128:768 + (h + 1) * 128, t0:t0 + 512], writes=[zn])
        vop(lambda e: e.activation(z[:], z[:], AF.Silu), [zn], [zn], eng=S)
        KK, KKn = gpp.next(); QK, QKn = gpp.next()
        for c in range(8):
            cs = slice(c * 64, (c + 1) * 64)
            P.op(T_, lambda e, cs=cs: e.matmul(KK[0:64, cs], kb[:, cs], kb[:, cs], start=True, stop=True), reads=[kbn], writes=[KKn])
            P.op(T_, lambda e, cs=cs: e.matmul(QK[0:64, cs], kb[:, cs], qb[:, cs], start=True, stop=True), reads=[kbn, qbn], writes=[QKn])
        KK3 = KK[0:64, :].rearrange("p (c i) -> p c i", i=64)
        QK3 = QK[0:64, :].rearrange("p (c i) -> p c i", i=64)
        gcb_b = gcb.unsqueeze(2).to_broadcast([64, 8, 64])
        bcb_b = bcb.unsqueeze(2).to_broadcast([64, 8, 64])
        ES, ESn = f64p.next(); EA, EAn = f64p.next()
        vop(lambda e: e.tensor_tensor(ES[:], Gr3, negs[:], ALU.add), [Grn, "g_negs"], [ESn], eng=G)
        vop(lambda e: e.tensor_tensor(ES[:], ES[:], gcb_b, ALU.subtract), [ESn, gct], [ESn])
        vop(lambda e: e.activation(ES[:], ES[:], AF.Exp), [ESn], [ESn], eng=S)
        vop(lambda e: e.scalar_tensor_tensor(EA[:], Gr3, -1.0, negA[:], ALU.mult, ALU.add), [Grn, "g_nega"], [EAn])
        vop(lambda e: e.tensor_tensor(EA[:], EA[:], gcb_b, ALU.add), [EAn, gct], [EAn])
        vop(lambda e: e.activation(EA[:], EA[:], AF.Exp), [EAn], [EAn], eng=S)
        Pk, Pkn = f64p.next(); PTk, PTkn = f64p.next()
        vop(lambda e: e.scalar_tensor_tensor(Pk[:], KK3, -1.0, EA[:], ALU.mult, ALU.mult), [KKn, EAn], [Pkn])
        vop(lambda e: e.tensor_tensor(Pk[:], Pk[:], bcb_b, ALU.mult), [Pkn, bct], [Pkn], eng=G)
        vop(lambda e: e.scalar_tensor_tensor(PTk[:], KK3, -1.0, ES[:], ALU.mult, ALU.mult), [KKn, ESn], [PTkn])
        vop(lambda e: e.tensor_tensor(PTk[:], PTk[:], Br3, ALU.mult), [PTkn, Brn], [PTkn], eng=G)
        if dbg is not None and h == 0 and blk == 0:
            P.dma("sync", dbg["d_M"], Pk[:].rearrange("p c i -> p (c i)"), reads=[Pkn])
            P.dma("sync", dbg["d_ES"], ES[:].rearrange("p c i -> p (c i)"), reads=[ESn])
            P.dma("sync", dbg["d_EA"], EA[:].rearrange("p c i -> p (c i)"), reads=[EAn])
            P.dma("sync", dbg["d_MT"], PTk[:].rearrange("p c i -> p (c i)"), reads=[PTkn])
        QKm, QKmn = p_QKm[h].next()
        vop(lambda e: e.tensor_tensor(ES[:], ES[:], id8[:], ALU.add), [ESn, "g_id8"], [ESn], eng=G)
        vop(lambda e: e.tensor_tensor(QKm[:], QK3, ES[:], ALU.mult), [QKn, ESn], [QKmn])
        TT, TTn = f64p.next()
        vop(lambda e: e.tensor_tensor(TT[:], PTk[:], id8[:], ALU.add), [PTkn, "g_id8"], [TTn], eng=G)
        Pc, Pcn, PTc, PTcn, TTc, TTcn = Pk, Pkn, PTk, PTkn, TT, TTn
        for k in range(1, 6):
            Pp, Ppn = gpp.next()
            for c in range(8):
                P.op(T_, lambda e, c=c, Pp=Pp, PTc=PTc, Pc=Pc: e.matmul(Pp[0:64, c * 64:(c + 1) * 64], PTc[:, c, :], Pc[:, c, :], start=True, stop=True), reads=[PTcn, Pcn], writes=[Ppn])
            if k < 5:
                PTp, PTpn = gpp.next()
                for c in range(8):
                    P.op(T_, lambda e, c=c, PTp=PTp, PTc=PTc, Pc=Pc: e.matmul(PTp[0:64, c * 64:(c + 1) * 64], Pc[:, c, :], PTc[:, c, :], start=True, stop=True), reads=[PTcn, Pcn], writes=[PTpn])
            Pn_, Pnn = f64p.next()
            vop(lambda e, Pn_=Pn_, Pp=Pp: e.copy(Pn_[:].rearrange("p c i -> p (c i)"), Pp[0:64, :]), [Ppn], [Pnn], eng=S)
            if k < 5:
                PTn_, PTnn = f64p.next()
                vop(lambda e, PTn_=PTn_, PTp=PTp: e.tensor_copy(PTn_[:].rearrange("p c i -> p (c i)"), PTp[0:64, :]), [PTpn], [PTnn])
            Tu, Tun = gpp.next()
            for c in range(8):
                P.op(T_, lambda e, c=c, Tu=Tu, Pn_=Pn_, TTc=TTc: e.matmul(Tu[0:64, c * 64:(c + 1) * 64], Pn_[:, c, :], TTc[:, c, :], start=True, stop=True), reads=[Pnn, TTcn], writes=[Tun])
            TT2, TT2n = f64p.next()
            vop(lambda e, TT2=TT2, TTc=TTc, Tu=Tu: e.tensor_tensor(TT2[:].rearrange("p c i -> p (c i)"), TTc[:].rearrange("p c i -> p (c i)"), Tu[0:64, :], ALU.add), [TTcn, Tun], [TT2n])
            TTc, TTcn = TT2, TT2n
            Pc, Pcn = Pn_, Pnn
            if k < 5:
                PTc, PTcn = PTn_, PTnn
            if dbg is not None and h == 0 and blk == 0 and k == 1:
                P.dma("sync", dbg["d_P1"], Pc[:].rearrange("p c i -> p (c i)"), reads=[Pcn])
                P.dma("sync", dbg["d_T1"], TTc[:].rearrange("p c i -> p (c i)"), reads=[TTcn])
        TTb, TTbn = b64p.next()
        vop(lambda e, TTc=TTc: e.copy(TTb[:], TTc[:]), [TTcn], [TTbn], eng=S)
        Kw, Kwn = tokp.next(); Kd, Kdn = p_Kd[h].next(); Vb, Vbn = tokp.next()
        for c in range(8):
            P.op(T_, lambda e, c=c: e.transpose(trps[0:64, c * 128:(c + 1) * 128], kb[:, c * 64:(c + 1) * 64], identb[:]), reads=[kbn, "g_identb"], writes=["g_trps"])
        ktv = trps[0:64, :].rearrange("p (c d) -> p c d", d=128)
        vop(lambda e: e.tensor_tensor(Kw[:], ktv, bg[:].unsqueeze(2).to_broadcast([64, 8, 128]), ALU.mult), ["g_trps", bgn], [Kwn])
        vop(lambda e: e.tensor_tensor(Kd[:], ktv, ed[:].unsqueeze(2).to_broadcast([64, 8, 128]), ALU.mult), ["g_trps", edn], [Kdn])
        for c in range(8):
            P.op(T_, lambda e, c=c: e.transpose(trps[0:64, c * 128:(c + 1) * 128], vb[:, c * 64:(c + 1) * 64], identb[:]), reads=[vbn, "g_identb"], writes=["g_trps"])
        vop(lambda e: e.tensor_tensor(Vb[:], ktv, bcb.unsqueeze(2).to_broadcast([64, 8, 128]), ALU.mult), ["g_trps", bct], [Vbn])
        U, Un = p_U[h].next()
        for half in range(2):
            Ups, Upsn = gpp.next()
            for cc in range(4):
                c = half * 4 + cc
                P.op(T_, lambda e, c=c, cc=cc, Ups=Ups: e.matmul(Ups[0:64, cc * 128:(cc + 1) * 128], TTb[:, c, :], Vb[:, c, :], start=True, stop=True), reads=[TTbn, Vbn], writes=[Upsn])
            vop(lambda e, half=half, Ups=Ups: e.copy(U[:, half * 4:half * 4 + 4, :].rearrange("p c d -> p (c d)"), Ups[0:64, :]), [Upsn], [Un], eng=S)
        Wps, Wpsn = gpp.next()
        for c in range(8):
            P.op(T_, lambda e, c=c: e.matmul(Wps[:, c * 64:(c + 1) * 64], Kw[:, c, :], TTb[:, c, :], start=True, stop=True), reads=[Kwn, TTbn], writes=[Wpsn])
        WT, WTn = p_WT[h].next()
        vop(lambda e: e.tensor_copy(WT[:], Wps[:]), [Wpsn], [WTn])
        if dbg is not None and h == 0 and blk == 0:
            P.dma("sync", dbg["d_TT"], TTc[:].rearrange("p c i -> p (c i)"), reads=[TTcn])
            P.dma("sync", dbg["d_U"], U[:].rearrange("p c i -> p (c i)"), reads=[Un])
            P.dma("sync", dbg["d_Gr"], Gr[:], reads=[Grn])
            P.dma("sync", dbg["d_gcol"], gcol[:, 0, :], reads=[("gcol", 0)])
        st.update(dict(egl=egl, egln=egln, qg=qg, qgn=qgn, QKm=QKm, QKmn=QKmn, U=U, Un=Un, WT=WT, WTn=WTn, Kd=Kd, Kdn=Kdn, z=z, zn=zn))
        blkstate[(h, blk)] = st

    cur = [0, 0]
    def chunk_step(h, blk, c):
        st = blkstate[(h, blk)]
        ci = cur[h]; n = 1 - ci
        cs = slice(c * 64, (c + 1) * 64)
        ws = chps[h][0:64, 0:128]; wsn = ("chps", h, 0)
        ds = chps[h][:, 128:256]; dsn = ("chps", h, 1)
        vn, vnn = vnp[h].next()
        ob = ops_[h]; obn = ("ops", h)
        P.op(T_, lambda e: e.matmul(ws, st["WT"][:, cs], Sbf[h][ci][:], start=True, stop=True), reads=[st["WTn"], ("Sbf", h, ci)], writes=[wsn])
        vop(lambda e: e.tensor_tensor(vn[:], st["U"][:, c, :], ws, ALU.subtract), [st["Un"], wsn], [vnn])
        P.op(T_, lambda e: e.matmul(ob[:, cs], Sbf[h][ci][:], st["qg"][:, cs], start=True, stop=False), reads=[("Sbf", h, ci), st["qgn"]], writes=[obn])
        P.op(T_, lambda e: e.matmul(ob[:, cs], vn[:], st["QKm"][:, c, :], start=False, stop=True), reads=[vnn, st["QKmn"]], writes=[obn])
        P.op(T_, lambda e: e.matmul(ds, st["Kd"][:, c, :], vn[:], start=True, stop=True), reads=[st["Kdn"], vnn], writes=[dsn])
        vop(lambda e: e.scalar_tensor_tensor(Sbf[h][n][:], S32[h][ci][:], st["egl"][:, c:c + 1], ds, ALU.mult, ALU.add),
            [("S32", h, ci), st["egln"], dsn], [("Sbf", h, n)])
        vop(lambda e: e.scalar_tensor_tensor(S32[h][n][:], S32[h][ci][:], st["egl"][:, c:c + 1], ds, ALU.mult, ALU.add),
            [("S32", h, ci), st["egln"], dsn], [("S32", h, n)])
        cur[h] = n

    def finish(h, blk):
        st = blkstate[(h, blk)]
        ob = ops_[h]; obn = ("ops", h)
        sq, sn = sqp.next(); r, rn_ = rnp.next(); on, onn = onp.next(); o16, o16n = obp.next(); ps, pn = gpp.next()
        vop(lambda e: e.activation(sq[:], ob[:], AF.Square), [obn], [sn], eng=S)
        P.op(T_, lambda e: e.matmul(ps[:], ones_bf[:], sq[:], start=True, stop=True), reads=["g_ones", sn], writes=[pn])
        vop(lambda e: e.activation(r[:], ps[:], AF.Sqrt, bias=epsc[:, 0:1], scale=1.0 / 128), [pn, "g_eps"], [rn_], eng=S)
        vop(lambda e: e.reciprocal(r[:], r[:]), [rn_], [rn_])
        vop(lambda e: e.scalar_tensor_tensor(on[:], ob[:], nw[:, 0:1], r[:], ALU.mult, ALU.mult), [obn, "g_nw", rn_], [onn])
        vop(lambda e: e.tensor_tensor(o16[:], on[:], st["z"][:], ALU.mult), [onn, st["zn"]], [o16n], eng=G)
        P.dma("sync", oT[h * 128:(h + 1) * 128, blk * 512:(blk + 1) * 512], o16[:], reads=[o16n])

    for h in range(2):
        prep(h, 0)
    for blk in range(NB):
        if blk + 1 < NB:
            for h in range(2):
                prep(h, blk + 1)
        for c in range(8):
            for h in range(2):
                chunk_step(h, blk, c)
        for h in range(2):
            finish(h, blk)


EPS = 1e-6

def vop(P, fn, reads, writes, eng=V):
    P.op(eng, fn, reads=reads, writes=writes)

class Norm:
    def __init__(self, P, nfeat=4096, nxc=3, nsq=2, nr=2):
        self.P = P
        self.ones = P.sbuf("n_ones", [128, 128], BF16)
        vop(P, lambda e: e.memset(self.ones[:], 1.0), [], ["n_ones"])
        self.epsc = P.sbuf("n_eps", [128, 1], F32)
        vop(P, lambda e: e.memset(self.epsc[:], EPS), [], ["n_eps"])
        self.xc = Pool(P, "n_xc", nxc, [128, 4, 512], F32)
        self.xc2 = Pool(P, "n_xc2", 2, [128, 4, 512], F32)
        self.sq = Pool(P, "n_sq", nsq, [128, 4, 512], BF16)
        self.ps = Pool(P, "n_ps", 1, [128, 512], F32, psum=True)
        self.r = Pool(P, "n_r", nr, [128, 512], F32)
        self.nfeat = nfeat

    def rstd(self, src, tb, KT, src_tok=None):
        P = self.P
        ps, pn = self.ps.next()
        sv = src.rearrange("(kt p) t -> p kt t", p=128)
        for c in range(KT // 4):
            xc, xn = self.xc.next(); sq, sn = self.sq.next()
            P.dma("sync", xc[:], sv[:, c * 4:c * 4 + 4, tb * 512:(tb + 1) * 512], reads=([(src_tok, c * 4 + k_, tb) for k_ in range(4)] if src_tok else []), writes=[xn])
            vop(P, lambda e, xc=xc, sq=sq: e.activation(sq[:], xc[:], AF.Square), [xn], [sn], eng=S)
            for k in range(4):
                P.op(T_, lambda e, ps=ps, sq=sq, k=k, c=c: e.matmul(ps[:], self.ones[:], sq[:, k, :], start=(c == 0 and k == 0), stop=(c == KT // 4 - 1 and k == 3)),
                     reads=["n_ones", sn], writes=[pn])
        return self.finish(ps, pn, KT * 128)

    def finish(self, ps, pn, n):
        P = self.P
        r, rn = self.r.next()
        vop(P, lambda e: e.activation(r[:], ps[:], AF.Sqrt, bias=self.epsc[:, 0:1], scale=1.0 / n), [pn, "n_eps"], [rn], eng=S)
        vop(P, lambda e: e.reciprocal(r[:], r[:]), [rn], [rn])
        return r, rn

    def to_bf16(self, src, nw, nwn, dst, dstn, T):
        P = self.P
        sv = src.rearrange("(kt p) t -> p kt t", p=128)
        for tb in range(T // 512):
            r, rn = self.rstd(src, tb, 32)
            for c in range(8):
                xc, xn = self.xc.next()
                P.dma("sync", xc[:], sv[:, c * 4:c * 4 + 4, tb * 512:(tb + 1) * 512], writes=[xn])
                for k in range(4):
                    kt = c * 4 + k
                    vop(P, lambda e, xc=xc, k=k, kt=kt, r=r, tb=tb: e.scalar_tensor_tensor(dst[:, kt, tb * 512:(tb + 1) * 512], xc[:, k, :], nw[:, kt:kt + 1], r[:], ALU.mult, ALU.mult),
                        [xn, nwn, rn], [dstn])

    def resid(self, raw, x, nw, nwn, out, T, raw_tok=None):
        P = self.P
        rv = raw.rearrange("(kt p) t -> p kt t", p=128)
        xv = x.rearrange("(kt p) t -> p kt t", p=128)
        ov = out.rearrange("(kt p) t -> p kt t", p=128)
        for tb in range(T // 512):
            r, rn = self.rstd(raw, tb, 32, src_tok=raw_tok)
            for c in range(8):
                xc, xn = self.xc.next(); x2, x2n = self.xc2.next()
                P.dma("sync", xc[:], rv[:, c * 4:c * 4 + 4, tb * 512:(tb + 1) * 512], reads=([(raw_tok, c * 4 + k_, tb) for k_ in range(4)] if raw_tok else []), writes=[xn])
                P.dma("sync", x2[:], xv[:, c * 4:c * 4 + 4, tb * 512:(tb + 1) * 512], writes=[x2n])
                for k in range(4):
                    kt = c * 4 + k
                    vop(P, lambda e, xc=xc, k=k, kt=kt, r=r: e.scalar_tensor_tensor(xc[:, k, :], xc[:, k, :], nw[:, kt:kt + 1], r[:], ALU.mult, ALU.mult), [xn, nwn, rn], [xn])
                vop(P, lambda e, xc=xc, x2=x2: e.tensor_tensor(x2[:], x2[:], xc[:], ALU.add), [xn, x2n], [x2n], eng=G)
                P.dma("sync", ov[:, c * 4:c * 4 + 4, tb * 512:(tb + 1) * 512], x2[:], reads=[x2n])

def ld_const(P, name, shape, src, dt=F32, eng="sync"):
    t = P.sbuf("sb_" + name, shape, dt)
    P.dma(eng, t[:], src, writes=[name])
    return t

NPROJ = 18560
def build_A():
    P = Prog()
    xT = P.dram("xT", [4096, 1024], F32, "ExternalInput")
    nwd = P.dram("nw", [128, 32], F32, "ExternalInput")
    w = P.dram("w", [4096, NPROJ], F32, "ExternalInput")
    out = P.dram("projT", [NPROJ, 1024], F32, "ExternalOutput")
    nw = ld_const(P, "nw", [128, 32], nwd)
    N = Norm(P)
    h = P.sbuf("hT", [128, 32, 1024], BF16)
    N.to_bf16(xT, nw, "nw", h, "hT", 1024)
    Gm = Gemm(P)
    Gm.run(h, "hT", 32, w, NPROJ, Gm.evac_to_dram(out))
    return P.build()

def build_C1a():
    P = Prog()
    yT = P.dram("yT", [2048, 1024], BF16, "ExternalInput")
    oT = P.dram("oT", [2048, 1024], BF16, "ExternalInput")
    gs = P.dram("gsT", [4096, 1024], F32, "ExternalInput")
    gd = P.dram("gdT", [4096, 1024], F32, "ExternalInput")
    wglu = P.dram("w_glu", [2048, 8192], F32, "ExternalInput")
    wgdn = P.dram("w_gdn", [2048, 4096], F32, "ExternalInput")
    mT = P.dram("mT", [4096, 1024], BF16, "ExternalOutput")
    y = P.sbuf("y_sb", [128, 16, 1024], BF16)
    o = P.sbuf("o_sb", [128, 16, 1024], BF16)
    P.dma("sync", y[:], yT.rearrange("(kt p) t -> p kt t", p=128), writes=["y_sb"])
    P.dma("sync", o[:], oT.rearrange("(kt p) t -> p kt t", p=128), writes=["o_sb"])
    Gm = Gemm(P, wbytes=8192, npsum=7)
    gsp = Pool(P, "c_gs", 2, [128, 512], F32); gdp = Pool(P, "c_gd", 2, [128, 512], F32)
    t1p = Pool(P, "c_t1", 2, [128, 512], F32); t2p = Pool(P, "c_t2", 2, [128, 512], F32)
    sbp = Pool(P, "c_sb", 2, [128, 512], F32); mp = Pool(P, "c_m", 3, [128, 512], BF16)
    for nt in range(32):
        stash = {}
        def ev_ab(i, tb, ps, ptok, stash=stash):
            stash[(i, tb)] = (ps, ptok)
        Gm.run(y, "y_sb", 16, wglu, 256, ev_ab, col_groups=[[(nt * 128, 128), (4096 + nt * 128, 128)]])
        def ev_g(i, tb, G_, Gn, stash=stash, nt=nt):
            A_, An = stash[(0, tb)]; B_, Bn = stash[(1, tb)]
            gst, gsn = gsp.next(); gdt, gdn = gdp.next(); t1, t1n = t1p.next(); t2, t2n = t2p.next(); sb, sbn = sbp.next(); m, mn = mp.next()
            rs = slice(nt * 128, (nt + 1) * 128); cs = slice(tb * 512, (tb + 1) * 512)
            P.dma("sync", gst[:], gs[rs, cs], writes=[gsn])
            P.dma("sync", gdt[:], gd[rs, cs], writes=[gdn])
            vop(P, lambda e: e.activation(sb[:], B_[:], AF.Sigmoid), [Bn], [sbn], eng=S)
            vop(P, lambda e: e.tensor_tensor(t1[:], A_[:], sb[:], ALU.mult), [An, sbn], [t1n])
            vop(P, lambda e: e.activation(gst[:], gst[:], AF.Sigmoid), [gsn], [gsn], eng=S)
            vop(P, lambda e: e.activation(gdt[:], gdt[:], AF.Sigmoid), [gdn], [gdn], eng=S)
            vop(P, lambda e: e.tensor_tensor(t1[:], t1[:], gst[:], ALU.mult), [t1n, gsn], [t1n])
            vop(P, lambda e: e.tensor_tensor(t2[:], G_[:], gdt[:], ALU.mult), [Gn, gdn], [t2n])
            vop(P, lambda e: e.tensor_tensor(m[:], t1[:], t2[:], ALU.add), [t1n, t2n], [mn])
            P.dma("sync", mT[rs, cs], m[:], reads=[mn])
        Gm.run(o, "o_sb", 16, wgdn, 128, ev_g, col_groups=[[(nt * 128, 128)]])
    return P.build()

def build_C1b():
    P = Prog()
    mT = P.dram("mT", [4096, 1024], BF16, "ExternalInput")
    xT = P.dram("xT", [4096, 1024], F32, "ExternalInput")
    w = P.dram("w_out", [4096, 4096], F32, "ExternalInput")
    nwd = P.dram("nw", [128, 32], F32, "ExternalInput")
    x1 = P.dram("x1T", [4096, 1024], F32, "ExternalOutput")
    raw = P.nc.dram_tensor("c_raw", [4096, 1024], F32).ap()
    nw = ld_const(P, "nw", [128, 32], nwd)
    m = P.sbuf("m_sb", [128, 32, 1024], BF16)
    P.dma("sync", m[:], mT.rearrange("(kt p) t -> p kt t", p=128), writes=["m_sb"])
    Gm = Gemm(P)
    Gm.run(m, "m_sb", 32, w, 4096, Gm.evac_to_dram(raw, tok="c_raw"))
    N = Norm(P)
    N.resid(raw, xT, nw, "nw", x1, 1024, raw_tok="c_raw")
    return P.build()

def build_C2(FF=16384):
    P = Prog()
    nq = FF // 4096
    x1 = P.dram("x1T", [4096, 1024], F32, "ExternalInput")
    w1 = P.dram("w1", [4096, FF], F32, "ExternalInput")
    w2 = P.dram("w2", [FF, 4096], F32, "ExternalInput")
    nw1d = P.dram("nw1", [128, 32], F32, "ExternalInput")
    nw2d = P.dram("nw2", [128, 32], F32, "ExternalInput")
    x2 = P.dram("x2T", [4096, 1024], F32, "ExternalOutput")
    parts = [P.nc.dram_tensor("f_part%d" % q, [4096, 1024], F32).ap() for q in range(nq)]
    raw2 = P.nc.dram_tensor("f_raw2", [4096, 1024], F32).ap()
    nw1 = ld_const(P, "nw1", [128, 32], nw1d)
    nw2 = ld_const(P, "nw2", [128, 32], nw2d)
    N = Norm(P, nxc=2, nsq=1, nr=1)
    h = P.sbuf("hT", [128, 32, 1024], BF16)
    hq = P.sbuf("hqT", [128, 32, 1024], BF16)
    N.to_bf16(x1, nw1, "nw1", h, "hT", 1024)
    Gm = Gemm(P, nwb=2, nstage=2)
    relp = Pool(P, "f_rel", 2, [128, 512], F32)
    for q in range(nq):
        def ev1(nt, tb, ps, ptok):
            r, rn = relp.next()
            vop(P, lambda e: e.activation(r[:], ps[:], AF.Relu), [ptok], [rn], eng=S)
            vop(P, lambda e: e.tensor_tensor(hq[:, nt, tb * 512:(tb + 1) * 512], r[:], r[:], ALU.mult), [rn], [("hq", nt, tb)])
        Gm.run(h, "hT", 32, w1[:, q * 4096:(q + 1) * 4096], 4096, ev1)
        Gm.run(hq, (lambda kt, tb: ("hq", kt, tb)), 32, w2[q * 4096:(q + 1) * 4096, :], 4096, Gm.evac_to_dram(parts[q], tok=("part", q)))
    for tb in range(2):
        for c in range(8):
            acc, accn = N.xc2.next()
            sl = (slice(None), slice(c * 4, c * 4 + 4), slice(tb * 512, (tb + 1) * 512))
            P.dma("sync", acc[:], parts[0].rearrange("(kt p) t -> p kt t", p=128)[sl], reads=[(("part", 0), c * 4 + k_, tb) for k_ in range(4)], writes=[accn])
            for q in range(1, nq):
                xc, xn = N.xc.next()
                P.dma("sync", xc[:], parts[q].rearrange("(kt p) t -> p kt t", p=128)[sl], reads=[(("part", q), c * 4 + k_, tb) for k_ in range(4)], writes=[xn])
                vop(P, lambda e, acc=acc, xc=xc: e.tensor_tensor(acc[:], acc[:], xc[:], ALU.add), [accn, xn], [accn])
            P.dma("sync", raw2.rearrange("(kt p) t -> p kt t", p=128)[sl], acc[:], reads=[accn], writes=[("f_raw2", c * 4 + k_, tb) for k_ in range(4)])
    N.resid(raw2, x1, nw2, "nw2", x2, 1024, raw_tok="f_raw2")
    return P.build()


def build_BS():
    P = Prog()
    uT = P.dram("uT", [256, 8192], F32, "ExternalInput")
    yT = P.dram("yT", [256, 8192], BF16, "ExternalOutput")
    z16 = np.zeros((128, 64), np.float32)
    shapes = {"ssmA": [128, 5, 2, 64], "ssmB": [128, 3, 16], "ssmcc": [128, 2, 16, 16], "ssmd": [128, 2]}
    shapes.update({k: list(v.shape) for k, v in ssm_consts().items()})
    D = {k: P.dram(k, s, F32, "ExternalInput") for k, s in shapes.items()}
    emit_ssm(P, uT, yT, D, NCH=16)
    return P.build()


def build_BG():
    P = Prog()
    gin = P.dram("gin", [1028, 8192], F32, "ExternalInput")
    oT = P.dram("oT", [256, 8192], BF16, "ExternalOutput")
    shapes = {"g_cw": [128, 3, 2, 4], "g_alog": [2, 1], "g_dtb": [2, 1], "g_nw": [128, 1], "g_ident": [128, 128]}
    shapes.update({k: list(v.shape) for k, v in gdn2_consts().items()})
    D = {k: P.dram(k, s, F32, "ExternalInput") for k, s in shapes.items()}
    emit_gdn2(P, gin, oT, D, T=8192)
    return P.build()


def pack_w_in(w_in):
    ab = np.zeros((4096, 128), np.float32)
    for j in range(8):
        ab[:, 4 * j + 0] = w_in[:, 8192 + 2 * j]
        ab[:, 4 * j + 1] = w_in[:, 8192 + 2 * j + 1]
        ab[:, 4 * j + 2] = w_in[:, 8208 + 2 * j]
        ab[:, 4 * j + 3] = w_in[:, 8208 + 2 * j + 1]
    return np.ascontiguousarray(np.concatenate([w_in[:, 0:8192], w_in[:, 8224:], ab], axis=1))


_PROGS = {}


def _prog(name, builder):
    if name not in _PROGS:
        _PROGS[name] = builder()
    return _PROGS[name]


def _run(name, builder, ins):
    nc = _prog(name, builder)
    return run_bass_kernel_spmd(nc, ins, core_ids=list(range(8))).results


def _col(v):
    return np.ascontiguousarray(np.asarray(v, np.float32).reshape(32, 128).T)


def kernel(**inp):
    inp = {k: np.asarray(v) for k, v in inp.items()}
    C = np.ascontiguousarray
    x = inp["x"][0]
    xs = [C(x[c * 1024:(c + 1) * 1024].T) for c in range(8)]
    consts = ssm_consts()
    gconsts = gdn2_consts()
    for l in range(2):
        wp = pack_w_in(inp["w_in"][l])
        nw = _col(inp["mix_pre_w"][l])
        rA = _run("A", build_A, [{"xT": xs[c], "nw": nw, "w": wp} for c in range(8)])
        del wp
        projT = np.concatenate([r["projT"] for r in rA], axis=1)
        gates = [(C(r["projT"][10240:14336]), C(r["projT"][14336:18432])) for r in rA]
        del rA
        ins = []
        for j in range(8):
            d = ssm_host_inputs(j, inp["ssm_a_re"][l], inp["ssm_a_im"][l], inp["ssm_log_dt"][l], inp["ssm_b_re"][l],
                                inp["ssm_b_im"][l], inp["ssm_c_re"][l], inp["ssm_c_im"][l], inp["ssm_d"][l])
            d.update(consts)
            d["uT"] = C(projT[256 * j:256 * (j + 1)])
            ins.append(d)
        rS = _run("BS", build_BS, ins)
        yfull = np.concatenate([r["yT"] for r in rS], axis=0)
        ins = []
        for j in range(8):
            d = gdn_host_params(j, inp["conv_w"][l], inp["gdn_a_log"][l], inp["gdn_dt_bias"][l], inp["gdn_norm_w"][l])
            d.update(gconsts)
            d["gin"] = C(np.concatenate([projT[2048 + 256 * j:2048 + 256 * (j + 1)], projT[4096 + 256 * j:4096 + 256 * (j + 1)],
                                         projT[6144 + 256 * j:6144 + 256 * (j + 1)], projT[8192 + 256 * j:8192 + 256 * (j + 1)],
                                         projT[18432 + 4 * j:18432 + 4 * j + 4]], axis=0))
            ins.append(d)
        del projT
        rG = _run("BG", build_BG, ins)
        ofull = np.concatenate([r["oT"] for r in rG], axis=0)
        wglu = C(inp["w_glu"][l]); wgdn = C(inp["w_gdn_out"][l])
        rM = _run("C1a", build_C1a, [{"yT": C(yfull[:, c * 1024:(c + 1) * 1024]), "oT": C(ofull[:, c * 1024:(c + 1) * 1024]),
                                       "gsT": gates[c][0], "gdT": gates[c][1], "w_glu": wglu, "w_gdn": wgdn} for c in range(8)])
        del gates
        wout = C(inp["w_out"][l]); nwp = _col(inp["mix_post_w"][l])
        r1 = _run("C1b", build_C1b, [{"mT": rM[c]["mT"], "xT": xs[c], "w_out": wout, "nw": nwp} for c in range(8)])
        w1 = C(inp["w_ff1"][l]); w2 = C(inp["w_ff2"][l])
        n1 = _col(inp["ffn_pre_w"][l]); n2 = _col(inp["ffn_post_w"][l])
        r2 = _run("C2", build_C2, [{"x1T": r1[c]["x1T"], "w1": w1, "w2": w2, "nw1": n1, "nw2": n2} for c in range(8)])
        xs = [r2[c]["x2T"] for c in range(8)]
    out = np.concatenate([xc.T for xc in xs], axis=0)
    return np.ascontiguousarray(out.reshape(1, 8192, 4096).astype(np.float32))
```
